# Optimizing a Trainium2 kernel written in Bass

```python
import math
import jax, jax.numpy as jnp
from jax import lax
import numpy as np

D_MODEL = 4096
BATCH = 1
SEQ = 8192
DEPTH = 1
DEC_BATCH = 8
DEC_SEQ = 2048
PAST_LEN = 128

HEAD_DIM = 128
N_HEADS_TOTAL = D_MODEL // HEAD_DIM
NA_HEADS = N_HEADS_TOTAL // 2
SWA_Q_HEADS = N_HEADS_TOTAL - NA_HEADS
SWA_KV_HEADS = max(1, SWA_Q_HEADS // 4)
NA_WIDTH = NA_HEADS * HEAD_DIM
SWA_Q_WIDTH = SWA_Q_HEADS * HEAD_DIM
SWA_KV_WIDTH = SWA_KV_HEADS * HEAD_DIM
MIX_WIDTH = NA_WIDTH + SWA_Q_WIDTH
IN_WIDTH = 3 * NA_WIDTH + SWA_Q_WIDTH + 2 * SWA_KV_WIDTH
IN_SPLITS = (NA_WIDTH, 2 * NA_WIDTH, 3 * NA_WIDTH, 3 * NA_WIDTH + SWA_Q_WIDTH,
             3 * NA_WIDTH + SWA_Q_WIDTH + SWA_KV_WIDTH)
D_FF = 4 * D_MODEL
GRID_W = 64
NA_ROWS = 8
NA_COLS = 16
SWA_WINDOW = 128
SWA_BLOCK = 128
T5_BUCKETS = 32
T5_MAX_DIST = 128
N_META = 16
NORM_EPS = 1e-6
NEG_INF = -1e30

kernel_name = "hymba_na_swa_encoder"


def rms_norm(x, gain):
    xf = x.astype(jnp.float32)
    y = xf * lax.rsqrt(jnp.mean(xf * xf, axis=-1, keepdims=True) + NORM_EPS)
    return (y * gain.astype(jnp.float32)).astype(x.dtype)


def t5_bucket(rel):
    nb = T5_BUCKETS // 2
    max_exact = nb // 2
    ret = np.where(rel > 0, nb, 0)
    n = np.abs(rel)
    large = max_exact + (np.log(np.maximum(n, 1) / max_exact)
                         / math.log(T5_MAX_DIST / max_exact) * (nb - max_exact)).astype(np.int64)
    large = np.minimum(large, nb - 1)
    return ret + np.where(n < max_exact, n, large)


def neighborhood_attention(q, k, v, rpb, with_meta_queries):
    batch, L, H, Dh = q.shape
    T = L - N_META
    rows = T // GRID_W
    kr_ = min(NA_ROWS, rows)
    scale = Dh ** -0.5
    qm, km, vm = q[:, :N_META], k[:, :N_META], v[:, :N_META]
    r = np.arange(rows)
    rs = np.clip(r - kr_ // 2, 0, rows - kr_)
    row_idx = rs[:, None] + np.arange(kr_)[None, :]
    dr = row_idx - r[:, None]
    c = np.arange(GRID_W)
    cs = np.clip(c - NA_COLS // 2, 0, GRID_W - NA_COLS)
    dc = c[None, :] - c[:, None]
    col_in = (c[None, :] >= cs[:, None]) & (c[None, :] < cs[:, None] + NA_COLS)
    bias = rpb[:, dr[:, None, :, None] + (NA_ROWS - 1),
               np.clip(dc, -(NA_COLS - 1), NA_COLS - 1)[None, :, None, :] + (NA_COLS - 1)]
    qg = q[:, N_META:].reshape(batch, rows, GRID_W, H, Dh)
    kg = k[:, N_META:].reshape(batch, rows, GRID_W, H, Dh)[:, row_idx]
    vg = v[:, N_META:].reshape(batch, rows, GRID_W, H, Dh)[:, row_idx]
    s = jnp.einsum('brqhd,brikhd->bhrqik', qg, kg,
                   preferred_element_type=jnp.float32) * scale + bias[None].astype(jnp.float32)
    s = jnp.where(col_in[:, None, :], s, NEG_INF).reshape(batch, H, rows, GRID_W, kr_ * GRID_W)
    s_m = jnp.einsum('brqhd,bmhd->bhrqm', qg, km, preferred_element_type=jnp.float32) * scale
    p = jax.nn.softmax(jnp.concatenate([s_m, s], axis=-1), axis=-1).astype(v.dtype)
    p_m = p[..., :N_META]
    p_w = p[..., N_META:].reshape(batch, H, rows, GRID_W, kr_, GRID_W)
    o = (jnp.einsum('bhrqm,bmhd->brqhd', p_m, vm, preferred_element_type=jnp.float32)
         + jnp.einsum('bhrqik,brikhd->brqhd', p_w, vg, preferred_element_type=jnp.float32))
    o_real = o.reshape(batch, T, H * Dh).astype(q.dtype)
    o_meta = None
    if with_meta_queries:
        sm = jnp.einsum('bqhd,bmhd->bhqm', qm, km, preferred_element_type=jnp.float32) * scale
        pm = jax.nn.softmax(sm, axis=-1).astype(v.dtype)
        o_meta = jnp.einsum('bhqm,bmhd->bqhd', pm, vm).reshape(batch, N_META, H * Dh)
    return o_meta, o_real


def window_attention(q, k, v, t5_bias, sink, with_meta_queries):
    batch, L, Hq, Dh = q.shape
    Hkv = k.shape[2]
    G = Hq // Hkv
    T = L - N_META
    blk = SWA_BLOCK
    nb = T // blk
    scale = Dh ** -0.5
    qm, km, vm = q[:, :N_META], k[:, :N_META], v[:, :N_META]
    kr, vr = k[:, N_META:], v[:, N_META:]
    qb = q[:, N_META:].reshape(batch, nb, blk, Hkv, G, Dh)
    pad = ((0, 0), (blk, blk), (0, 0), (0, 0))
    kp, vp = jnp.pad(kr, pad), jnp.pad(vr, pad)

    def band(a):
        return jnp.concatenate([a[:, o * blk:o * blk + T].reshape(batch, nb, blk, Hkv, Dh)
                                for o in range(3)], axis=2)

    kw, vw = band(kp), band(vp)
    qq = np.arange(blk)
    jj = np.arange(3 * blk)
    rel_w = (jj[None, :] - blk) - qq[:, None]
    key_t = np.arange(nb)[:, None] * blk - blk + jj[None, :]
    ok_w = (np.abs(rel_w) <= SWA_WINDOW)[None] & ((key_t >= 0) & (key_t < T))[:, None, :]
    bias_w = t5_bias[t5_bucket(rel_w)].transpose(2, 0, 1).reshape(Hkv, G, blk, 3 * blk)
    rel_m = np.arange(N_META)[None, :] - (N_META + np.arange(T))[:, None]
    bias_m = t5_bias[t5_bucket(rel_m)].reshape(nb, blk, N_META, Hkv, G).transpose(0, 3, 4, 1, 2)
    s_w = jnp.einsum('bnqhgd,bnkhd->bnhgqk', qb, kw,
                     preferred_element_type=jnp.float32) * scale + bias_w.astype(jnp.float32)
    s_w = jnp.where(ok_w[:, None, None], s_w, NEG_INF)
    s_m = jnp.einsum('bnqhgd,bmhd->bnhgqm', qb, km,
                     preferred_element_type=jnp.float32) * scale + bias_m.astype(jnp.float32)
    sink_col = jnp.broadcast_to(sink.astype(jnp.float32).reshape(Hkv, G, 1, 1),
                                s_w.shape[:-1] + (1,))
    p = jax.nn.softmax(jnp.concatenate([s_m, s_w, sink_col], axis=-1), axis=-1).astype(v.dtype)
    o = (jnp.einsum('bnhgqm,bmhd->bnqhgd', p[..., :N_META], vm, preferred_element_type=jnp.float32)
         + jnp.einsum('bnhgqk,bnkhd->bnqhgd', p[..., N_META:N_META + 3 * blk], vw,
                      preferred_element_type=jnp.float32))
    o_real = o.reshape(batch, T, Hq * Dh).astype(q.dtype)
    o_meta = None
    if with_meta_queries:
        kq = jnp.concatenate([km, kr[:, :blk]], axis=1)
        vq = jnp.concatenate([vm, vr[:, :blk]], axis=1)
        kpos = np.arange(N_META + blk)
        rel_q = kpos[None, :] - np.arange(N_META)[:, None]
        ok_q = (kpos[None, :] < N_META) | (np.abs(rel_q) <= SWA_WINDOW)
        bias_q = t5_bias[t5_bucket(rel_q)].transpose(2, 0, 1).reshape(Hkv, G, N_META, N_META + blk)
        qmg = qm.reshape(batch, N_META, Hkv, G, Dh)
        sq = jnp.einsum('bqhgd,bkhd->bhgqk', qmg, kq,
                        preferred_element_type=jnp.float32) * scale + bias_q.astype(jnp.float32)
        sq = jnp.where(ok_q, sq, NEG_INF)
        sink_q = jnp.broadcast_to(sink.astype(jnp.float32).reshape(Hkv, G, 1, 1), sq.shape[:-1] + (1,))
        pq = jax.nn.softmax(jnp.concatenate([sq, sink_q], axis=-1), axis=-1).astype(v.dtype)
        o_meta = jnp.einsum('bhgqk,bkhd->bqhgd', pq[..., :-1], vq).reshape(batch, N_META, Hq * Dh)
    return o_meta, o_real


def sq_relu_mlp(h, gain, w_up, w_down):
    u = rms_norm(h, gain) @ w_up
    return jnp.square(jax.nn.relu(u)) @ w_down


def encoder_layer(m, x, t5_bias, g_attn, w_in, qn_na, kn_na, rpb, qn_sw, kn_sw, sink,
                  w_out, g_mlp, w_up, w_down, update_meta):
    batch = x.shape[0]
    h = jnp.concatenate([m, x], axis=1)
    L = h.shape[1]
    proj = rms_norm(h, g_attn) @ w_in
    q_na, k_na, v_na, q_sw, k_sw, v_sw = jnp.split(proj, IN_SPLITS, axis=-1)

    def heads(a, n):
        return a.reshape(batch, L, n, HEAD_DIM)

    q_na = rms_norm(heads(q_na, NA_HEADS), qn_na)
    k_na = rms_norm(heads(k_na, NA_HEADS), kn_na)
    q_sw = rms_norm(heads(q_sw, SWA_Q_HEADS), qn_sw)
    k_sw = rms_norm(heads(k_sw, SWA_KV_HEADS), kn_sw)
    na_m, na_r = neighborhood_attention(q_na, k_na, heads(v_na, NA_HEADS), rpb, update_meta)
    sw_m, sw_r = window_attention(q_sw, k_sw, heads(v_sw, SWA_KV_HEADS), t5_bias, sink, update_meta)
    x = x + jnp.concatenate([na_r, sw_r], axis=-1) @ w_out
    x = x + sq_relu_mlp(x, g_mlp, w_up, w_down)
    if update_meta:
        m = m + jnp.concatenate([na_m, sw_m], axis=-1) @ w_out
        m = m + sq_relu_mlp(m, g_mlp, w_up, w_down)
    return m, x


def encoder_trunk(x, meta_tokens, t5_bias, norm_attn, w_in, q_norm_na, k_norm_na, na_rpb,
                  q_norm_swa, k_norm_swa, swa_sink, w_out, norm_mlp, w_up, w_down):
    batch = x.shape[0]
    m = jnp.broadcast_to(meta_tokens.astype(x.dtype)[None], (batch, N_META, D_MODEL))
    for layer in range(DEPTH):
        m, x = encoder_layer(m, x, t5_bias, norm_attn[layer], w_in[layer], q_norm_na[layer],
                             k_norm_na[layer], na_rpb[layer], q_norm_swa[layer], k_norm_swa[layer],
                             swa_sink[layer], w_out[layer], norm_mlp[layer], w_up[layer],
                             w_down[layer], update_meta=layer < DEPTH - 1)
    return x


def setup_inputs(seed: int = 0) -> dict:
    key = jax.random.key(seed)
    ks = jax.random.split(key, 16)
    f32 = jnp.float32
    nrm = jax.random.normal
    return {
        "x_prompt": nrm(ks[0], (BATCH, SEQ, D_MODEL), f32),
        "x_sample": nrm(ks[1], (DEC_BATCH, DEC_SEQ, D_MODEL), f32),
        "meta_tokens": nrm(ks[2], (N_META, D_MODEL), f32),
        "t5_bias": 0.3 * nrm(ks[3], (T5_BUCKETS, SWA_Q_HEADS), f32),
        "norm_attn": 1.0 + 0.02 * nrm(ks[4], (DEPTH, D_MODEL), f32),
        "w_in": nrm(ks[5], (DEPTH, D_MODEL, IN_WIDTH), f32) * D_MODEL ** -0.5,
        "q_norm_na": 1.0 + 0.02 * nrm(ks[6], (DEPTH, HEAD_DIM), f32),
        "k_norm_na": 1.0 + 0.02 * nrm(ks[7], (DEPTH, HEAD_DIM), f32),
        "na_rpb": 0.3 * nrm(ks[8], (DEPTH, NA_HEADS, 2 * NA_ROWS - 1, 2 * NA_COLS - 1), f32),
        "q_norm_swa": 1.0 + 0.02 * nrm(ks[9], (DEPTH, HEAD_DIM), f32),
        "k_norm_swa": 1.0 + 0.02 * nrm(ks[10], (DEPTH, HEAD_DIM), f32),
        "swa_sink": 0.5 * nrm(ks[11], (DEPTH, SWA_Q_HEADS), f32),
        "w_out": nrm(ks[12], (DEPTH, MIX_WIDTH, D_MODEL), f32) * MIX_WIDTH ** -0.5,
        "norm_mlp": 1.0 + 0.02 * nrm(ks[13], (DEPTH, D_MODEL), f32),
        "w_up": nrm(ks[14], (DEPTH, D_MODEL, D_FF), f32) * D_MODEL ** -0.5,
        "w_down": nrm(ks[15], (DEPTH, D_FF, D_MODEL), f32) * D_FF ** -0.5,
    }


def reference(x_prompt, x_sample, meta_tokens, t5_bias, norm_attn, w_in, q_norm_na, k_norm_na,
              na_rpb, q_norm_swa, k_norm_swa, swa_sink, w_out, norm_mlp, w_up, w_down):
    y_prompt = encoder_trunk(x_prompt, meta_tokens, t5_bias, norm_attn, w_in, q_norm_na, k_norm_na,
                             na_rpb, q_norm_swa, k_norm_swa, swa_sink, w_out, norm_mlp, w_up, w_down)
    y_sample = encoder_trunk(x_sample, meta_tokens, t5_bias, norm_attn, w_in, q_norm_na, k_norm_na,
                             na_rpb, q_norm_swa, k_norm_swa, swa_sink, w_out, norm_mlp, w_up, w_down)
    return (y_prompt, y_sample)
```

```python
import numpy as np
from contextlib import ExitStack
import concourse.bass as bass
import concourse.mybir as mybir
from concourse.bass_utils import run_bass_kernel_spmd

F32 = mybir.dt.float32
BF16 = mybir.dt.bfloat16
AF = mybir.ActivationFunctionType
ALU = mybir.AluOpType
AX = mybir.AxisListType

NCORES = 8
D = 4096
NKVB = 32
NQB = 24
META_BLK = 30
NEG = -30000.0
EPS = 1e-6
SEM_LIMIT = 30000


class Ev:
    __slots__ = ("sem", "val")

    def __init__(self, sem, val):
        self.sem = sem
        self.val = val


class Buf:
    def __init__(self):
        self.w = {}
        self.r = {}

    def rdeps(self):
        return [Ev(k, v) for k, v in self.w.items()]

    def wdeps(self):
        return [Ev(k, v) for k, v in self.w.items()] + [Ev(k, v) for k, v in self.r.items()]

    def start_write(self):
        self.w = {}
        self.r = {}

    def wrote(self, ev):
        self.w[ev.sem] = max(self.w.get(ev.sem, 0), ev.val)

    def read(self, ev):
        self.r[ev.sem] = max(self.r.get(ev.sem, 0), ev.val)


class Eng:
    def __init__(self, k, eng, name):
        self.k = k
        self.eng = eng
        self.name = name
        self.sem = None
        self.cnt = 0
        self.nep = 0
        self.waited = {}
        self.last = None

    def wait(self, evs):
        for ev in evs:
            if ev is None:
                continue
            if self.waited.get(ev.sem, 0) >= ev.val:
                continue
            self.eng.wait_ge(ev.sem, ev.val)
            self.waited[ev.sem] = ev.val

    def sig(self, ins):
        if self.sem is None or self.cnt >= SEM_LIMIT:
            self.sem = self.k.newsem(f"{self.name}{self.nep}")
            self.nep += 1
            self.cnt = 0
        self.cnt += 1
        ins.then_inc(self.sem, 1)
        ev = Ev(self.sem, self.cnt)
        self.last = ev
        return ev


class Chan:
    def __init__(self, k, name):
        self.sem = k.newsem(name)
        self.cnt = 0
        k.chans.append(self)

    def ev(self):
        return Ev(self.sem, self.cnt)


class K:
    def __init__(self, nc, st):
        self.nc = nc
        self.st = st
        self.nsem = 0
        self.chans = []
        self.PE = Eng(self, nc.tensor, "pe")
        self.ACT = Eng(self, nc.scalar, "act")
        self.DVE = Eng(self, nc.vector, "dve")
        self.POOL = Eng(self, nc.gpsimd, "pool")
        self.SP = Eng(self, nc.sync, "sp")
        self.engs = [self.PE, self.ACT, self.DVE, self.POOL]

    def newsem(self, name):
        self.nsem += 1
        return self.st.enter_context(self.nc.semaphore(name))

    def dma(self, chan, out, in_, deps, q=None):
        q = q or self.SP
        q.wait(deps)
        ins = q.eng.dma_start(out=out, in_=in_)
        chan.cnt += 16
        assert chan.cnt < SEM_LIMIT, chan.cnt
        ins.then_inc(chan.sem, 16)
        return Ev(chan.sem, chan.cnt)

    def barrier(self):
        evs = [e.last for e in self.engs if e.last is not None]
        evs += [c.ev() for c in self.chans if c.cnt > 0]
        for e in self.engs + [self.SP]:
            e.wait(evs)


class WStream:
    def __init__(self, k, slots, bufs, chans):
        self.k = k
        self.slots = slots
        self.bufs = bufs
        self.chans = chans
        self.n = len(slots)
        self.pos = 0
        self.reset([])

    def reset(self, aps):
        self.aps = aps
        self.issued = 0
        self.consumed = 0
        self.base = self.pos

    def _issue(self):
        i = self.issued
        s = (self.base + i) % self.n
        b = self.bufs[s]
        deps = b.wdeps()
        b.start_write()
        key, ap = self.aps[i]
        deps = deps + self.k.ensure(key)
        ev = self.k.dma(self.chans[s], self.slots[s][:], ap, deps)
        b.wrote(ev)
        self.issued += 1

    def topup(self):
        while self.issued < min(self.consumed + self.n, len(self.aps)):
            self._issue()

    def get(self):
        self.topup()
        s = (self.base + self.consumed) % self.n
        self.consumed += 1
        self.pos = self.base + self.consumed
        return self.slots[s], self.bufs[s]


def build_nc(stop_after=None, debug=False):
    nc = bass.Bass("TRN2", target_bir_lowering=False)
    dk = "ExternalOutput" if debug else "Internal"

    def din(name, shape, dt=F32):
        return nc.dram_tensor(name, list(shape), dt, kind="ExternalInput").ap()

    xkv = din("xkv", [NKVB * 128, D])
    w_in = din("w_in", [D, 9216])
    w_out = din("w_out", [D, D])
    w_up = din("w_up", [D, 4 * D])
    w_down = din("w_down", [4 * D, D])
    pvec = din("pvec", [128, 84])
    ident_in = din("ident", [128, 128])
    bna = din("bna", [16, 128, 7 * 128])
    bsw = din("bsw", [16, 128, 3 * 128])
    bmeta = din("bmeta", [16, 16, NQB * 128])
    mcna = din("mcna", [128, NQB * 14])
    mcsw = din("mcsw", [128, NQB * 3])
    y = nc.dram_tensor("y", [NQB * 128, D], F32, kind="ExternalOutput").ap()

    WI = nc.dram_tensor("WI", [36, 128, 8192], BF16, kind=dk).ap()
    WO = nc.dram_tensor("WO", [16, 128, 8192], BF16).ap()
    WU = nc.dram_tensor("WU", [64, 128, 8192], BF16).ap()
    WD = nc.dram_tensor("WD", [64, 128, 8192], BF16).ap()
    QTna = nc.dram_tensor("QTna", [16, 128, NQB * 128], BF16, kind=dk).ap()
    QTsw = nc.dram_tensor("QTsw", [16, 128, NQB * 128], BF16, kind=dk).ap()
    KTna = nc.dram_tensor("KTna", [16, 128, NKVB * 128], BF16, kind=dk).ap()
    KTsw = nc.dram_tensor("KTsw", [4, 128, NKVB * 128], BF16, kind=dk).ap()
    Vna = nc.dram_tensor("Vna", [NKVB * 128, 2048], BF16, kind=dk).ap()
    Vsw = nc.dram_tensor("Vsw", [NKVB * 128, 512], BF16, kind=dk).ap()
    OT = nc.dram_tensor("OT", [32, 128, NQB * 128], BF16, kind=dk).ap()

    with ExitStack() as st:
        k = K(nc, st)
        PE, ACT, DVE, POOL, SP = k.PE, k.ACT, k.DVE, k.POOL, k.SP

        def sb(name, shape, dt, stack=st):
            return stack.enter_context(nc.sbuf_tensor(name, list(shape), dt))

        def ps(name, shape, dt, stack=st):
            return stack.enter_context(nc.psum_tensor(name, list(shape), dt))

        ident = sb("ident_sb", [128, 128], BF16)
        ones = sb("ones", [128, 128], BF16)
        pv = sb("pv", [128, 84], F32)
        gsc = sb("gsc", [128, 4], F32)
        esink = sb("esink", [128, 32], F32)
        epsb = sb("epsb", [128, 1], F32)
        mna = sb("mna", [128, NQB * 14], F32)
        msw = sb("msw", [128, NQB * 3], F32)
        wchans = [Chan(k, f"wch{i}") for i in range(4)]

        def make_ws(nw, stack, tag):
            slots = [sb(f"w{tag}{i}", [128, 8192], BF16, stack) for i in range(nw)]
            return WStream(k, slots, [Buf() for _ in range(nw)], wchans[:nw])
        c_const = Chan(k, "cconst")
        with ExitStack() as s0:
            id32 = sb("id32", [128, 128], F32, s0)
            e1 = k.dma(c_const, id32[:], ident_in[:, :], [])
            e2 = k.dma(c_const, pv[:], pvec[:, :], [])
            e3 = k.dma(c_const, mna[:], mcna[:, :], [])
            e4 = k.dma(c_const, msw[:], mcsw[:, :], [])
            DVE.wait([e4])
            DVE.sig(nc.vector.tensor_copy(out=ident[:], in_=id32[:]))
            DVE.sig(nc.vector.memset(ones[:], 1.0))
            DVE.sig(nc.vector.memset(epsb[:], EPS))
            DVE.sig(nc.vector.memset(esink[:], 0.0))
            sc = 128.0 ** -0.5
            DVE.sig(nc.vector.tensor_scalar(out=gsc[:, 0:1], in0=pv[:, 64:65], scalar1=sc, scalar2=None, op0=ALU.mult))
            DVE.sig(nc.vector.tensor_copy(out=gsc[:, 1:2], in_=pv[:, 65:66]))
            DVE.sig(nc.vector.tensor_scalar(out=gsc[:, 2:3], in0=pv[:, 66:67], scalar1=sc, scalar2=None, op0=ALU.mult))
            ev = DVE.sig(nc.vector.tensor_copy(out=gsc[:, 3:4], in_=pv[:, 67:68]))
            ACT.wait([e4, ev])
            ACT.sig(nc.scalar.activation(out=esink[:, 16:32], in_=pv[:, 68:84], func=AF.Exp))
            k.barrier()

        NCC = 8
        cch = [Chan(k, f"cch{i}") for i in range(NCC)]
        cchunks = []
        ready = {}

        def add_chunk(key, view, dst, k0, nk, c0, ncol):
            cchunks.append((key, view[:, k0:k0 + nk, c0:c0 + ncol], dst.rearrange("p (k n) -> p k n", k=nk)))

        wi_v = w_in.rearrange("(k p) n -> p k n", p=128)
        for c in range(18):
            for hf in range(2):
                add_chunk(("WI", c * 2 + hf), wi_v, WI[c * 2 + hf], hf * 16, 16, c * 512, 512)
        wo_v = w_out.rearrange("(k p) n -> p k n", p=128)
        for c in range(16):
            add_chunk(("WO", c), wo_v, WO[c], 0, 32, c * 256, 256)
        wu_v = w_up.rearrange("(k p) n -> p k n", p=128)
        wd_v = w_down.rearrange("(k p) n -> p k n", p=128)
        for j in range(8):
            for fq in range(8):
                c = j * 8 + fq
                add_chunk(("WU", c), wu_v, WU[c], 0, 32, c * 256, 256)
            for db in range(8):
                add_chunk(("WD", j * 8 + db), wd_v, WD[j * 8 + db], j * 16, 16, db * 512, 512)
        cstate = {"i": 0}

        def issue_cast(deps):
            i = cstate["i"]
            if i >= len(cchunks):
                return False
            key, src, dst = cchunks[i]
            ch = cch[i % NCC]
            ev = k.dma(ch, dst, src, [Ev(ch.sem, ch.cnt)] + deps, q=POOL)
            ready[key] = [ev]
            cstate["i"] = i + 1
            return True

        def pump(n_):
            for _ in range(n_):
                deps = [PE.last] if PE.last is not None else []
                if not issue_cast(deps):
                    return

        def ensure(key):
            while key not in ready:
                assert issue_cast([])
            return ready[key]

        k.pump = pump
        k.ensure = ensure
        if stop_after == "S":
            return nc

        def rms_part1(xb, xbuf, xn, xnbuf, ssx, rsx, statbuf, src_ap, chan):
            deps = xbuf.wdeps()
            xbuf.start_write()
            ev = k.dma(chan, xb[:], src_ap, deps)
            xbuf.wrote(ev)
            ACT.wait(xbuf.rdeps() + xnbuf.wdeps() + statbuf.wdeps())
            xnbuf.start_write()
            statbuf.start_write()
            ev = ACT.sig(nc.scalar.activation(out=xn[:], in_=xb[:], func=AF.Square, accum_out=ssx[:, 0:1]))
            xbuf.read(ev)
            ACT.wait([ev])
            ev = ACT.sig(nc.scalar.activation(out=rsx[:, 0:1], in_=ssx[:, 0:1], func=AF.Sqrt, scale=1.0 / D, bias=epsb[:, 0:1]))
            DVE.wait([ev])
            ev = DVE.sig(nc.vector.reciprocal(out=rsx[:, 0:1], in_=rsx[:, 0:1]))
            DVE.wait([ev])
            ev = DVE.sig(nc.vector.tensor_scalar(out=xn[:], in0=xb[:], scalar1=rsx[:, 0:1], scalar2=None, op0=ALU.mult))
            xbuf.read(ev)
            xnbuf.wrote(ev)
            statbuf.wrote(ev)

        def rms_part2(xn, xnbuf, tpx, tpbufs, hT, hbuf_deps, hbuf, tb, gcol0, tpi):
            for g in range(4):
                t = tpx[(tpi + g) % len(tpx)]
                tbuf = tpbufs[(tpi + g) % len(tpx)]
                PE.wait(xnbuf.rdeps() + tbuf.wdeps())
                tbuf.start_write()
                for i in range(8):
                    kc = g * 8 + i
                    ins = nc.tensor.transpose(out=t[:, i * 128:(i + 1) * 128], in_=xn[:, kc * 128:(kc + 1) * 128], identity=ident[:])
                ev = PE.sig(ins)
                xnbuf.read(ev)
                tbuf.wrote(ev)
                for i in range(8):
                    kc = g * 8 + i
                    E = ACT if ((tpi + g) % 2 == 0) else DVE
                    E.wait(tbuf.rdeps() + hbuf_deps)
                    if E is ACT:
                        ins = nc.scalar.activation(out=hT[:, kc, tb * 128:(tb + 1) * 128], in_=t[:, i * 128:(i + 1) * 128],
                                                   func=AF.Copy, scale=pv[:, gcol0 + kc:gcol0 + kc + 1])
                    else:
                        ins = nc.vector.tensor_scalar(out=hT[:, kc, tb * 128:(tb + 1) * 128], in0=t[:, i * 128:(i + 1) * 128],
                                                      scalar1=pv[:, gcol0 + kc:gcol0 + kc + 1], scalar2=None, op0=ALU.mult)
                    ev = E.sig(ins)
                    tbuf.read(ev)
                    hbuf.wrote(ev)

        def phase_A():
            with ExitStack() as s1:
                ws = make_ws(4, s1, "a")
                NXB = 2
                xb = [sb(f"xb{i}", [128, D], F32, s1) for i in range(NXB)]
                xbb = [Buf() for _ in range(NXB)]
                xch = [Chan(k, f"xch{i}") for i in range(NXB)]
                xn = [sb(f"xn{i}", [128, D], BF16, s1) for i in range(NXB)]
                xnb = [Buf() for _ in range(NXB)]
                stt = [sb(f"stt{i}", [128, 2], F32, s1) for i in range(NXB)]
                sttb = [Buf() for _ in range(NXB)]
                hT = [sb(f"hT{i}", [128, 32, 512], BF16, s1) for i in range(2)]
                hTb = [Buf() for _ in range(2)]
                sq = [sb(f"sq{i}", [128, 512], F32, s1) for i in range(2)]
                sqb = [Buf() for _ in range(2)]
                ss = [sb(f"ss{i}", [128, 8], F32, s1) for i in range(2)]
                NQN = 8
                qn = [sb(f"qn{i}", [128, 512], BF16, s1) for i in range(NQN)]
                qnb = [Buf() for _ in range(NQN)]
                stg = [sb(f"stg{i}", [128, 512], BF16, s1) for i in range(4)]
                stgb = [Buf() for _ in range(4)]
                stch = [Chan(k, f"stch{i}") for i in range(4)]
                NPJ = 5
                pj = [ps(f"pj{i}", [128, 512], F32, s1) for i in range(NPJ)]
                pjb = [Buf() for _ in range(NPJ)]
                tpx = [ps(f"tpx{i}", [128, 1024], BF16, s1) for i in range(2)]
                tpxb = [Buf() for _ in range(2)]
                tpq = [ps(f"tpq{i}", [128, 1024], BF16, s1) for i in range(1)]
                tpqb = [Buf() for _ in range(1)]

                full_chunks = list(range(18))
                kv_chunks = [4, 5, 6, 7, 8, 9, 10, 11, 16, 17]
                tiles = [(t, full_chunks) for t in range(6)] + [(t, kv_chunks) for t in (6, 7)]
                aps = []
                for t, chs in tiles:
                    for c in chs:
                        for hf in range(2):
                            aps.append((("WI", c * 2 + hf), WI[c * 2 + hf]))
                ws.reset(aps)
                ws.topup()
                cnt = {"x": 0, "qn": 0, "stg": 0, "tpq": 0, "sq": 0, "pj": 0, "step": 0}

                def norm1(t, tb):
                    i = cnt["x"] % NXB
                    blk = t * 4 + tb
                    rms_part1(xb[i], xbb[i], xn[i], xnb[i], stt[i][:, 0:1], stt[i][:, 1:2], sttb[i],
                              xkv[blk * 128:(blk + 1) * 128, :], xch[i])
                    cnt["x"] += 1
                    return i

                def norm2(t, tb, i, hdeps):
                    rms_part2(xn[i], xnb[i], tpx, tpxb, hT[t % 2], hdeps, hTb[t % 2], tb, 0, 0)

                def dest_for(c, blk):
                    tsl = slice(blk * 128, (blk + 1) * 128)
                    if c < 4:
                        return ("T", QTna[4 * c:4 * c + 4, :, tsl], 0)
                    if c < 8:
                        return ("T", KTna[4 * (c - 4):4 * (c - 4) + 4, :, tsl], 1)
                    if c < 12:
                        return ("V", Vna[tsl, (c - 8) * 512:(c - 7) * 512], None)
                    if c < 16:
                        return ("T", QTsw[4 * (c - 12):4 * (c - 12) + 4, :, tsl], 2)
                    if c == 16:
                        return ("T", KTsw[0:4, :, tsl], 3)
                    return ("V", Vsw[tsl, 0:512], None)

                def evac1(t, tb_, c, bk):
                    blk = t * 4 + tb_
                    tb = bk
                    kind, dst, gi = dest_for(c, blk)
                    if kind == "V":
                        si = cnt["stg"] % 4
                        cnt["stg"] += 1
                        E = ACT if (tb_ % 2 == 0) else DVE
                        E.wait(pjb[tb].rdeps() + stgb[si].wdeps())
                        stgb[si].start_write()
                        if E is ACT:
                            ins = nc.scalar.copy(out=stg[si][:], in_=pj[tb][:])
                        else:
                            ins = nc.vector.tensor_copy(out=stg[si][:], in_=pj[tb][:])
                        ev = E.sig(ins)
                        pjb[tb].read(ev)
                        stgb[si].wrote(ev)
                        ev2 = k.dma(stch[si], dst, stg[si][:], stgb[si].rdeps())
                        stgb[si].read(ev2)
                        return None
                    qi = cnt["sq"] % 2
                    cnt["sq"] += 1
                    ni = cnt["qn"] % NQN
                    cnt["qn"] += 1
                    ACT.wait(pjb[tb].rdeps() + sqb[qi].wdeps())
                    sqb[qi].start_write()
                    ev = ACT.sig(nc.scalar.activation(out=sq[qi][:], in_=pj[tb][:], func=AF.Square))
                    pjb[tb].read(ev)
                    DVE.wait([ev])
                    ev = DVE.sig(nc.vector.tensor_reduce(out=ss[qi][:, 0:4], in_=sq[qi][:].rearrange("p (h d) -> p h d", h=4),
                                                         axis=AX.X, op=ALU.add))
                    ACT.wait([ev])
                    ev = ACT.sig(nc.scalar.activation(out=ss[qi][:, 4:8], in_=ss[qi][:, 0:4], func=AF.Sqrt, scale=1.0 / 128,
                                                      bias=epsb[:, 0:1]))
                    DVE.wait([ev])
                    ev = DVE.sig(nc.vector.reciprocal(out=ss[qi][:, 4:8], in_=ss[qi][:, 4:8]))
                    DVE.wait([ev] + qnb[ni].wdeps())
                    qnb[ni].start_write()
                    ev = DVE.sig(nc.vector.tensor_tensor(out=qn[ni][:].rearrange("p (h d) -> p h d", h=4),
                                                         in0=pj[tb][:].rearrange("p (h d) -> p h d", h=4),
                                                         in1=ss[qi][:, 4:8].unsqueeze(2).to_broadcast([128, 4, 128]), op=ALU.mult))
                    pjb[tb].read(ev)
                    qnb[ni].wrote(ev)
                    sqb[qi].wrote(ev)
                    return (ni, dst, gi)

                def evac2(state):
                    if state is None:
                        return
                    ni, dst, gi = state
                    ti = 0
                    cnt["tpq"] += 1
                    PE.wait(qnb[ni].rdeps() + tpqb[ti].wdeps())
                    tpqb[ti].start_write()
                    for hh in range(4):
                        ins = nc.tensor.transpose(out=tpq[ti][:, hh * 128:(hh + 1) * 128], in_=qn[ni][:, hh * 128:(hh + 1) * 128],
                                                  identity=ident[:])
                    ev = PE.sig(ins)
                    qnb[ni].read(ev)
                    tpqb[ti].wrote(ev)
                    si = cnt["stg"] % 4
                    cnt["stg"] += 1
                    ACT.wait(tpqb[ti].rdeps() + stgb[si].wdeps())
                    stgb[si].start_write()
                    ev = ACT.sig(nc.scalar.activation(out=stg[si][:], in_=tpq[ti][:, 0:512], func=AF.Copy, scale=gsc[:, gi:gi + 1]))
                    tpqb[ti].read(ev)
                    stgb[si].wrote(ev)
                    ev2 = k.dma(stch[si], dst.rearrange("h d t -> d h t"), stg[si][:].rearrange("d (h t) -> d h t", h=4),
                                stgb[si].rdeps())
                    stgb[si].read(ev2)

                for tb in range(4):
                    i = norm1(0, tb)
                    hd = hTb[0].wdeps() if tb == 0 else []
                    if tb == 0:
                        hTb[0].start_write()
                    norm2(0, tb, i, hd)
                pending = []
                for ti_, (t, chs) in enumerate(tiles):
                    hcur = hT[t % 2]
                    hb = hTb[t % 2]
                    nxt = tiles[ti_ + 1][0] if ti_ + 1 < len(tiles) else None
                    nstate = {}
                    for ci, c in enumerate(chs):
                        if nxt is not None and ci < 8 and ci % 2 == 0:
                            nstate[ci // 2] = norm1(nxt, ci // 2)
                        bks = []
                        for tb in range(4):
                            bks.append(cnt["pj"] % NPJ)
                            cnt["pj"] += 1
                        for hf in range(2):
                            wsl, wb = ws.get()
                            wv = wsl[:].rearrange("p (a b) -> p a b", a=16)
                            for tb in range(4):
                                bk = bks[tb]
                                deps = wb.rdeps() + hb.rdeps()
                                if hf == 0:
                                    deps = deps + pjb[bk].wdeps()
                                PE.wait(deps)
                                if hf == 0:
                                    pjb[bk].start_write()
                                for kc in range(16):
                                    ins = nc.tensor.matmul(pj[bk][:], hcur[:, hf * 16 + kc, tb * 128:(tb + 1) * 128], wv[:, kc, :],
                                                           start=(hf == 0 and kc == 0), stop=(hf == 1 and kc == 15))
                                if hf == 1 or tb == 3:
                                    ev = PE.sig(ins)
                                    wb.read(ev)
                                    hb.read(ev)
                                    if hf == 1:
                                        pjb[bk].wrote(ev)
                                if hf == 1 and pending:
                                    evac2(pending[tb])
                            cnt["step"] += 1
                            if cnt["step"] % 3 == 0:
                                k.pump(1)
                        pending = [evac1(t, tb, c, bks[tb]) for tb in range(4)]
                        if nxt is not None and ci < 8 and ci % 2 == 1:
                            tb2 = ci // 2
                            hd = hTb[nxt % 2].wdeps() if tb2 == 0 else []
                            if tb2 == 0:
                                hTb[nxt % 2].start_write()
                            norm2(nxt, tb2, nstate[tb2], hd)
                for stt_ in pending:
                    evac2(stt_)
                k.barrier()

        phase_A()
        if stop_after == "A":
            return nc

        def kvblock(j, dp):
            if j < 16:
                jj = j + dp
                return jj if 0 <= jj < 16 else None
            jq = j - 16 + dp
            if jq < 0:
                return 24 + 3 + jq
            if jq >= 8:
                return 27 + jq - 8
            return 16 + jq

        NKB = 11
        B_chans = {}
        ot_events = {}

        def alloc_B(stk, tag, cset=0):
            BR = {}
            if cset not in B_chans:
                B_chans[cset] = ([Chan(k, f"hch{cset}_{i}") for i in range(2)], [Chan(k, f"och{cset}_{i}") for i in range(2)])
            BR["hch"], BR["och"] = B_chans[cset]
            BR["nb"] = 0
            BR["KT"] = [sb(f"KT{tag}{i}", [128, NKB * 128], BF16, stk) for i in range(2)]
            BR["VV"] = [sb(f"VV{tag}{i}", [128, NKB, 128], BF16, stk) for i in range(2)]
            BR["QT"] = [sb(f"QT{tag}{i}", [128, 512], BF16, stk) for i in range(2)]
            BR["BT"] = [sb(f"BT{tag}{i}", [128, 7 * 128], F32, stk) for i in range(2)]
            BR["BM"] = [sb(f"BM{tag}{i}", [16, 512], F32, stk) for i in range(2)]
            BR["OS"] = [sb(f"OS{tag}{i}", [128, 512], BF16, stk) for i in range(2)]
            BR["tmp"] = sb(f"tmpB{tag}", [128, 8, 128], F32, stk)
            BR["PT"] = [sb(f"PTB{tag}{i}", [128, 8, 128], BF16, stk) for i in range(2)]
            BR["rden"] = sb(f"rdenB{tag}", [128, 128], F32, stk)
            BR["S"] = [ps(f"SpsB{tag}{i}", [128, 512], F32, stk) for i in range(2)]
            BR["O"] = ps(f"OpsB{tag}", [128, 512], F32, stk)
            BR["hbuf"] = [Buf() for _ in range(2)]
            BR["osb"] = [Buf() for _ in range(2)]
            BR["tmpb"] = Buf()
            BR["PTb"] = [Buf() for _ in range(2)]
            BR["rdb"] = Buf()
            BR["Sb"] = Buf()
            BR["Ob"] = Buf()
            return BR

        def kvblock(j, dp):
            if j < 16:
                jj = j + dp
                return jj if 0 <= jj < 16 else None
            jq = j - 16 + dp
            if jq < 0:
                return 24 + 3 + jq
            if jq >= 8:
                return 27 + jq - 8
            return 16 + jq

        def runs_of(blks):
            out = []
            i = 0
            while i < len(blks):
                j2 = i
                while j2 + 1 < len(blks) and blks[j2 + 1] == blks[j2] + 1:
                    j2 += 1
                out.append((i, blks[i], j2 - i + 1))
                i = j2 + 1
            return out

        def gen_B(T, BR, heads):
            B_hch, B_och = BR["hch"], BR["och"]
            Sps = BR["S"]
            Ops = BR["O"]
            B_KT, B_VV, B_QT, B_BT, B_BM, B_OS = BR["KT"], BR["VV"], BR["QT"], BR["BT"], BR["BM"], BR["OS"]
            B_tmp, B_PT, B_rden = BR["tmp"], BR["PT"], BR["rden"]
            B_hbuf, B_osb, B_tmpb, B_PTb, B_rdb, B_Sb, B_Ob = (BR["hbuf"], BR["osb"], BR["tmpb"], BR["PTb"], BR["rdb"], BR["Sb"],
                                                               BR["Ob"])
            js = list(range(4 * T, 4 * T + 4))
            kb = []
            for j in js:
                for dp in range(-3, 4):
                    blk = kvblock(j, dp)
                    if blk is not None and blk not in kb:
                        kb.append(blk)
            kb = sorted(kb) + [META_BLK]
            assert len(kb) <= NKB
            pos = {blk: i for i, blk in enumerate(kb)}
            runs = runs_of(kb)
            tsl = slice(T * 512, (T + 1) * 512)
            ot_events.setdefault(T, [])

            def load_head(hi):
                typ, h = heads[hi]
                s_ = hi % 2
                deps = B_hbuf[s_].wdeps()
                B_hbuf[s_].start_write()
                ch = B_hch[s_]
                if typ == "na":
                    ksrc, vsrc, vcol, qsrc, bsrc, nb_ = KTna[h], Vna, h, QTna[h], bna[h], 7
                else:
                    g = h // 4
                    ksrc, vsrc, vcol, qsrc, bsrc, nb_ = KTsw[g], Vsw, g, QTsw[h], bsw[h], 3
                ev = None
                for (i0, b0, n_) in runs:
                    ev = k.dma(ch, B_KT[s_][:, i0 * 128:(i0 + n_) * 128], ksrc[:, b0 * 128:(b0 + n_) * 128], deps)
                    deps = []
                    ev = k.dma(ch, B_VV[s_][:, i0:i0 + n_, :],
                               vsrc[b0 * 128:(b0 + n_) * 128, vcol * 128:(vcol + 1) * 128].rearrange("(b p) d -> p b d", p=128), [])
                ev = k.dma(ch, B_QT[s_][:], qsrc[:, tsl], [])
                ev = k.dma(ch, B_BT[s_][:, 0:nb_ * 128], bsrc[:, :], [])
                if typ == "sw":
                    ev = k.dma(ch, B_BM[s_][:], bmeta[h][:, tsl], [])
                B_hbuf[s_].wrote(ev)

            def Sview(ci):
                return Sps[ci // 4][:, (ci % 4) * 128:(ci % 4 + 1) * 128]

            prev = None

            def finish(pb):
                s_, hglob, jl, cl, p_, hb, is_last_j = pb
                PE.wait(B_PTb[p_].rdeps() + hb.rdeps() + B_Ob.wdeps())
                B_Ob.start_write()
                n_ = len(cl)
                for ci, (kind, blk, dpi) in enumerate(cl):
                    first = ci == 0
                    last = ci == n_ - 1
                    if kind == "k":
                        nc.tensor.matmul(Ops[:, 0:128], B_VV[s_][:, pos[blk], :], B_PT[p_][:, ci, :], start=first, stop=last)
                        ins = nc.tensor.matmul(Ops[:, 128:256], ones[:], B_PT[p_][:, ci, :], start=False, stop=last,
                                               skip_group_check=True)
                    else:
                        nc.tensor.matmul(Ops[:, 0:128], B_VV[s_][0:16, pos[blk], :], B_PT[p_][0:16, ci, :], start=first, stop=last)
                        ins = nc.tensor.matmul(Ops[:, 128:256], ones[0:16, :], B_PT[p_][0:16, ci, :], start=False, stop=last,
                                               skip_group_check=True)
                ev = PE.sig(ins)
                B_PTb[p_].read(ev)
                hb.read(ev)
                B_Ob.wrote(ev)
                DVE.wait(B_Ob.rdeps() + B_rdb.wdeps())
                B_rdb.start_write()
                ev = DVE.sig(nc.vector.tensor_scalar(out=B_rden[:], in0=Ops[:, 128:256], scalar1=esink[:, hglob:hglob + 1],
                                                     scalar2=None, op0=ALU.add))
                B_Ob.read(ev)
                DVE.wait([ev])
                ev = DVE.sig(nc.vector.reciprocal(out=B_rden[:], in_=B_rden[:]))
                B_rdb.wrote(ev)
                DVE.wait([ev] + B_osb[s_].wdeps())
                ev = DVE.sig(nc.vector.tensor_tensor(out=B_OS[s_][:, jl * 128:(jl + 1) * 128], in0=Ops[:, 0:128], in1=B_rden[:],
                                                     op=ALU.mult))
                B_Ob.read(ev)
                B_rdb.read(ev)
                B_osb[s_].wrote(ev)
                if is_last_j:
                    ev = k.dma(B_och[s_], OT[hglob][:, tsl], B_OS[s_][:], B_osb[s_].rdeps())
                    B_osb[s_].read(ev)
                    ot_events[T].append(ev)

            load_head(0)
            for hi, (typ, h) in enumerate(heads):
                s_ = hi % 2
                hb = B_hbuf[s_]
                hglob = h if typ == "na" else 16 + h
                dps = list(range(-3, 4)) if typ == "na" else [-1, 0, 1]
                for jl, j in enumerate(js):
                    cl = []
                    for dpi, dp in enumerate(dps):
                        blk = kvblock(j, dp)
                        if blk is not None:
                            cl.append(("k", blk, dpi))
                    cl.append(("m", META_BLK, None))
                    if prev is not None:
                        finish(prev)
                    if jl == 0 and hi + 1 < len(heads):
                        load_head(hi + 1)
                    if jl == 0:
                        pass
                    p_ = BR["nb"] % 2
                    BR["nb"] += 1
                    PE.wait(hb.rdeps() + B_Sb.wdeps())
                    B_Sb.start_write()
                    qsl = B_QT[s_][:, jl * 128:(jl + 1) * 128]
                    for ci, (kind, blk, dpi) in enumerate(cl):
                        if kind == "k":
                            ins = nc.tensor.matmul(Sview(ci), B_KT[s_][:, pos[blk] * 128:(pos[blk] + 1) * 128], qsl, start=True, stop=True)
                        else:
                            ins = nc.tensor.matmul(Sview(ci)[0:16, :], B_KT[s_][:, pos[blk] * 128:pos[blk] * 128 + 16], qsl,
                                                   start=True, stop=True)
                    ev = PE.sig(ins)
                    hb.read(ev)
                    B_Sb.wrote(ev)
                    DVE.wait(B_Sb.rdeps() + hb.rdeps() + B_tmpb.wdeps())
                    B_tmpb.start_write()
                    for ci, (kind, blk, dpi) in enumerate(cl):
                        if kind == "k":
                            ins = nc.vector.tensor_tensor(out=B_tmp[:, ci, :], in0=Sview(ci), in1=B_BT[s_][:, dpi * 128:(dpi + 1) * 128],
                                                          op=ALU.add)
                        elif typ == "na":
                            ins = nc.vector.tensor_copy(out=B_tmp[0:16, ci, :], in_=Sview(ci)[0:16, :])
                        else:
                            ins = nc.vector.tensor_tensor(out=B_tmp[0:16, ci, :], in0=Sview(ci)[0:16, :],
                                                          in1=B_BM[s_][:, jl * 128:(jl + 1) * 128], op=ALU.add)
                    ev = DVE.sig(ins)
                    B_Sb.read(ev)
                    hb.read(ev)
                    B_tmpb.wrote(ev)
                    ACT.wait(B_tmpb.rdeps() + B_PTb[p_].wdeps())
                    B_PTb[p_].start_write()
                    for ci, (kind, blk, dpi) in enumerate(cl):
                        if kind == "k":
                            if typ == "na":
                                e0 = (j * 7 + dpi) * 2
                                nc.scalar.activation(out=B_PT[p_][:, ci, 0:64], in_=B_tmp[:, ci, 0:64], func=AF.Exp,
                                                     bias=mna[:, e0:e0 + 1])
                                ins = nc.scalar.activation(out=B_PT[p_][:, ci, 64:128], in_=B_tmp[:, ci, 64:128], func=AF.Exp,
                                                           bias=mna[:, e0 + 1:e0 + 2])
                            else:
                                e0 = j * 3 + dpi
                                ins = nc.scalar.activation(out=B_PT[p_][:, ci, :], in_=B_tmp[:, ci, :], func=AF.Exp,
                                                           bias=msw[:, e0:e0 + 1])
                        else:
                            ins = nc.scalar.activation(out=B_PT[p_][0:16, ci, :], in_=B_tmp[0:16, ci, :], func=AF.Exp)
                    ev = ACT.sig(ins)
                    B_tmpb.read(ev)
                    B_PTb[p_].wrote(ev)
                    prev = (s_, hglob, jl, cl, p_, hb, jl == 3)
                    yield
            finish(prev)
            yield

        bgen = {"g": None, "T": None, "hook": 0, "gs": []}
        ALL_HEADS = [("na", h) for h in range(16)] + [("sw", h) for h in range(16)]

        def pumpB(n_=1):
            for _ in range(n_):
                if not bgen["gs"]:
                    bgen["g"] = None
                    return
                g_ = bgen["gs"].pop(0)
                try:
                    next(g_)
                    bgen["gs"].append(g_)
                except StopIteration:
                    pass
                if not bgen["gs"]:
                    bgen["g"] = None

        def hookB():
            pumpB(1)
            bgen["hook"] += 1
            if bgen["hook"] % 3 == 0:
                k.pump(1)

        def startB(T, rsets):
            n_ = len(rsets)
            bgen["gs"] = [gen_B(T, R_, ALL_HEADS[i::n_]) for i, R_ in enumerate(rsets)]
            bgen["g"] = True
            bgen["T"] = T

        def drainB():
            while bgen["g"] is not None:
                pumpB(1)

        with ExitStack() as sB0:
            startB(0, [alloc_B(sB0, "s", 0), alloc_B(sB0, "t", 1)])
            nstep = 0
            while bgen["g"] is not None:
                pumpB(1)
                nstep += 1
                if nstep % 8 == 0:
                    k.pump(1)
            k.barrier()
        if stop_after == "B":
            return nc

        def phase_CD():
            with ExitStack() as s1:
                BRc = alloc_B(s1, "c", 0)
                ws = make_ws(3, s1, "c")
                x1 = sb("x1", [128, 4, D], F32, s1)
                x1b = [Buf() for _ in range(4)]
                xch = [Chan(k, f"x1ch{i}") for i in range(4)]
                ych = Chan(k, "ych")
                hT = sb("h2T", [128, 32, 512], BF16, s1)
                hTb = Buf()
                och = Chan(k, "otch")
                uT = sb("uT", [128, 16, 512], BF16, s1)
                uTb = Buf()
                xn = uT[:].rearrange("p a b -> p (a b)")[:, 0:D]
                xnb = uTb
                stt = sb("stt2", [128, 8], F32, s1)
                sttb = Buf()
                ssp = sb("ssp", [128, 64], F32, s1)
                sspb = Buf()
                sqj = sb("sqj", [128, 256], BF16, s1)
                NXS = 4
                xs = [sb(f"xs{i}", [128, 256], F32, s1) for i in range(NXS)]
                xsb = [Buf() for _ in range(NXS)]
                xsch = [Chan(k, f"xsch{i}") for i in range(NXS)]
                rl = [sb(f"rl{i}", [128, 512], F32, s1) for i in range(2)]
                rlb = [Buf() for _ in range(2)]
                pc = [ps(f"pc{i}", [128, 512], F32, s1) for i in range(2)]
                pcb = [Buf() for _ in range(2)]
                tpx = [ps("tpy0", [128, 1024], BF16, s1)]
                tpxb = [Buf()]
                pu = [ps(f"pu{i}", [128, 512], F32, s1) for i in range(2)]
                pub = [Buf() for _ in range(2)]
                tpx.append(pu[0][:].bitcast(BF16))
                tpxb.append(pub[0])
                cnt = {"pc": 0, "pu": 0, "rl": 0}

                aps = []
                for t in range(6):
                    for c in range(16):
                        aps.append((("WO", c), WO[c]))
                    for j in range(8):
                        for fq in range(8):
                            aps.append((("WU", j * 8 + fq), WU[j * 8 + fq]))
                        for db in range(8):
                            aps.append((("WD", j * 8 + db), WD[j * 8 + db]))
                ws.reset(aps)
                ws.topup()

                def load_oT(t_):
                    deps = hTb.wdeps() + ot_events[t_]
                    hTb.start_write()
                    ev_ = k.dma(och, hT[:], OT[:, :, t_ * 512:(t_ + 1) * 512].rearrange("h d t -> d h t"), deps)
                    hTb.wrote(ev_)

                drainB()
                load_oT(0)
                for t in range(6):
                    tok0 = t * 512
                    if t + 1 < 6:
                        startB(t + 1, [BRc])
                    xpieces = [(c_, tb_) for c_ in range(16) for tb_ in range(4)]
                    xstate = {"i": 0}

                    def issue_x():
                        i_ = xstate["i"]
                        if i_ >= len(xpieces):
                            return
                        c_, tb_ = xpieces[i_]
                        sl_ = i_ % NXS
                        deps_ = xsb[sl_].wdeps()
                        xsb[sl_].start_write()
                        ev_ = k.dma(xsch[sl_], xs[sl_][:], xkv[tok0 + tb_ * 128:tok0 + (tb_ + 1) * 128, c_ * 256:(c_ + 1) * 256], deps_)
                        xsb[sl_].wrote(ev_)
                        xstate["i"] = i_ + 1

                    for _ in range(NXS - 1):
                        issue_x()
                    sdeps = sspb.wdeps()
                    sspb.start_write()
                    for c in range(16):
                        wsl, wb = ws.get()
                        wv = wsl[:].rearrange("p (a b) -> p a b", a=32)
                        for tb in range(4):
                            bi = cnt["pc"] % 2
                            cnt["pc"] += 1
                            PE.wait(wb.rdeps() + hTb.rdeps() + pcb[bi].wdeps())
                            pcb[bi].start_write()
                            for kc in range(32):
                                ins = nc.tensor.matmul(pc[bi][:, 0:256], hT[:, kc, tb * 128:(tb + 1) * 128], wv[:, kc, :],
                                                       start=(kc == 0), stop=(kc == 31))
                            ev = PE.sig(ins)
                            wb.read(ev)
                            hTb.read(ev)
                            pcb[bi].wrote(ev)
                            issue_x()
                            sl = (c * 4 + tb) % NXS
                            DVE.wait(pcb[bi].rdeps() + xsb[sl].rdeps() + x1b[tb].wdeps())
                            ev = DVE.sig(nc.vector.tensor_tensor(out=x1[:, tb, c * 256:(c + 1) * 256], in0=pc[bi][:, 0:256],
                                                                 in1=xs[sl][:], op=ALU.add))
                            pcb[bi].read(ev)
                            xsb[sl].read(ev)
                            x1b[tb].wrote(ev)
                            ACT.wait([ev] + sdeps)
                            ev = ACT.sig(nc.scalar.activation(out=sqj[:], in_=x1[:, tb, c * 256:(c + 1) * 256], func=AF.Square,
                                                              accum_out=ssp[:, tb * 16 + c:tb * 16 + c + 1]))
                            x1b[tb].read(ev)
                            sspb.wrote(ev)
                            if tb % 2 == 1:
                                hookB()
                    hdeps = hTb.wdeps()
                    hTb.start_write()
                    DVE.wait(sspb.rdeps() + sttb.wdeps())
                    sttb.start_write()
                    ev = DVE.sig(nc.vector.tensor_reduce(out=stt[:, 0:4], in_=ssp[:].rearrange("p (t c) -> p t c", t=4), axis=AX.X,
                                                         op=ALU.add))
                    sspb.read(ev)
                    ACT.wait([ev])
                    ev = ACT.sig(nc.scalar.activation(out=stt[:, 4:8], in_=stt[:, 0:4], func=AF.Sqrt, scale=1.0 / D,
                                                      bias=epsb[:, 0:1]))
                    DVE.wait([ev])
                    ev = DVE.sig(nc.vector.reciprocal(out=stt[:, 4:8], in_=stt[:, 4:8]))
                    sttb.wrote(ev)
                    for tb in range(4):
                        DVE.wait(x1b[tb].rdeps() + xnb.wdeps() + sttb.rdeps())
                        xnb.start_write()
                        ev = DVE.sig(nc.vector.tensor_scalar(out=xn, in0=x1[:, tb, :], scalar1=stt[:, 4 + tb:5 + tb], scalar2=None,
                                                             op0=ALU.mult))
                        x1b[tb].read(ev)
                        xnb.wrote(ev)
                        sttb.read(ev)
                        rms_part2(xn, xnb, tpx, tpxb, hT, hdeps, hTb, tb, 32, 2 * tb)
                    for j in range(8):
                        udeps = uTb.wdeps()
                        uTb.start_write()
                        for fq in range(8):
                            wsl, wb = ws.get()
                            wv = wsl[:].rearrange("p (a b) -> p a b", a=32)
                            for f2 in range(2):
                                fb_ = fq * 2 + f2
                                ui = cnt["pu"] % 2
                                cnt["pu"] += 1
                                PE.wait(wb.rdeps() + hTb.rdeps() + pub[ui].wdeps())
                                pub[ui].start_write()
                                for kc in range(32):
                                    ins = nc.tensor.matmul(pu[ui][:], wv[:, kc, f2 * 128:(f2 + 1) * 128], hT[:, kc, :],
                                                           start=(kc == 0), stop=(kc == 31))
                                ev = PE.sig(ins)
                                wb.read(ev)
                                hTb.read(ev)
                                pub[ui].wrote(ev)
                                ri = cnt["rl"] % 2
                                cnt["rl"] += 1
                                DVE.wait(pub[ui].rdeps() + rlb[ri].wdeps())
                                rlb[ri].start_write()
                                ev = DVE.sig(nc.vector.tensor_scalar(out=rl[ri][:], in0=pu[ui][:], scalar1=0.0, scalar2=None,
                                                                     op0=ALU.max))
                                pub[ui].read(ev)
                                rlb[ri].wrote(ev)
                                POOL.wait(rlb[ri].rdeps() + udeps)
                                ev = POOL.sig(nc.gpsimd.tensor_tensor(out=uT[:, fb_, :], in0=rl[ri][:], in1=rl[ri][:], op=ALU.mult))
                                rlb[ri].read(ev)
                                uTb.wrote(ev)
                                hookB()
                        if j == 7 and t + 1 < 6:
                            drainB()
                            load_oT(t + 1)
                        for db in range(8):
                            wsl, wb = ws.get()
                            wv = wsl[:].rearrange("p (a b) -> p a b", a=16)
                            for tb in range(4):
                                bi = cnt["pc"] % 2
                                cnt["pc"] += 1
                                PE.wait(wb.rdeps() + uTb.rdeps() + pcb[bi].wdeps())
                                pcb[bi].start_write()
                                for fc in range(16):
                                    ins = nc.tensor.matmul(pc[bi][:], uT[:, fc, tb * 128:(tb + 1) * 128], wv[:, fc, :],
                                                           start=(fc == 0), stop=(fc == 15))
                                ev = PE.sig(ins)
                                wb.read(ev)
                                uTb.read(ev)
                                pcb[bi].wrote(ev)
                                DVE.wait(pcb[bi].rdeps() + x1b[tb].wdeps())
                                ev = DVE.sig(nc.vector.tensor_tensor(out=x1[:, tb, db * 512:(db + 1) * 512], in0=pc[bi][:],
                                                                     in1=x1[:, tb, db * 512:(db + 1) * 512], op=ALU.add))
                                pcb[bi].read(ev)
                                x1b[tb].wrote(ev)
                                if tb % 2 == 1:
                                    hookB()
                    for tb in range(4):
                        ev = k.dma(ych, y[tok0 + tb * 128:tok0 + (tb + 1) * 128, :], x1[:, tb, :], x1b[tb].rdeps())
                        x1b[tb].read(ev)
                k.barrier()

        phase_CD()
    return nc


def _t5_bucket(rel):
    nb = 16
    max_exact = 8
    ret = np.where(rel > 0, nb, 0)
    n = np.abs(rel)
    large = max_exact + (np.log(np.maximum(n, 1) / max_exact) / np.log(128 / max_exact) * (nb - max_exact)).astype(np.int64)
    large = np.minimum(large, nb - 1)
    return ret + np.where(n < max_exact, n, large)


def _host_prep(x_prompt, x_sample, meta_tokens, t5_bias, norm_attn, q_norm_na, k_norm_na, na_rpb, q_norm_swa, k_norm_swa,
               swa_sink, norm_mlp):
    f32 = np.float32
    xp = np.asarray(x_prompt, f32)[0]
    xs = np.asarray(x_sample, f32)
    rpb = np.asarray(na_rpb, f32)[0]
    t5 = np.asarray(t5_bias, f32)
    a = np.arange(2)[:, None]
    kc = np.arange(64)[None, :]
    cs = np.clip(np.arange(64) - 8, 0, 48)
    bna = np.empty((16, 128, 7, 128), f32)
    for dpi, dp in enumerate(range(-3, 4)):
        A = np.repeat(np.arange(2), 64)
        KC = np.tile(np.arange(64), 2)
        dr = 2 * dp + A[:, None] - A[None, :]
        dc = KC[:, None] - KC[None, :]
        col_in = (KC[:, None] >= cs[KC][None, :]) & (KC[:, None] < cs[KC][None, :] + 16)
        ok = col_in & (np.abs(dr) <= 7)
        g = rpb[:, np.clip(dr, -7, 7) + 7, np.clip(dc, -15, 15) + 15]
        bna[:, :, dpi, :] = np.where(ok[None], g, f32(NEG))
    bsw = np.empty((16, 128, 3, 128), f32)
    P = np.arange(128)
    for dpi, dp in enumerate((-1, 0, 1)):
        rel = (dp * 128 + P[:, None]) - P[None, :]
        ok = np.abs(rel) <= 128
        g = t5[_t5_bucket(rel)]
        bsw[:, :, dpi, :] = np.where(ok[None], g.transpose(2, 0, 1), f32(NEG))
    pv_common = np.zeros((128, 84), f32)
    pv_common[:, 0:32] = np.asarray(norm_attn, f32)[0].reshape(32, 128).T
    pv_common[:, 32:64] = np.asarray(norm_mlp, f32)[0].reshape(32, 128).T
    pv_common[:, 64] = np.asarray(q_norm_na, f32)[0]
    pv_common[:, 65] = np.asarray(k_norm_na, f32)[0]
    pv_common[:, 66] = np.asarray(q_norm_swa, f32)[0]
    pv_common[:, 67] = np.asarray(k_norm_swa, f32)[0]
    pv_common[:, 68:84] = np.asarray(swa_sink, f32)[0][None, :]
    ident = np.eye(128, dtype=f32)
    meta = np.asarray(meta_tokens, f32)
    per_core = []
    for c in range(NCORES):
        xkv = np.zeros((NKVB * 128, D), f32)
        xkv[0:2048] = xs[c]
        xkv[2048:3072] = xp[1024 * c:1024 * (c + 1)]
        for i in range(3):
            pr = 8 * c - 3 + i
            if pr >= 0:
                xkv[(24 + i) * 128:(25 + i) * 128] = xp[pr * 128:(pr + 1) * 128]
            pr = 8 * c + 8 + i
            if pr < 64:
                xkv[(27 + i) * 128:(28 + i) * 128] = xp[pr * 128:(pr + 1) * 128]
        xkv[META_BLK * 128:META_BLK * 128 + 16] = meta
        mcna = np.zeros((128, NQB, 7, 2), f32)
        mcsw = np.zeros((128, NQB, 3), f32)
        tpos = np.empty(NQB * 128, np.int64)
        for j in range(NQB):
            if j < 16:
                rows, jg = 32, j
                tpos[j * 128:(j + 1) * 128] = j * 128 + np.arange(128)
            else:
                rows, jg = 128, 8 * c + (j - 16)
                tpos[j * 128:(j + 1) * 128] = jg * 128 + np.arange(128)
            nbk = rows // 2
            for dpi, dp in enumerate(range(-3, 4)):
                for b in range(2):
                    qr = 2 * jg + b
                    rs = min(max(qr - 4, 0), rows - 8)
                    for a_ in range(2):
                        kr = 2 * (jg + dp) + a_
                        ok = (0 <= kr < rows) and (rs <= kr < rs + 8)
                        mcna[a_ * 64:(a_ + 1) * 64, j, dpi, b] = 0.0 if ok else NEG
            for dpi, dp in enumerate((-1, 0, 1)):
                ok = 0 <= jg + dp < nbk
                mcsw[:, j, dpi] = 0.0 if ok else NEG
        relm = np.arange(16)[:, None] - (16 + tpos)[None, :]
        bmeta = np.ascontiguousarray(t5[_t5_bucket(relm)].transpose(2, 0, 1))
        per_core.append({
            "xkv": xkv,
            "pvec": pv_common,
            "ident": ident,
            "bna": bna.reshape(16, 128, 7 * 128),
            "bsw": bsw.reshape(16, 128, 3 * 128),
            "bmeta": bmeta,
            "mcna": mcna.reshape(128, NQB * 14),
            "mcsw": mcsw.reshape(128, NQB * 3),
        })
    return per_core


_NC_CACHE = {}


def kernel(x_prompt, x_sample, meta_tokens, t5_bias, norm_attn, w_in, q_norm_na, k_norm_na, na_rpb, q_norm_swa, k_norm_swa,
           swa_sink, w_out, norm_mlp, w_up, w_down):
    per_core = _host_prep(x_prompt, x_sample, meta_tokens, t5_bias, norm_attn, q_norm_na, k_norm_na, na_rpb, q_norm_swa,
                          k_norm_swa, swa_sink, norm_mlp)
    wi = np.ascontiguousarray(np.asarray(w_in, np.float32)[0])
    wo = np.ascontiguousarray(np.asarray(w_out, np.float32)[0])
    wu = np.ascontiguousarray(np.asarray(w_up, np.float32)[0])
    wd = np.ascontiguousarray(np.asarray(w_down, np.float32)[0])
    for d in per_core:
        d.update({"w_in": wi, "w_out": wo, "w_up": wu, "w_down": wd})
    if "nc" not in _NC_CACHE:
        _NC_CACHE["nc"] = build_nc()
    nc = _NC_CACHE["nc"]
    res = run_bass_kernel_spmd(nc, per_core, core_ids=list(range(NCORES)))
    y_prompt = np.empty((1, 8192, D), np.float32)
    y_sample = np.empty((8, 2048, D), np.float32)
    for c in range(NCORES):
        yc = np.asarray(res.results[c]["y"])
        y_sample[c] = yc[0:2048]
        y_prompt[0, 1024 * c:1024 * (c + 1)] = yc[2048:3072]
    return (y_prompt, y_sample)
```

```python
import numpy as np
from contextlib import ExitStack
import concourse.bass as bass
import concourse.mybir as mybir
from concourse.bass_utils import run_bass_kernel_spmd

F32 = mybir.dt.float32
BF16 = mybir.dt.bfloat16
AF = mybir.ActivationFunctionType
ALU = mybir.AluOpType
AX = mybir.AxisListType

NCORES = 8
D = 4096
NKVB = 32
NQB = 24
META_BLK = 30
NEG = -30000.0
EPS = 1e-6
SEM_LIMIT = 30000


class Ev:
    __slots__ = ("sem", "val")

    def __init__(self, sem, val):
        self.sem = sem
        self.val = val


class Buf:
    def __init__(self):
        self.w = {}
        self.r = {}

    def rdeps(self):
        return [Ev(k, v) for k, v in self.w.items()]

    def wdeps(self):
        return [Ev(k, v) for k, v in self.w.items()] + [Ev(k, v) for k, v in self.r.items()]

    def start_write(self):
        self.w = {}
        self.r = {}

    def wrote(self, ev):
        self.w[ev.sem] = max(self.w.get(ev.sem, 0), ev.val)

    def read(self, ev):
        self.r[ev.sem] = max(self.r.get(ev.sem, 0), ev.val)


class Eng:
    def __init__(self, k, eng, name):
        self.k = k
        self.eng = eng
        self.name = name
        self.sem = None
        self.cnt = 0
        self.nep = 0
        self.waited = {}
        self.last = None

    def wait(self, evs):
        for ev in evs:
            if ev is None:
                continue
            if self.waited.get(ev.sem, 0) >= ev.val:
                continue
            self.eng.wait_ge(ev.sem, ev.val)
            self.waited[ev.sem] = ev.val

    def sig(self, ins):
        if self.sem is None or self.cnt >= SEM_LIMIT:
            self.sem = self.k.newsem(f"{self.name}{self.nep}")
            self.nep += 1
            self.cnt = 0
        self.cnt += 1
        ins.then_inc(self.sem, 1)
        ev = Ev(self.sem, self.cnt)
        self.last = ev
        return ev


class Chan:
    def __init__(self, k, name):
        self.sem = k.newsem(name)
        self.cnt = 0
        k.chans.append(self)

    def ev(self):
        return Ev(self.sem, self.cnt)


class K:
    def __init__(self, nc, st):
        self.nc = nc
        self.st = st
        self.nsem = 0
        self.chans = []
        self.PE = Eng(self, nc.tensor, "pe")
        self.ACT = Eng(self, nc.scalar, "act")
        self.DVE = Eng(self, nc.vector, "dve")
        self.POOL = Eng(self, nc.gpsimd, "pool")
        self.SP = Eng(self, nc.sync, "sp")
        self.engs = [self.PE, self.ACT, self.DVE, self.POOL]

    def newsem(self, name):
        self.nsem += 1
        return self.st.enter_context(self.nc.semaphore(name))

    def dma(self, chan, out, in_, deps, q=None):
        q = q or self.SP
        q.wait(deps)
        ins = q.eng.dma_start(out=out, in_=in_)
        chan.cnt += 16
        assert chan.cnt < SEM_LIMIT, chan.cnt
        ins.then_inc(chan.sem, 16)
        return Ev(chan.sem, chan.cnt)

    def barrier(self):
        evs = [e.last for e in self.engs if e.last is not None]
        evs += [c.ev() for c in self.chans if c.cnt > 0]
        for e in self.engs + [self.SP]:
            e.wait(evs)


class WStream:
    def __init__(self, k, slots, bufs, chans):
        self.k = k
        self.slots = slots
        self.bufs = bufs
        self.chans = chans
        self.n = len(slots)
        self.pos = 0
        self.reset([])

    def reset(self, aps):
        self.aps = aps
        self.issued = 0
        self.consumed = 0
        self.base = self.pos

    def _issue(self):
        i = self.issued
        s = (self.base + i) % self.n
        b = self.bufs[s]
        deps = b.wdeps()
        b.start_write()
        key, ap = self.aps[i]
        deps = deps + self.k.ensure(key)
        ev = self.k.dma(self.chans[s], self.slots[s][:], ap, deps)
        b.wrote(ev)
        self.issued += 1

    def topup(self):
        while self.issued < min(self.consumed + self.n, len(self.aps)):
            self._issue()

    def get(self):
        self.topup()
        s = (self.base + self.consumed) % self.n
        self.consumed += 1
        self.pos = self.base + self.consumed
        return self.slots[s], self.bufs[s]


def build_nc(stop_after=None, debug=False):
    nc = bass.Bass("TRN2", target_bir_lowering=False)
    dk = "ExternalOutput" if debug else "Internal"

    def din(name, shape, dt=F32):
        return nc.dram_tensor(name, list(shape), dt, kind="ExternalInput").ap()

    xkv = din("xkv", [NKVB * 128, D])
    w_in = din("w_in", [D, 9216])
    w_out = din("w_out", [D, D])
    w_up = din("w_up", [D, 4 * D])
    w_down = din("w_down", [4 * D, D])
    pvec = din("pvec", [128, 84])
    ident_in = din("ident", [128, 128])
    bna = din("bna", [16, 128, 7 * 128])
    bsw = din("bsw", [16, 128, 3 * 128])
    bmeta = din("bmeta", [16, 16, NQB * 128])
    mcna = din("mcna", [128, NQB * 14])
    mcsw = din("mcsw", [128, NQB * 3])
    y = nc.dram_tensor("y", [NQB * 128, D], F32, kind="ExternalOutput").ap()

    WI = nc.dram_tensor("WI", [36, 128, 8192], BF16, kind=dk).ap()
    WO = nc.dram_tensor("WO", [16, 128, 8192], BF16).ap()
    WU = nc.dram_tensor("WU", [64, 128, 8192], BF16).ap()
    WD = nc.dram_tensor("WD", [64, 128, 8192], BF16).ap()
    QTna = nc.dram_tensor("QTna", [16, 128, NQB * 128], BF16, kind=dk).ap()
    QTsw = nc.dram_tensor("QTsw", [16, 128, NQB * 128], BF16, kind=dk).ap()
    KTna = nc.dram_tensor("KTna", [16, 128, NKVB * 128], BF16, kind=dk).ap()
    KTsw = nc.dram_tensor("KTsw", [4, 128, NKVB * 128], BF16, kind=dk).ap()
    Vna = nc.dram_tensor("Vna", [NKVB * 128, 2048], BF16, kind=dk).ap()
    Vsw = nc.dram_tensor("Vsw", [NKVB * 128, 512], BF16, kind=dk).ap()
    OT = nc.dram_tensor("OT", [32, 128, NQB * 128], BF16, kind=dk).ap()

    with ExitStack() as st:
        k = K(nc, st)
        PE, ACT, DVE, POOL, SP = k.PE, k.ACT, k.DVE, k.POOL, k.SP

        def sb(name, shape, dt, stack=st):
            return stack.enter_context(nc.sbuf_tensor(name, list(shape), dt))

        def ps(name, shape, dt, stack=st):
            return stack.enter_context(nc.psum_tensor(name, list(shape), dt))

        ident = sb("ident_sb", [128, 128], BF16)
        ones = sb("ones", [128, 128], BF16)
        pv = sb("pv", [128, 84], F32)
        gsc = sb("gsc", [128, 4], F32)
        esink = sb("esink", [128, 32], F32)
        epsb = sb("epsb", [128, 1], F32)
        mna = sb("mna", [128, NQB * 14], F32)
        msw = sb("msw", [128, NQB * 3], F32)
        wchans = [Chan(k, f"wch{i}") for i in range(4)]

        def make_ws(nw, stack, tag):
            slots = [sb(f"w{tag}{i}", [128, 8192], BF16, stack) for i in range(nw)]
            return WStream(k, slots, [Buf() for _ in range(nw)], wchans[:nw])
        c_const = Chan(k, "cconst")
        with ExitStack() as s0:
            id32 = sb("id32", [128, 128], F32, s0)
            e1 = k.dma(c_const, id32[:], ident_in[:, :], [])
            e2 = k.dma(c_const, pv[:], pvec[:, :], [])
            e3 = k.dma(c_const, mna[:], mcna[:, :], [])
            e4 = k.dma(c_const, msw[:], mcsw[:, :], [])
            DVE.wait([e4])
            DVE.sig(nc.vector.tensor_copy(out=ident[:], in_=id32[:]))
            DVE.sig(nc.vector.memset(ones[:], 1.0))
            DVE.sig(nc.vector.memset(epsb[:], EPS))
            DVE.sig(nc.vector.memset(esink[:], 0.0))
            sc = 128.0 ** -0.5
            DVE.sig(nc.vector.tensor_scalar(out=gsc[:, 0:1], in0=pv[:, 64:65], scalar1=sc, scalar2=None, op0=ALU.mult))
            DVE.sig(nc.vector.tensor_copy(out=gsc[:, 1:2], in_=pv[:, 65:66]))
            DVE.sig(nc.vector.tensor_scalar(out=gsc[:, 2:3], in0=pv[:, 66:67], scalar1=sc, scalar2=None, op0=ALU.mult))
            ev = DVE.sig(nc.vector.tensor_copy(out=gsc[:, 3:4], in_=pv[:, 67:68]))
            ACT.wait([e4, ev])
            ACT.sig(nc.scalar.activation(out=esink[:, 16:32], in_=pv[:, 68:84], func=AF.Exp))
            k.barrier()

        NCC = 8
        cch = [Chan(k, f"cch{i}") for i in range(NCC)]
        cchunks = []
        ready = {}

        def add_chunk(key, view, dst, k0, nk, c0, ncol):
            cchunks.append((key, view[:, k0:k0 + nk, c0:c0 + ncol], dst.rearrange("p (k n) -> p k n", k=nk)))

        wi_v = w_in.rearrange("(k p) n -> p k n", p=128)
        for c in range(18):
            for hf in range(2):
                add_chunk(("WI", c * 2 + hf), wi_v, WI[c * 2 + hf], hf * 16, 16, c * 512, 512)
        wo_v = w_out.rearrange("(k p) n -> p k n", p=128)
        for c in range(16):
            add_chunk(("WO", c), wo_v, WO[c], 0, 32, c * 256, 256)
        wu_v = w_up.rearrange("(k p) n -> p k n", p=128)
        wd_v = w_down.rearrange("(k p) n -> p k n", p=128)
        for j in range(8):
            for fq in range(8):
                c = j * 8 + fq
                add_chunk(("WU", c), wu_v, WU[c], 0, 32, c * 256, 256)
            for db in range(8):
                add_chunk(("WD", j * 8 + db), wd_v, WD[j * 8 + db], j * 16, 16, db * 512, 512)
        cstate = {"i": 0}

        def issue_cast(deps):
            i = cstate["i"]
            if i >= len(cchunks):
                return False
            key, src, dst = cchunks[i]
            ch = cch[i % NCC]
            ev = k.dma(ch, dst, src, [Ev(ch.sem, ch.cnt)] + deps, q=POOL)
            ready[key] = [ev]
            cstate["i"] = i + 1
            return True

        def pump(n_):
            for _ in range(n_):
                deps = [PE.last] if PE.last is not None else []
                if not issue_cast(deps):
                    return

        def ensure(key):
            while key not in ready:
                assert issue_cast([])
            return ready[key]

        k.pump = pump
        k.ensure = ensure
        if stop_after == "S":
            return nc
        ensure(("WI", 35))

        def rms_part1(xb, xbuf, xn, xnbuf, ssx, rsx, statbuf, src_ap, chan):
            deps = xbuf.wdeps()
            xbuf.start_write()
            ev = k.dma(chan, xb[:], src_ap, deps)
            xbuf.wrote(ev)
            ACT.wait(xbuf.rdeps() + xnbuf.wdeps() + statbuf.wdeps())
            xnbuf.start_write()
            statbuf.start_write()
            ev = ACT.sig(nc.scalar.activation(out=xn[:], in_=xb[:], func=AF.Square, accum_out=ssx[:, 0:1]))
            xbuf.read(ev)
            ACT.wait([ev])
            ev = ACT.sig(nc.scalar.activation(out=rsx[:, 0:1], in_=ssx[:, 0:1], func=AF.Sqrt, scale=1.0 / D, bias=epsb[:, 0:1]))
            DVE.wait([ev])
            ev = DVE.sig(nc.vector.reciprocal(out=rsx[:, 0:1], in_=rsx[:, 0:1]))
            DVE.wait([ev])
            ev = DVE.sig(nc.vector.tensor_scalar(out=xn[:], in0=xb[:], scalar1=rsx[:, 0:1], scalar2=None, op0=ALU.mult))
            xbuf.read(ev)
            xnbuf.wrote(ev)
            statbuf.wrote(ev)

        def rms_part2(xn, xnbuf, tpx, tpbufs, hT, hbuf_deps, hbuf, tb, gcol0, tpi, groups=(0, 1, 2, 3)):
            for g in groups:
                t = tpx[(tpi + g) % len(tpx)]
                tbuf = tpbufs[(tpi + g) % len(tpx)]
                PE.wait(xnbuf.rdeps() + tbuf.wdeps())
                tbuf.start_write()
                for i in range(8):
                    kc = g * 8 + i
                    ins = nc.tensor.transpose(out=t[:, i * 128:(i + 1) * 128], in_=xn[:, kc * 128:(kc + 1) * 128], identity=ident[:])
                ev = PE.sig(ins)
                xnbuf.read(ev)
                tbuf.wrote(ev)
                for i in range(8):
                    kc = g * 8 + i
                    E = ACT if ((tpi + g) % 2 == 0) else DVE
                    E.wait(tbuf.rdeps() + hbuf_deps)
                    if E is ACT:
                        ins = nc.scalar.activation(out=hT[:, kc, tb * 128:(tb + 1) * 128], in_=t[:, i * 128:(i + 1) * 128],
                                                   func=AF.Copy, scale=pv[:, gcol0 + kc:gcol0 + kc + 1])
                    else:
                        ins = nc.vector.tensor_scalar(out=hT[:, kc, tb * 128:(tb + 1) * 128], in0=t[:, i * 128:(i + 1) * 128],
                                                      scalar1=pv[:, gcol0 + kc:gcol0 + kc + 1], scalar2=None, op0=ALU.mult)
                    ev = E.sig(ins)
                    tbuf.read(ev)
                    hbuf.wrote(ev)

        def phase_A():
            with ExitStack() as s1:
                ws = make_ws(4, s1, "a")
                NXB = 2
                xb = [sb(f"xb{i}", [128, D], F32, s1) for i in range(NXB)]
                xbb = [Buf() for _ in range(NXB)]
                xch = [Chan(k, f"xch{i}") for i in range(NXB)]
                xn = [sb(f"xn{i}", [128, D], BF16, s1) for i in range(NXB)]
                xnb = [Buf() for _ in range(NXB)]
                stt = [sb(f"stt{i}", [128, 2], F32, s1) for i in range(NXB)]
                sttb = [Buf() for _ in range(NXB)]
                hT = [sb(f"hT{i}", [128, 32, 512], BF16, s1) for i in range(2)]
                hTb = [Buf() for _ in range(2)]
                sq = [sb(f"sq{i}", [128, 512], F32, s1) for i in range(2)]
                sqb = [Buf() for _ in range(2)]
                ss = [sb(f"ss{i}", [128, 8], F32, s1) for i in range(2)]
                NQN = 8
                qn = [sb(f"qn{i}", [128, 512], BF16, s1) for i in range(NQN)]
                qnb = [Buf() for _ in range(NQN)]
                stg = [sb(f"stg{i}", [128, 512], BF16, s1) for i in range(4)]
                stgb = [Buf() for _ in range(4)]
                stch = [Chan(k, f"stch{i}") for i in range(4)]
                NPJ = 5
                pj = [ps(f"pj{i}", [128, 512], F32, s1) for i in range(NPJ)]
                pjb = [Buf() for _ in range(NPJ)]
                tpx = [ps(f"tpx{i}", [128, 1024], BF16, s1) for i in range(2)]
                tpxb = [Buf() for _ in range(2)]
                tpq = [ps(f"tpq{i}", [128, 1024], BF16, s1) for i in range(1)]
                tpqb = [Buf() for _ in range(1)]

                full_chunks = list(range(18))
                kv_chunks = [4, 5, 6, 7, 8, 9, 10, 11, 16, 17]
                tiles = [(t, full_chunks) for t in range(6)] + [(t, kv_chunks) for t in (6, 7)]
                aps = []
                for t, chs in tiles:
                    for c in chs:
                        for hf in range(2):
                            aps.append((("WI", c * 2 + hf), WI[c * 2 + hf]))
                ws.reset(aps)
                ws.topup()
                cnt = {"x": 0, "qn": 0, "stg": 0, "tpq": 0, "sq": 0, "pj": 0, "step": 0}

                def norm1(t, tb):
                    i = cnt["x"] % NXB
                    blk = t * 4 + tb
                    rms_part1(xb[i], xbb[i], xn[i], xnb[i], stt[i][:, 0:1], stt[i][:, 1:2], sttb[i],
                              xkv[blk * 128:(blk + 1) * 128, :], xch[i])
                    cnt["x"] += 1
                    return i

                def norm2(t, tb, i, hdeps, groups=(0, 1, 2, 3)):
                    rms_part2(xn[i], xnb[i], tpx, tpxb, hT[t % 2], hdeps, hTb[t % 2], tb, 0, 0, groups)

                def dest_for(c, blk):
                    tsl = slice(blk * 128, (blk + 1) * 128)
                    if c < 4:
                        return ("T", QTna[4 * c:4 * c + 4, :, tsl], 0)
                    if c < 8:
                        return ("T", KTna[4 * (c - 4):4 * (c - 4) + 4, :, tsl], 1)
                    if c < 12:
                        return ("V", Vna[tsl, (c - 8) * 512:(c - 7) * 512], None)
                    if c < 16:
                        return ("T", QTsw[4 * (c - 12):4 * (c - 12) + 4, :, tsl], 2)
                    if c == 16:
                        return ("T", KTsw[0:4, :, tsl], 3)
                    return ("V", Vsw[tsl, 0:512], None)

                def evac1(t, tb_, c, bk):
                    blk = t * 4 + tb_
                    tb = bk
                    kind, dst, gi = dest_for(c, blk)
                    if kind == "V":
                        si = cnt["stg"] % 4
                        cnt["stg"] += 1
                        E = ACT if (tb_ % 2 == 0) else DVE
                        E.wait(pjb[tb].rdeps() + stgb[si].wdeps())
                        stgb[si].start_write()
                        if E is ACT:
                            ins = nc.scalar.copy(out=stg[si][:], in_=pj[tb][:])
                        else:
                            ins = nc.vector.tensor_copy(out=stg[si][:], in_=pj[tb][:])
                        ev = E.sig(ins)
                        pjb[tb].read(ev)
                        stgb[si].wrote(ev)
                        ev2 = k.dma(stch[si], dst, stg[si][:], stgb[si].rdeps())
                        stgb[si].read(ev2)
                        return None
                    qi = cnt["sq"] % 2
                    cnt["sq"] += 1
                    ni = cnt["qn"] % NQN
                    cnt["qn"] += 1
                    ACT.wait(pjb[tb].rdeps() + sqb[qi].wdeps())
                    sqb[qi].start_write()
                    ev = ACT.sig(nc.scalar.activation(out=sq[qi][:], in_=pj[tb][:], func=AF.Square))
                    pjb[tb].read(ev)
                    DVE.wait([ev])
                    ev = DVE.sig(nc.vector.tensor_reduce(out=ss[qi][:, 0:4], in_=sq[qi][:].rearrange("p (h d) -> p h d", h=4),
                                                         axis=AX.X, op=ALU.add))
                    ACT.wait([ev])
                    ev = ACT.sig(nc.scalar.activation(out=ss[qi][:, 4:8], in_=ss[qi][:, 0:4], func=AF.Sqrt, scale=1.0 / 128,
                                                      bias=epsb[:, 0:1]))
                    DVE.wait([ev])
                    ev = DVE.sig(nc.vector.reciprocal(out=ss[qi][:, 4:8], in_=ss[qi][:, 4:8]))
                    DVE.wait([ev] + qnb[ni].wdeps())
                    qnb[ni].start_write()
                    ev = DVE.sig(nc.vector.tensor_tensor(out=qn[ni][:].rearrange("p (h d) -> p h d", h=4),
                                                         in0=pj[tb][:].rearrange("p (h d) -> p h d", h=4),
                                                         in1=ss[qi][:, 4:8].unsqueeze(2).to_broadcast([128, 4, 128]), op=ALU.mult))
                    pjb[tb].read(ev)
                    qnb[ni].wrote(ev)
                    sqb[qi].wrote(ev)
                    return (ni, dst, gi)

                def evac2(state):
                    if state is None:
                        return
                    ni, dst, gi = state
                    ti = 0
                    cnt["tpq"] += 1
                    PE.wait(qnb[ni].rdeps() + tpqb[ti].wdeps())
                    tpqb[ti].start_write()
                    for hh in range(4):
                        ins = nc.tensor.transpose(out=tpq[ti][:, hh * 128:(hh + 1) * 128], in_=qn[ni][:, hh * 128:(hh + 1) * 128],
                                                  identity=ident[:])
                    ev = PE.sig(ins)
                    qnb[ni].read(ev)
                    tpqb[ti].wrote(ev)
                    si = cnt["stg"] % 4
                    cnt["stg"] += 1
                    ACT.wait(tpqb[ti].rdeps() + stgb[si].wdeps())
                    stgb[si].start_write()
                    ev = ACT.sig(nc.scalar.activation(out=stg[si][:], in_=tpq[ti][:, 0:512], func=AF.Copy, scale=gsc[:, gi:gi + 1]))
                    tpqb[ti].read(ev)
                    stgb[si].wrote(ev)
                    ev2 = k.dma(stch[si], dst.rearrange("h d t -> d h t"), stg[si][:].rearrange("d (h t) -> d h t", h=4),
                                stgb[si].rdeps())
                    stgb[si].read(ev2)

                for tb in range(4):
                    i = norm1(0, tb)
                    hd = hTb[0].wdeps() if tb == 0 else []
                    if tb == 0:
                        hTb[0].start_write()
                    norm2(0, tb, i, hd)
                pending = []
                for ti_, (t, chs) in enumerate(tiles):
                    hcur = hT[t % 2]
                    hb = hTb[t % 2]
                    nxt = tiles[ti_ + 1][0] if ti_ + 1 < len(tiles) else None
                    nstate = {}
                    for ci, c in enumerate(chs):
                        if nxt is not None and ci < 8 and ci % 2 == 0:
                            nstate[ci // 2] = norm1(nxt, ci // 2)
                        bks = []
                        for tb in range(4):
                            bks.append(cnt["pj"] % NPJ)
                            cnt["pj"] += 1
                        n2 = None
                        if nxt is not None and ci < 8 and ci % 2 == 1:
                            tb2 = ci // 2
                            hd = hTb[nxt % 2].wdeps() if tb2 == 0 else []
                            if tb2 == 0:
                                hTb[nxt % 2].start_write()
                            n2 = (tb2, hd)
                        for hf in range(2):
                            wsl, wb = ws.get()
                            wv = wsl[:].rearrange("p (a b) -> p a b", a=16)
                            for tb in range(4):
                                bk = bks[tb]
                                deps = wb.rdeps() + hb.rdeps()
                                if hf == 0:
                                    deps = deps + pjb[bk].wdeps()
                                PE.wait(deps)
                                if hf == 0:
                                    pjb[bk].start_write()
                                for kc in range(16):
                                    ins = nc.tensor.matmul(pj[bk][:], hcur[:, hf * 16 + kc, tb * 128:(tb + 1) * 128], wv[:, kc, :],
                                                           start=(hf == 0 and kc == 0), stop=(hf == 1 and kc == 15))
                                if hf == 1 or tb == 3:
                                    ev = PE.sig(ins)
                                    wb.read(ev)
                                    hb.read(ev)
                                    if hf == 1:
                                        pjb[bk].wrote(ev)
                                if hf == 1 and pending:
                                    evac2(pending[tb])
                                if hf == 1 and n2 is not None:
                                    norm2(nxt, n2[0], nstate[n2[0]], n2[1], groups=(tb,))
                            cnt["step"] += 1
                            if cnt["step"] % 3 == 0:
                                k.pump(1)
                        pending = [evac1(t, tb, c, bks[tb]) for tb in range(4)]

                for stt_ in pending:
                    evac2(stt_)
                k.barrier()

        phase_A()
        if stop_after == "A":
            return nc

        def kvblock(j, dp):
            if j < 16:
                jj = j + dp
                return jj if 0 <= jj < 16 else None
            jq = j - 16 + dp
            if jq < 0:
                return 24 + 3 + jq
            if jq >= 8:
                return 27 + jq - 8
            return 16 + jq

        NKB = 11
        B_chans = {}
        ot_events = {}

        def alloc_B(stk, tag, cset=0):
            BR = {}
            if cset not in B_chans:
                B_chans[cset] = ([Chan(k, f"hch{cset}_{i}") for i in range(2)], [Chan(k, f"och{cset}_{i}") for i in range(2)])
            BR["hch"], BR["och"] = B_chans[cset]
            BR["nb"] = 0
            BR["KT"] = [sb(f"KT{tag}{i}", [128, NKB * 128], BF16, stk) for i in range(2)]
            BR["VV"] = [sb(f"VV{tag}{i}", [128, NKB, 128], BF16, stk) for i in range(2)]
            BR["QT"] = [sb(f"QT{tag}{i}", [128, 512], BF16, stk) for i in range(2)]
            BR["BT"] = [sb(f"BT{tag}{i}", [128, 7 * 128], F32, stk) for i in range(2)]
            BR["BM"] = [sb(f"BM{tag}{i}", [16, 512], F32, stk) for i in range(2)]
            BR["OS"] = [sb(f"OS{tag}{i}", [128, 512], BF16, stk) for i in range(2)]
            BR["tmp"] = sb(f"tmpB{tag}", [128, 8, 128], F32, stk)
            BR["PT"] = [sb(f"PTB{tag}{i}", [128, 8, 128], BF16, stk) for i in range(2)]
            BR["rden"] = sb(f"rdenB{tag}", [128, 128], F32, stk)
            BR["S"] = [ps(f"SpsB{tag}{i}", [128, 512], F32, stk) for i in range(2)]
            BR["O"] = ps(f"OpsB{tag}", [128, 512], F32, stk)
            BR["hbuf"] = [Buf() for _ in range(2)]
            BR["osb"] = [Buf() for _ in range(2)]
            BR["tmpb"] = Buf()
            BR["PTb"] = [Buf() for _ in range(2)]
            BR["rdb"] = Buf()
            BR["Sb"] = Buf()
            BR["Ob"] = Buf()
            return BR

        def kvblock(j, dp):
            if j < 16:
                jj = j + dp
                return jj if 0 <= jj < 16 else None
            jq = j - 16 + dp
            if jq < 0:
                return 24 + 3 + jq
            if jq >= 8:
                return 27 + jq - 8
            return 16 + jq

        def runs_of(blks):
            out = []
            i = 0
            while i < len(blks):
                j2 = i
                while j2 + 1 < len(blks) and blks[j2 + 1] == blks[j2] + 1:
                    j2 += 1
                out.append((i, blks[i], j2 - i + 1))
                i = j2 + 1
            return out

        def gen_B(T, BR, heads):
            B_hch, B_och = BR["hch"], BR["och"]
            Sps = BR["S"]
            Ops = BR["O"]
            B_KT, B_VV, B_QT, B_BT, B_BM, B_OS = BR["KT"], BR["VV"], BR["QT"], BR["BT"], BR["BM"], BR["OS"]
            B_tmp, B_PT, B_rden = BR["tmp"], BR["PT"], BR["rden"]
            B_hbuf, B_osb, B_tmpb, B_PTb, B_rdb, B_Sb, B_Ob = (BR["hbuf"], BR["osb"], BR["tmpb"], BR["PTb"], BR["rdb"], BR["Sb"],
                                                               BR["Ob"])
            js = list(range(4 * T, 4 * T + 4))
            kb = []
            for j in js:
                for dp in range(-3, 4):
                    blk = kvblock(j, dp)
                    if blk is not None and blk not in kb:
                        kb.append(blk)
            kb = sorted(kb) + [META_BLK]
            assert len(kb) <= NKB
            pos = {blk: i for i, blk in enumerate(kb)}
            runs = runs_of(kb)
            tsl = slice(T * 512, (T + 1) * 512)
            ot_events.setdefault(T, [])

            def load_head(hi):
                typ, h = heads[hi]
                s_ = hi % 2
                deps = B_hbuf[s_].wdeps()
                B_hbuf[s_].start_write()
                ch = B_hch[s_]
                if typ == "na":
                    ksrc, vsrc, vcol, qsrc, bsrc, nb_ = KTna[h], Vna, h, QTna[h], bna[h], 7
                else:
                    g = h // 4
                    ksrc, vsrc, vcol, qsrc, bsrc, nb_ = KTsw[g], Vsw, g, QTsw[h], bsw[h], 3
                ev = None
                for (i0, b0, n_) in runs:
                    ev = k.dma(ch, B_KT[s_][:, i0 * 128:(i0 + n_) * 128], ksrc[:, b0 * 128:(b0 + n_) * 128], deps)
                    deps = []
                    ev = k.dma(ch, B_VV[s_][:, i0:i0 + n_, :],
                               vsrc[b0 * 128:(b0 + n_) * 128, vcol * 128:(vcol + 1) * 128].rearrange("(b p) d -> p b d", p=128), [])
                ev = k.dma(ch, B_QT[s_][:], qsrc[:, tsl], [])
                ev = k.dma(ch, B_BT[s_][:, 0:nb_ * 128], bsrc[:, :], [])
                if typ == "sw":
                    ev = k.dma(ch, B_BM[s_][:], bmeta[h][:, tsl], [])
                B_hbuf[s_].wrote(ev)

            def Sview(ci):
                return Sps[ci // 4][:, (ci % 4) * 128:(ci % 4 + 1) * 128]

            prev = None

            def finish(pb):
                s_, hglob, jl, cl, p_, hb, is_last_j = pb
                PE.wait(B_PTb[p_].rdeps() + hb.rdeps() + B_Ob.wdeps())
                B_Ob.start_write()
                n_ = len(cl)
                for ci, (kind, blk, dpi) in enumerate(cl):
                    first = ci == 0
                    last = ci == n_ - 1
                    if kind == "k":
                        nc.tensor.matmul(Ops[:, 0:128], B_VV[s_][:, pos[blk], :], B_PT[p_][:, ci, :], start=first, stop=last)
                        ins = nc.tensor.matmul(Ops[:, 128:256], ones[:], B_PT[p_][:, ci, :], start=False, stop=last,
                                               skip_group_check=True)
                    else:
                        nc.tensor.matmul(Ops[:, 0:128], B_VV[s_][0:16, pos[blk], :], B_PT[p_][0:16, ci, :], start=first, stop=last)
                        ins = nc.tensor.matmul(Ops[:, 128:256], ones[0:16, :], B_PT[p_][0:16, ci, :], start=False, stop=last,
                                               skip_group_check=True)
                ev = PE.sig(ins)
                B_PTb[p_].read(ev)
                hb.read(ev)
                B_Ob.wrote(ev)
                DVE.wait(B_Ob.rdeps() + B_rdb.wdeps())
                B_rdb.start_write()
                ev = DVE.sig(nc.vector.tensor_scalar(out=B_rden[:], in0=Ops[:, 128:256], scalar1=esink[:, hglob:hglob + 1],
                                                     scalar2=None, op0=ALU.add))
                B_Ob.read(ev)
                DVE.wait([ev])
                ev = DVE.sig(nc.vector.reciprocal(out=B_rden[:], in_=B_rden[:]))
                B_rdb.wrote(ev)
                DVE.wait([ev] + B_osb[s_].wdeps())
                ev = DVE.sig(nc.vector.tensor_tensor(out=B_OS[s_][:, jl * 128:(jl + 1) * 128], in0=Ops[:, 0:128], in1=B_rden[:],
                                                     op=ALU.mult))
                B_Ob.read(ev)
                B_rdb.read(ev)
                B_osb[s_].wrote(ev)
                if is_last_j:
                    ev = k.dma(B_och[s_], OT[hglob][:, tsl], B_OS[s_][:], B_osb[s_].rdeps())
                    B_osb[s_].read(ev)
                    ot_events[T].append(ev)

            load_head(0)
            for hi, (typ, h) in enumerate(heads):
                s_ = hi % 2
                hb = B_hbuf[s_]
                hglob = h if typ == "na" else 16 + h
                dps = list(range(-3, 4)) if typ == "na" else [-1, 0, 1]
                for jl, j in enumerate(js):
                    cl = []
                    full = {}
                    for dpi, dp in enumerate(dps):
                        blk = kvblock(j, dp)
                        if blk is None:
                            continue
                        if typ == "na" and j < 16:
                            v = []
                            for b_ in range(2):
                                qr = 2 * j + b_
                                rs_ = min(max(qr - 4, 0), 32 - 8)
                                for a_ in range(2):
                                    kr = 2 * (j + dp) + a_
                                    v.append(0 <= kr < 32 and rs_ <= kr < rs_ + 8)
                            if not any(v):
                                continue
                            full[dpi] = all(v)
                        cl.append(("k", blk, dpi))
                    cl.append(("m", META_BLK, None))
                    if prev is not None:
                        finish(prev)
                    if jl == 0 and hi + 1 < len(heads):
                        load_head(hi + 1)
                    if jl == 0:
                        pass
                    p_ = BR["nb"] % 2
                    BR["nb"] += 1
                    PE.wait(hb.rdeps() + B_Sb.wdeps())
                    B_Sb.start_write()
                    qsl = B_QT[s_][:, jl * 128:(jl + 1) * 128]
                    for ci, (kind, blk, dpi) in enumerate(cl):
                        if kind == "k":
                            ins = nc.tensor.matmul(Sview(ci), B_KT[s_][:, pos[blk] * 128:(pos[blk] + 1) * 128], qsl, start=True, stop=True)
                        else:
                            ins = nc.tensor.matmul(Sview(ci)[0:16, :], B_KT[s_][:, pos[blk] * 128:pos[blk] * 128 + 16], qsl,
                                                   start=True, stop=True)
                    ev = PE.sig(ins)
                    hb.read(ev)
                    B_Sb.wrote(ev)
                    DVE.wait(B_Sb.rdeps() + hb.rdeps() + B_tmpb.wdeps())
                    B_tmpb.start_write()
                    for ci, (kind, blk, dpi) in enumerate(cl):
                        if kind == "k":
                            ins = nc.vector.tensor_tensor(out=B_tmp[:, ci, :], in0=Sview(ci), in1=B_BT[s_][:, dpi * 128:(dpi + 1) * 128],
                                                          op=ALU.add)
                        elif typ == "na":
                            ins = nc.vector.tensor_copy(out=B_tmp[0:16, ci, :], in_=Sview(ci)[0:16, :])
                        else:
                            ins = nc.vector.tensor_tensor(out=B_tmp[0:16, ci, :], in0=Sview(ci)[0:16, :],
                                                          in1=B_BM[s_][:, jl * 128:(jl + 1) * 128], op=ALU.add)
                    ev = DVE.sig(ins)
                    B_Sb.read(ev)
                    hb.read(ev)
                    B_tmpb.wrote(ev)
                    ACT.wait(B_tmpb.rdeps() + B_PTb[p_].wdeps())
                    B_PTb[p_].start_write()
                    for ci, (kind, blk, dpi) in enumerate(cl):
                        if kind == "k":
                            if typ == "na" and full.get(dpi, False):
                                ins = nc.scalar.activation(out=B_PT[p_][:, ci, :], in_=B_tmp[:, ci, :], func=AF.Exp)
                            elif typ == "na":
                                e0 = (j * 7 + dpi) * 2
                                nc.scalar.activation(out=B_PT[p_][:, ci, 0:64], in_=B_tmp[:, ci, 0:64], func=AF.Exp,
                                                     bias=mna[:, e0:e0 + 1])
                                ins = nc.scalar.activation(out=B_PT[p_][:, ci, 64:128], in_=B_tmp[:, ci, 64:128], func=AF.Exp,
                                                           bias=mna[:, e0 + 1:e0 + 2])
                            else:
                                e0 = j * 3 + dpi
                                ins = nc.scalar.activation(out=B_PT[p_][:, ci, :], in_=B_tmp[:, ci, :], func=AF.Exp,
                                                           bias=msw[:, e0:e0 + 1])
                        else:
                            ins = nc.scalar.activation(out=B_PT[p_][0:16, ci, :], in_=B_tmp[0:16, ci, :], func=AF.Exp)
                    ev = ACT.sig(ins)
                    B_tmpb.read(ev)
                    B_PTb[p_].wrote(ev)
                    prev = (s_, hglob, jl, cl, p_, hb, jl == 3)
                    yield
            finish(prev)
            yield

        bgen = {"g": None, "T": None, "hook": 0, "gs": []}
        ALL_HEADS = [("na", h) for h in range(16)] + [("sw", h) for h in range(16)]

        def pumpB(n_=1):
            for _ in range(n_):
                if not bgen["gs"]:
                    bgen["g"] = None
                    return
                g_ = bgen["gs"].pop(0)
                try:
                    next(g_)
                    bgen["gs"].append(g_)
                except StopIteration:
                    pass
                if not bgen["gs"]:
                    bgen["g"] = None

        def hookB():
            pumpB(1)
            bgen["hook"] += 1
            if bgen["hook"] % 3 == 0:
                k.pump(1)

        def startB(T, rsets):
            n_ = len(rsets)
            bgen["gs"] = [gen_B(T, R_, ALL_HEADS[i::n_]) for i, R_ in enumerate(rsets)]
            bgen["g"] = True
            bgen["T"] = T

        def drainB():
            while bgen["g"] is not None:
                pumpB(1)

        with ExitStack() as sB0:
            startB(0, [alloc_B(sB0, "s", 0), alloc_B(sB0, "t", 1)])
            nstep = 0
            while bgen["g"] is not None:
                pumpB(1)
                nstep += 1
                if nstep % 8 == 0:
                    k.pump(1)
            k.barrier()
        if stop_after == "B":
            return nc

        def phase_CD():
            with ExitStack() as s1:
                BRc = alloc_B(s1, "c", 0)
                ws = make_ws(3, s1, "c")
                x1 = sb("x1", [128, 4, D], F32, s1)
                x1b = [Buf() for _ in range(4)]
                xch = [Chan(k, f"x1ch{i}") for i in range(4)]
                ych = Chan(k, "ych")
                hT = sb("h2T", [128, 32, 512], BF16, s1)
                hTb = Buf()
                och = Chan(k, "otch")
                uT = sb("uT", [128, 16, 512], BF16, s1)
                uTb = Buf()
                xn = uT[:].rearrange("p a b -> p (a b)")[:, 0:D]
                xnb = uTb
                stt = sb("stt2", [128, 8], F32, s1)
                sttb = Buf()
                ssp = sb("ssp", [128, 64], F32, s1)
                sspb = Buf()
                sqj = sb("sqj", [128, 256], BF16, s1)
                NXS = 4
                xs = [sb(f"xs{i}", [128, 256], F32, s1) for i in range(NXS)]
                xsb = [Buf() for _ in range(NXS)]
                xsch = [Chan(k, f"xsch{i}") for i in range(NXS)]
                rl = [sb(f"rl{i}", [128, 512], F32, s1) for i in range(2)]
                rlb = [Buf() for _ in range(2)]
                pc = [ps(f"pc{i}", [128, 512], F32, s1) for i in range(2)]
                pcb = [Buf() for _ in range(2)]
                tpx = [ps("tpy0", [128, 1024], BF16, s1)]
                tpxb = [Buf()]
                pu = [ps(f"pu{i}", [128, 512], F32, s1) for i in range(2)]
                pub = [Buf() for _ in range(2)]
                tpx.append(pu[0][:].bitcast(BF16))
                tpxb.append(pub[0])
                cnt = {"pc": 0, "pu": 0, "rl": 0}

                aps = []
                for t in range(6):
                    for c in range(16):
                        aps.append((("WO", c), WO[c]))
                    for j in range(8):
                        for fq in range(8):
                            aps.append((("WU", j * 8 + fq), WU[j * 8 + fq]))
                        for db in range(8):
                            aps.append((("WD", j * 8 + db), WD[j * 8 + db]))
                ws.reset(aps)
                ws.topup()

                def load_oT(t_):
                    deps = hTb.wdeps() + ot_events[t_]
                    hTb.start_write()
                    ev_ = k.dma(och, hT[:], OT[:, :, t_ * 512:(t_ + 1) * 512].rearrange("h d t -> d h t"), deps)
                    hTb.wrote(ev_)

                drainB()
                load_oT(0)
                for t in range(6):
                    tok0 = t * 512
                    if t + 1 < 6:
                        startB(t + 1, [BRc])
                    xpieces = [(c_, tb_) for c_ in range(16) for tb_ in range(4)]
                    xstate = {"i": 0}

                    def issue_x():
                        i_ = xstate["i"]
                        if i_ >= len(xpieces):
                            return
                        c_, tb_ = xpieces[i_]
                        sl_ = i_ % NXS
                        deps_ = xsb[sl_].wdeps()
                        xsb[sl_].start_write()
                        ev_ = k.dma(xsch[sl_], xs[sl_][:], xkv[tok0 + tb_ * 128:tok0 + (tb_ + 1) * 128, c_ * 256:(c_ + 1) * 256], deps_)
                        xsb[sl_].wrote(ev_)
                        xstate["i"] = i_ + 1

                    for _ in range(NXS - 1):
                        issue_x()
                    sdeps = sspb.wdeps()
                    sspb.start_write()
                    for c in range(16):
                        wsl, wb = ws.get()
                        wv = wsl[:].rearrange("p (a b) -> p a b", a=32)
                        for tb in range(4):
                            bi = cnt["pc"] % 2
                            cnt["pc"] += 1
                            PE.wait(wb.rdeps() + hTb.rdeps() + pcb[bi].wdeps())
                            pcb[bi].start_write()
                            for kc in range(32):
                                ins = nc.tensor.matmul(pc[bi][:, 0:256], hT[:, kc, tb * 128:(tb + 1) * 128], wv[:, kc, :],
                                                       start=(kc == 0), stop=(kc == 31))
                            ev = PE.sig(ins)
                            wb.read(ev)
                            hTb.read(ev)
                            pcb[bi].wrote(ev)
                            issue_x()
                            sl = (c * 4 + tb) % NXS
                            DVE.wait(pcb[bi].rdeps() + xsb[sl].rdeps() + x1b[tb].wdeps())
                            ev = DVE.sig(nc.vector.tensor_tensor(out=x1[:, tb, c * 256:(c + 1) * 256], in0=pc[bi][:, 0:256],
                                                                 in1=xs[sl][:], op=ALU.add))
                            pcb[bi].read(ev)
                            xsb[sl].read(ev)
                            x1b[tb].wrote(ev)
                            ACT.wait([ev] + sdeps)
                            ev = ACT.sig(nc.scalar.activation(out=sqj[:], in_=x1[:, tb, c * 256:(c + 1) * 256], func=AF.Square,
                                                              accum_out=ssp[:, tb * 16 + c:tb * 16 + c + 1]))
                            x1b[tb].read(ev)
                            sspb.wrote(ev)
                            if tb % 2 == 1:
                                hookB()
                    hdeps = hTb.wdeps()
                    hTb.start_write()
                    DVE.wait(sspb.rdeps() + sttb.wdeps())
                    sttb.start_write()
                    ev = DVE.sig(nc.vector.tensor_reduce(out=stt[:, 0:4], in_=ssp[:].rearrange("p (t c) -> p t c", t=4), axis=AX.X,
                                                         op=ALU.add))
                    sspb.read(ev)
                    ACT.wait([ev])
                    ev = ACT.sig(nc.scalar.activation(out=stt[:, 4:8], in_=stt[:, 0:4], func=AF.Sqrt, scale=1.0 / D,
                                                      bias=epsb[:, 0:1]))
                    DVE.wait([ev])
                    ev = DVE.sig(nc.vector.reciprocal(out=stt[:, 4:8], in_=stt[:, 4:8]))
                    sttb.wrote(ev)
                    for tb in range(4):
                        DVE.wait(x1b[tb].rdeps() + xnb.wdeps() + sttb.rdeps())
                        xnb.start_write()
                        ev = DVE.sig(nc.vector.tensor_scalar(out=xn, in0=x1[:, tb, :], scalar1=stt[:, 4 + tb:5 + tb], scalar2=None,
                                                             op0=ALU.mult))
                        x1b[tb].read(ev)
                        xnb.wrote(ev)
                        sttb.read(ev)
                        rms_part2(xn, xnb, tpx, tpxb, hT, hdeps, hTb, tb, 32, 2 * tb)
                    for j in range(8):
                        udeps = uTb.wdeps()
                        uTb.start_write()
                        for fq in range(8):
                            wsl, wb = ws.get()
                            wv = wsl[:].rearrange("p (a b) -> p a b", a=32)
                            for f2 in range(2):
                                fb_ = fq * 2 + f2
                                ui = cnt["pu"] % 2
                                cnt["pu"] += 1
                                PE.wait(wb.rdeps() + hTb.rdeps() + pub[ui].wdeps())
                                pub[ui].start_write()
                                for kc in range(32):
                                    ins = nc.tensor.matmul(pu[ui][:], wv[:, kc, f2 * 128:(f2 + 1) * 128], hT[:, kc, :],
                                                           start=(kc == 0), stop=(kc == 31))
                                ev = PE.sig(ins)
                                wb.read(ev)
                                hTb.read(ev)
                                pub[ui].wrote(ev)
                                ri = cnt["rl"] % 2
                                cnt["rl"] += 1
                                DVE.wait(pub[ui].rdeps() + rlb[ri].wdeps())
                                rlb[ri].start_write()
                                ev = DVE.sig(nc.vector.tensor_scalar(out=rl[ri][:], in0=pu[ui][:], scalar1=0.0, scalar2=None,
                                                                     op0=ALU.max))
                                pub[ui].read(ev)
                                rlb[ri].wrote(ev)
                                POOL.wait(rlb[ri].rdeps() + udeps)
                                ev = POOL.sig(nc.gpsimd.tensor_tensor(out=uT[:, fb_, :], in0=rl[ri][:], in1=rl[ri][:], op=ALU.mult))
                                rlb[ri].read(ev)
                                uTb.wrote(ev)
                                hookB()
                        if j == 7 and t + 1 < 6:
                            drainB()
                            load_oT(t + 1)
                        for db in range(8):
                            wsl, wb = ws.get()
                            wv = wsl[:].rearrange("p (a b) -> p a b", a=16)
                            for tb in range(4):
                                bi = cnt["pc"] % 2
                                cnt["pc"] += 1
                                PE.wait(wb.rdeps() + uTb.rdeps() + pcb[bi].wdeps())
                                pcb[bi].start_write()
                                for fc in range(16):
                                    ins = nc.tensor.matmul(pc[bi][:], uT[:, fc, tb * 128:(tb + 1) * 128], wv[:, fc, :],
                                                           start=(fc == 0), stop=(fc == 15))
                                ev = PE.sig(ins)
                                wb.read(ev)
                                uTb.read(ev)
                                pcb[bi].wrote(ev)
                                DVE.wait(pcb[bi].rdeps() + x1b[tb].wdeps())
                                ev = DVE.sig(nc.vector.tensor_tensor(out=x1[:, tb, db * 512:(db + 1) * 512], in0=pc[bi][:],
                                                                     in1=x1[:, tb, db * 512:(db + 1) * 512], op=ALU.add))
                                pcb[bi].read(ev)
                                x1b[tb].wrote(ev)
                                if tb % 2 == 1:
                                    hookB()
                    for tb in range(4):
                        ev = k.dma(ych, y[tok0 + tb * 128:tok0 + (tb + 1) * 128, :], x1[:, tb, :], x1b[tb].rdeps())
                        x1b[tb].read(ev)
                k.barrier()

        phase_CD()
    return nc


def _t5_bucket(rel):
    nb = 16
    max_exact = 8
    ret = np.where(rel > 0, nb, 0)
    n = np.abs(rel)
    large = max_exact + (np.log(np.maximum(n, 1) / max_exact) / np.log(128 / max_exact) * (nb - max_exact)).astype(np.int64)
    large = np.minimum(large, nb - 1)
    return ret + np.where(n < max_exact, n, large)


def _host_prep(x_prompt, x_sample, meta_tokens, t5_bias, norm_attn, q_norm_na, k_norm_na, na_rpb, q_norm_swa, k_norm_swa,
               swa_sink, norm_mlp):
    f32 = np.float32
    xp = np.asarray(x_prompt, f32)[0]
    xs = np.asarray(x_sample, f32)
    rpb = np.asarray(na_rpb, f32)[0]
    t5 = np.asarray(t5_bias, f32)
    a = np.arange(2)[:, None]
    kc = np.arange(64)[None, :]
    cs = np.clip(np.arange(64) - 8, 0, 48)
    bna = np.empty((16, 128, 7, 128), f32)
    for dpi, dp in enumerate(range(-3, 4)):
        A = np.repeat(np.arange(2), 64)
        KC = np.tile(np.arange(64), 2)
        dr = 2 * dp + A[:, None] - A[None, :]
        dc = KC[:, None] - KC[None, :]
        col_in = (KC[:, None] >= cs[KC][None, :]) & (KC[:, None] < cs[KC][None, :] + 16)
        ok = col_in & (np.abs(dr) <= 7)
        g = rpb[:, np.clip(dr, -7, 7) + 7, np.clip(dc, -15, 15) + 15]
        bna[:, :, dpi, :] = np.where(ok[None], g, f32(NEG))
    bsw = np.empty((16, 128, 3, 128), f32)
    P = np.arange(128)
    for dpi, dp in enumerate((-1, 0, 1)):
        rel = (dp * 128 + P[:, None]) - P[None, :]
        ok = np.abs(rel) <= 128
        g = t5[_t5_bucket(rel)]
        bsw[:, :, dpi, :] = np.where(ok[None], g.transpose(2, 0, 1), f32(NEG))
    pv_common = np.zeros((128, 84), f32)
    pv_common[:, 0:32] = np.asarray(norm_attn, f32)[0].reshape(32, 128).T
    pv_common[:, 32:64] = np.asarray(norm_mlp, f32)[0].reshape(32, 128).T
    pv_common[:, 64] = np.asarray(q_norm_na, f32)[0]
    pv_common[:, 65] = np.asarray(k_norm_na, f32)[0]
    pv_common[:, 66] = np.asarray(q_norm_swa, f32)[0]
    pv_common[:, 67] = np.asarray(k_norm_swa, f32)[0]
    pv_common[:, 68:84] = np.asarray(swa_sink, f32)[0][None, :]
    ident = np.eye(128, dtype=f32)
    meta = np.asarray(meta_tokens, f32)
    per_core = []
    for c in range(NCORES):
        xkv = np.zeros((NKVB * 128, D), f32)
        xkv[0:2048] = xs[c]
        xkv[2048:3072] = xp[1024 * c:1024 * (c + 1)]
        for i in range(3):
            pr = 8 * c - 3 + i
            if pr >= 0:
                xkv[(24 + i) * 128:(25 + i) * 128] = xp[pr * 128:(pr + 1) * 128]
            pr = 8 * c + 8 + i
            if pr < 64:
                xkv[(27 + i) * 128:(28 + i) * 128] = xp[pr * 128:(pr + 1) * 128]
        xkv[META_BLK * 128:META_BLK * 128 + 16] = meta
        mcna = np.zeros((128, NQB, 7, 2), f32)
        mcsw = np.zeros((128, NQB, 3), f32)
        tpos = np.empty(NQB * 128, np.int64)
        for j in range(NQB):
            if j < 16:
                rows, jg = 32, j
                tpos[j * 128:(j + 1) * 128] = j * 128 + np.arange(128)
            else:
                rows, jg = 128, 8 * c + (j - 16)
                tpos[j * 128:(j + 1) * 128] = jg * 128 + np.arange(128)
            nbk = rows // 2
            for dpi, dp in enumerate(range(-3, 4)):
                for b in range(2):
                    qr = 2 * jg + b
                    rs = min(max(qr - 4, 0), rows - 8)
                    for a_ in range(2):
                        kr = 2 * (jg + dp) + a_
                        ok = (0 <= kr < rows) and (rs <= kr < rs + 8)
                        mcna[a_ * 64:(a_ + 1) * 64, j, dpi, b] = 0.0 if ok else NEG
            for dpi, dp in enumerate((-1, 0, 1)):
                ok = 0 <= jg + dp < nbk
                mcsw[:, j, dpi] = 0.0 if ok else NEG
        relm = np.arange(16)[:, None] - (16 + tpos)[None, :]
        bmeta = np.ascontiguousarray(t5[_t5_bucket(relm)].transpose(2, 0, 1))
        per_core.append({
            "xkv": xkv,
            "pvec": pv_common,
            "ident": ident,
            "bna": bna.reshape(16, 128, 7 * 128),
            "bsw": bsw.reshape(16, 128, 3 * 128),
            "bmeta": bmeta,
            "mcna": mcna.reshape(128, NQB * 14),
            "mcsw": mcsw.reshape(128, NQB * 3),
        })
    return per_core


_NC_CACHE = {}


def kernel(x_prompt, x_sample, meta_tokens, t5_bias, norm_attn, w_in, q_norm_na, k_norm_na, na_rpb, q_norm_swa, k_norm_swa,
           swa_sink, w_out, norm_mlp, w_up, w_down):
    per_core = _host_prep(x_prompt, x_sample, meta_tokens, t5_bias, norm_attn, q_norm_na, k_norm_na, na_rpb, q_norm_swa,
                          k_norm_swa, swa_sink, norm_mlp)
    wi = np.ascontiguousarray(np.asarray(w_in, np.float32)[0])
    wo = np.ascontiguousarray(np.asarray(w_out, np.float32)[0])
    wu = np.ascontiguousarray(np.asarray(w_up, np.float32)[0])
    wd = np.ascontiguousarray(np.asarray(w_down, np.float32)[0])
    for d in per_core:
        d.update({"w_in": wi, "w_out": wo, "w_up": wu, "w_down": wd})
    if "nc" not in _NC_CACHE:
        _NC_CACHE["nc"] = build_nc()
    nc = _NC_CACHE["nc"]
    res = run_bass_kernel_spmd(nc, per_core, core_ids=list(range(NCORES)))
    y_prompt = np.empty((1, 8192, D), np.float32)
    y_sample = np.empty((8, 2048, D), np.float32)
    for c in range(NCORES):
        yc = np.asarray(res.results[c]["y"])
        y_sample[c] = yc[0:2048]
        y_prompt[0, 1024 * c:1024 * (c + 1)] = yc[2048:3072]
    return (y_prompt, y_sample)
```

```python
import numpy as np
from contextlib import ExitStack
import concourse.bass as bass
import concourse.mybir as mybir
from concourse.bass_utils import run_bass_kernel_spmd

F32 = mybir.dt.float32
BF16 = mybir.dt.bfloat16
AF = mybir.ActivationFunctionType
ALU = mybir.AluOpType
AX = mybir.AxisListType

NCORES = 8
D = 4096
NKVB = 32
NQB = 24
META_BLK = 30
NEG = -30000.0
EPS = 1e-6
SEM_LIMIT = 30000


class Ev:
    __slots__ = ("sem", "val")

    def __init__(self, sem, val):
        self.sem = sem
        self.val = val


class Buf:
    def __init__(self):
        self.w = {}
        self.r = {}

    def rdeps(self):
        return [Ev(k, v) for k, v in self.w.items()]

    def wdeps(self):
        return [Ev(k, v) for k, v in self.w.items()] + [Ev(k, v) for k, v in self.r.items()]

    def start_write(self):
        self.w = {}
        self.r = {}

    def wrote(self, ev):
        self.w[ev.sem] = max(self.w.get(ev.sem, 0), ev.val)

    def read(self, ev):
        self.r[ev.sem] = max(self.r.get(ev.sem, 0), ev.val)


class Eng:
    def __init__(self, k, eng, name):
        self.k = k
        self.eng = eng
        self.name = name
        self.sem = None
        self.cnt = 0
        self.nep = 0
        self.waited = {}
        self.last = None

    def wait(self, evs):
        for ev in evs:
            if ev is None:
                continue
            if self.waited.get(ev.sem, 0) >= ev.val:
                continue
            self.eng.wait_ge(ev.sem, ev.val)
            self.waited[ev.sem] = ev.val

    def sig(self, ins):
        if self.sem is None or self.cnt >= SEM_LIMIT:
            self.sem = self.k.newsem(f"{self.name}{self.nep}")
            self.nep += 1
            self.cnt = 0
        self.cnt += 1
        ins.then_inc(self.sem, 1)
        ev = Ev(self.sem, self.cnt)
        self.last = ev
        return ev


class Chan:
    def __init__(self, k, name):
        self.sem = k.newsem(name)
        self.cnt = 0
        k.chans.append(self)

    def ev(self):
        return Ev(self.sem, self.cnt)


class K:
    def __init__(self, nc, st):
        self.nc = nc
        self.st = st
        self.nsem = 0
        self.chans = []
        self.PE = Eng(self, nc.tensor, "pe")
        self.ACT = Eng(self, nc.scalar, "act")
        self.DVE = Eng(self, nc.vector, "dve")
        self.POOL = Eng(self, nc.gpsimd, "pool")
        self.SP = Eng(self, nc.sync, "sp")
        self.engs = [self.PE, self.ACT, self.DVE, self.POOL]

    def newsem(self, name):
        self.nsem += 1
        return self.st.enter_context(self.nc.semaphore(name))

    def dma(self, chan, out, in_, deps, q=None):
        q = q or self.SP
        q.wait(deps)
        ins = q.eng.dma_start(out=out, in_=in_)
        chan.cnt += 16
        assert chan.cnt < SEM_LIMIT, chan.cnt
        ins.then_inc(chan.sem, 16)
        return Ev(chan.sem, chan.cnt)

    def barrier(self, exclude=()):
        evs = [e.last for e in self.engs if e.last is not None]
        evs += [c.ev() for c in self.chans if c.cnt > 0 and c not in exclude]
        for e in self.engs + [self.SP]:
            e.wait(evs)


class WStream:
    def __init__(self, k, slots, bufs, chans):
        self.k = k
        self.slots = slots
        self.bufs = bufs
        self.chans = chans
        self.n = len(slots)
        self.pos = 0
        self.reset([])

    def reset(self, aps):
        self.aps = aps
        self.issued = 0
        self.consumed = 0
        self.base = self.pos

    def _issue(self):
        i = self.issued
        s = (self.base + i) % self.n
        b = self.bufs[s]
        deps = b.wdeps()
        b.start_write()
        key, ap = self.aps[i]
        deps = deps + self.k.ensure(key)
        ev = self.k.dma(self.chans[s], self.slots[s][:], ap, deps)
        b.wrote(ev)
        self.issued += 1

    def topup(self):
        while self.issued < min(self.consumed + self.n, len(self.aps)):
            self._issue()

    def get(self):
        self.topup()
        s = (self.base + self.consumed) % self.n
        self.consumed += 1
        self.pos = self.base + self.consumed
        return self.slots[s], self.bufs[s]


def build_nc(stop_after=None, debug=False):
    nc = bass.Bass("TRN2", target_bir_lowering=False)
    dk = "ExternalOutput" if debug else "Internal"

    def din(name, shape, dt=F32):
        return nc.dram_tensor(name, list(shape), dt, kind="ExternalInput").ap()

    xkv = din("xkv", [NKVB * 128, D])
    w_in = din("w_in", [D, 9216])
    w_out = din("w_out", [D, D])
    w_up = din("w_up", [D, 4 * D])
    w_down = din("w_down", [4 * D, D])
    pvec = din("pvec", [128, 84])
    ident_in = din("ident", [128, 128])
    bna = din("bna", [16, 128, 7 * 128])
    bsw = din("bsw", [16, 128, 3 * 128])
    bmeta = din("bmeta", [16, 16, NQB * 128])
    mcna = din("mcna", [128, NQB * 14])
    mcsw = din("mcsw", [128, NQB * 3])
    y = nc.dram_tensor("y", [NQB * 128, D], F32, kind="ExternalOutput").ap()

    WI = nc.dram_tensor("WI", [36, 128, 8192], BF16, kind=dk).ap()
    WO = nc.dram_tensor("WO", [16, 128, 8192], BF16).ap()
    WU = nc.dram_tensor("WU", [64, 128, 8192], BF16).ap()
    WD = nc.dram_tensor("WD", [64, 128, 8192], BF16).ap()
    QTna = nc.dram_tensor("QTna", [16, 128, NQB * 128], BF16, kind=dk).ap()
    QTsw = nc.dram_tensor("QTsw", [16, 128, NQB * 128], BF16, kind=dk).ap()
    KTna = nc.dram_tensor("KTna", [16, 128, NKVB * 128], BF16, kind=dk).ap()
    KTsw = nc.dram_tensor("KTsw", [4, 128, NKVB * 128], BF16, kind=dk).ap()
    Vna = nc.dram_tensor("Vna", [NKVB * 128, 2048], BF16, kind=dk).ap()
    Vsw = nc.dram_tensor("Vsw", [NKVB * 128, 512], BF16, kind=dk).ap()
    OT = nc.dram_tensor("OT", [32, 128, NQB * 128], BF16, kind=dk).ap()

    with ExitStack() as st:
        k = K(nc, st)
        PE, ACT, DVE, POOL, SP = k.PE, k.ACT, k.DVE, k.POOL, k.SP

        def sb(name, shape, dt, stack=st):
            return stack.enter_context(nc.sbuf_tensor(name, list(shape), dt))

        def ps(name, shape, dt, stack=st):
            return stack.enter_context(nc.psum_tensor(name, list(shape), dt))

        ident = sb("ident_sb", [128, 128], BF16)
        ones = sb("ones", [128, 128], BF16)
        pv = sb("pv", [128, 84], F32)
        gsc = sb("gsc", [128, 4], F32)
        esink = sb("esink", [128, 32], F32)
        epsb = sb("epsb", [128, 1], F32)
        mna = sb("mna", [128, NQB * 14], F32)
        msw = sb("msw", [128, NQB * 3], F32)
        wchans = [Chan(k, f"wch{i}") for i in range(4)]

        def make_ws(nw, stack, tag):
            slots = [sb(f"w{tag}{i}", [128, 8192], BF16, stack) for i in range(nw)]
            return WStream(k, slots, [Buf() for _ in range(nw)], wchans[:nw])
        c_const = Chan(k, "cconst")
        with ExitStack() as s0:
            id32 = sb("id32", [128, 128], F32, s0)
            e1 = k.dma(c_const, id32[:], ident_in[:, :], [])
            e2 = k.dma(c_const, pv[:], pvec[:, :], [])
            e3 = k.dma(c_const, mna[:], mcna[:, :], [])
            e4 = k.dma(c_const, msw[:], mcsw[:, :], [])
            DVE.wait([e4])
            DVE.sig(nc.vector.tensor_copy(out=ident[:], in_=id32[:]))
            DVE.sig(nc.vector.memset(ones[:], 1.0))
            DVE.sig(nc.vector.memset(epsb[:], EPS))
            DVE.sig(nc.vector.memset(esink[:], 0.0))
            sc = 128.0 ** -0.5
            DVE.sig(nc.vector.tensor_scalar(out=gsc[:, 0:1], in0=pv[:, 64:65], scalar1=sc, scalar2=None, op0=ALU.mult))
            DVE.sig(nc.vector.tensor_copy(out=gsc[:, 1:2], in_=pv[:, 65:66]))
            DVE.sig(nc.vector.tensor_scalar(out=gsc[:, 2:3], in0=pv[:, 66:67], scalar1=sc, scalar2=None, op0=ALU.mult))
            ev = DVE.sig(nc.vector.tensor_copy(out=gsc[:, 3:4], in_=pv[:, 67:68]))
            ACT.wait([e4, ev])
            ACT.sig(nc.scalar.activation(out=esink[:, 16:32], in_=pv[:, 68:84], func=AF.Exp))
            k.barrier()

        NCC = 8
        cch = [Chan(k, f"cch{i}") for i in range(NCC)]
        cchunks = []
        ready = {}

        def add_chunk(key, view, dst, k0, nk, c0, ncol):
            cchunks.append((key, view[:, k0:k0 + nk, c0:c0 + ncol], dst.rearrange("p (k n) -> p k n", k=nk)))

        wi_v = w_in.rearrange("(k p) n -> p k n", p=128)
        for c in range(18):
            for hf in range(2):
                add_chunk(("WI", c * 2 + hf), wi_v, WI[c * 2 + hf], hf * 16, 16, c * 512, 512)
        wo_v = w_out.rearrange("(k p) n -> p k n", p=128)
        for c in range(16):
            add_chunk(("WO", c), wo_v, WO[c], 0, 32, c * 256, 256)
        wu_v = w_up.rearrange("(k p) n -> p k n", p=128)
        wd_v = w_down.rearrange("(k p) n -> p k n", p=128)
        for j in range(8):
            for fq in range(8):
                c = j * 8 + fq
                add_chunk(("WU", c), wu_v, WU[c], 0, 32, c * 256, 256)
            for db in range(8):
                add_chunk(("WD", j * 8 + db), wd_v, WD[j * 8 + db], j * 16, 16, db * 512, 512)
        cstate = {"i": 0}

        def issue_cast(deps):
            i = cstate["i"]
            if i >= len(cchunks):
                return False
            key, src, dst = cchunks[i]
            ch = cch[i % NCC]
            ev = k.dma(ch, dst, src, [Ev(ch.sem, ch.cnt)] + deps, q=POOL)
            ready[key] = [ev]
            cstate["i"] = i + 1
            return True

        def pump(n_):
            for _ in range(n_):
                deps = [PE.last] if PE.last is not None else []
                if not issue_cast(deps):
                    return

        def ensure(key):
            while key not in ready:
                assert issue_cast([])
            return ready[key]

        k.pump = pump
        k.ensure = ensure
        if stop_after == "S":
            return nc
        ensure(("WI", 35))

        def rms_part1(xb, xbuf, xn, xnbuf, ssx, rsx, statbuf, src_ap, chan):
            deps = xbuf.wdeps()
            xbuf.start_write()
            ev = k.dma(chan, xb[:], src_ap, deps)
            xbuf.wrote(ev)
            ACT.wait(xbuf.rdeps() + xnbuf.wdeps() + statbuf.wdeps())
            xnbuf.start_write()
            statbuf.start_write()
            ev = ACT.sig(nc.scalar.activation(out=xn[:], in_=xb[:], func=AF.Square, accum_out=ssx[:, 0:1]))
            xbuf.read(ev)
            ACT.wait([ev])
            ev = ACT.sig(nc.scalar.activation(out=rsx[:, 0:1], in_=ssx[:, 0:1], func=AF.Sqrt, scale=1.0 / D, bias=epsb[:, 0:1]))
            DVE.wait([ev])
            ev = DVE.sig(nc.vector.reciprocal(out=rsx[:, 0:1], in_=rsx[:, 0:1]))
            DVE.wait([ev])
            ev = DVE.sig(nc.vector.tensor_scalar(out=xn[:], in0=xb[:], scalar1=rsx[:, 0:1], scalar2=None, op0=ALU.mult))
            xbuf.read(ev)
            xnbuf.wrote(ev)
            statbuf.wrote(ev)

        def rms_part2(xn, xnbuf, tpx, tpbufs, hT, hbuf_deps, hbuf, tb, gcol0, tpi, groups=(0, 1, 2, 3)):
            for g in groups:
                t = tpx[(tpi + g) % len(tpx)]
                tbuf = tpbufs[(tpi + g) % len(tpx)]
                PE.wait(xnbuf.rdeps() + tbuf.wdeps())
                tbuf.start_write()
                for i in range(8):
                    kc = g * 8 + i
                    ins = nc.tensor.transpose(out=t[:, i * 128:(i + 1) * 128], in_=xn[:, kc * 128:(kc + 1) * 128], identity=ident[:])
                ev = PE.sig(ins)
                xnbuf.read(ev)
                tbuf.wrote(ev)
                for i in range(8):
                    kc = g * 8 + i
                    E = ACT if ((tpi + g) % 2 == 0) else DVE
                    E.wait(tbuf.rdeps() + hbuf_deps)
                    if E is ACT:
                        ins = nc.scalar.activation(out=hT[:, kc, tb * 128:(tb + 1) * 128], in_=t[:, i * 128:(i + 1) * 128],
                                                   func=AF.Copy, scale=pv[:, gcol0 + kc:gcol0 + kc + 1])
                    else:
                        ins = nc.vector.tensor_scalar(out=hT[:, kc, tb * 128:(tb + 1) * 128], in0=t[:, i * 128:(i + 1) * 128],
                                                      scalar1=pv[:, gcol0 + kc:gcol0 + kc + 1], scalar2=None, op0=ALU.mult)
                    ev = E.sig(ins)
                    tbuf.read(ev)
                    hbuf.wrote(ev)

        def phase_A():
            with ExitStack() as s1:
                ws = make_ws(4, s1, "a")
                NXB = 2
                xb = [sb(f"xb{i}", [128, D], F32, s1) for i in range(NXB)]
                xbb = [Buf() for _ in range(NXB)]
                xch = [Chan(k, f"xch{i}") for i in range(NXB)]
                xn = [sb(f"xn{i}", [128, D], BF16, s1) for i in range(NXB)]
                xnb = [Buf() for _ in range(NXB)]
                stt = [sb(f"stt{i}", [128, 2], F32, s1) for i in range(NXB)]
                sttb = [Buf() for _ in range(NXB)]
                hT = [sb(f"hT{i}", [128, 32, 512], BF16, s1) for i in range(2)]
                hTb = [Buf() for _ in range(2)]
                sq = [sb(f"sq{i}", [128, 512], F32, s1) for i in range(2)]
                sqb = [Buf() for _ in range(2)]
                ss = [sb(f"ss{i}", [128, 8], F32, s1) for i in range(2)]
                NQN = 8
                qn = [sb(f"qn{i}", [128, 512], BF16, s1) for i in range(NQN)]
                qnb = [Buf() for _ in range(NQN)]
                stg = [sb(f"stg{i}", [128, 512], BF16, s1) for i in range(4)]
                stgb = [Buf() for _ in range(4)]
                stch = [Chan(k, f"stch{i}") for i in range(4)]
                NPJ = 5
                pj = [ps(f"pj{i}", [128, 512], F32, s1) for i in range(NPJ)]
                pjb = [Buf() for _ in range(NPJ)]
                tpx = [ps(f"tpx{i}", [128, 1024], BF16, s1) for i in range(2)]
                tpxb = [Buf() for _ in range(2)]
                tpq = [ps(f"tpq{i}", [128, 1024], BF16, s1) for i in range(1)]
                tpqb = [Buf() for _ in range(1)]

                full_chunks = list(range(18))
                kv_chunks = [4, 5, 6, 7, 8, 9, 10, 11, 16, 17]
                tiles = [(t, full_chunks) for t in range(6)] + [(t, kv_chunks) for t in (6, 7)]
                aps = []
                for t, chs in tiles:
                    for c in chs:
                        for hf in range(2):
                            aps.append((("WI", c * 2 + hf), WI[c * 2 + hf]))
                ws.reset(aps)
                ws.topup()
                cnt = {"x": 0, "qn": 0, "stg": 0, "tpq": 0, "sq": 0, "pj": 0, "step": 0}

                def norm1(t, tb):
                    i = cnt["x"] % NXB
                    blk = t * 4 + tb
                    rms_part1(xb[i], xbb[i], xn[i], xnb[i], stt[i][:, 0:1], stt[i][:, 1:2], sttb[i],
                              xkv[blk * 128:(blk + 1) * 128, :], xch[i])
                    cnt["x"] += 1
                    return i

                def norm2(t, tb, i, hdeps, groups=(0, 1, 2, 3)):
                    rms_part2(xn[i], xnb[i], tpx, tpxb, hT[t % 2], hdeps, hTb[t % 2], tb, 0, 0, groups)

                def dest_for(c, blk):
                    tsl = slice(blk * 128, (blk + 1) * 128)
                    if c < 4:
                        return ("T", QTna[4 * c:4 * c + 4, :, tsl], 0)
                    if c < 8:
                        return ("T", KTna[4 * (c - 4):4 * (c - 4) + 4, :, tsl], 1)
                    if c < 12:
                        return ("V", Vna[tsl, (c - 8) * 512:(c - 7) * 512], None)
                    if c < 16:
                        return ("T", QTsw[4 * (c - 12):4 * (c - 12) + 4, :, tsl], 2)
                    if c == 16:
                        return ("T", KTsw[0:4, :, tsl], 3)
                    return ("V", Vsw[tsl, 0:512], None)

                def evac1(t, tb_, c, bk):
                    blk = t * 4 + tb_
                    tb = bk
                    kind, dst, gi = dest_for(c, blk)
                    if kind == "V":
                        si = cnt["stg"] % 4
                        cnt["stg"] += 1
                        E = ACT if (tb_ % 2 == 0) else DVE
                        E.wait(pjb[tb].rdeps() + stgb[si].wdeps())
                        stgb[si].start_write()
                        if E is ACT:
                            ins = nc.scalar.copy(out=stg[si][:], in_=pj[tb][:])
                        else:
                            ins = nc.vector.tensor_copy(out=stg[si][:], in_=pj[tb][:])
                        ev = E.sig(ins)
                        pjb[tb].read(ev)
                        stgb[si].wrote(ev)
                        ev2 = k.dma(stch[si], dst, stg[si][:], stgb[si].rdeps())
                        stgb[si].read(ev2)
                        return None
                    qi = cnt["sq"] % 2
                    cnt["sq"] += 1
                    ni = cnt["qn"] % NQN
                    cnt["qn"] += 1
                    ACT.wait(pjb[tb].rdeps() + sqb[qi].wdeps())
                    sqb[qi].start_write()
                    ev = ACT.sig(nc.scalar.activation(out=sq[qi][:], in_=pj[tb][:], func=AF.Square))
                    pjb[tb].read(ev)
                    DVE.wait([ev])
                    ev = DVE.sig(nc.vector.tensor_reduce(out=ss[qi][:, 0:4], in_=sq[qi][:].rearrange("p (h d) -> p h d", h=4),
                                                         axis=AX.X, op=ALU.add))
                    ACT.wait([ev])
                    ev = ACT.sig(nc.scalar.activation(out=ss[qi][:, 4:8], in_=ss[qi][:, 0:4], func=AF.Sqrt, scale=1.0 / 128,
                                                      bias=epsb[:, 0:1]))
                    DVE.wait([ev])
                    ev = DVE.sig(nc.vector.reciprocal(out=ss[qi][:, 4:8], in_=ss[qi][:, 4:8]))
                    DVE.wait([ev] + qnb[ni].wdeps())
                    qnb[ni].start_write()
                    ev = DVE.sig(nc.vector.tensor_tensor(out=qn[ni][:].rearrange("p (h d) -> p h d", h=4),
                                                         in0=pj[tb][:].rearrange("p (h d) -> p h d", h=4),
                                                         in1=ss[qi][:, 4:8].unsqueeze(2).to_broadcast([128, 4, 128]), op=ALU.mult))
                    pjb[tb].read(ev)
                    qnb[ni].wrote(ev)
                    sqb[qi].wrote(ev)
                    return (ni, dst, gi)

                def evac2(state):
                    if state is None:
                        return
                    ni, dst, gi = state
                    ti = 0
                    cnt["tpq"] += 1
                    PE.wait(qnb[ni].rdeps() + tpqb[ti].wdeps())
                    tpqb[ti].start_write()
                    for hh in range(4):
                        ins = nc.tensor.transpose(out=tpq[ti][:, hh * 128:(hh + 1) * 128], in_=qn[ni][:, hh * 128:(hh + 1) * 128],
                                                  identity=ident[:])
                    ev = PE.sig(ins)
                    qnb[ni].read(ev)
                    tpqb[ti].wrote(ev)
                    si = cnt["stg"] % 4
                    cnt["stg"] += 1
                    ACT.wait(tpqb[ti].rdeps() + stgb[si].wdeps())
                    stgb[si].start_write()
                    ev = ACT.sig(nc.scalar.activation(out=stg[si][:], in_=tpq[ti][:, 0:512], func=AF.Copy, scale=gsc[:, gi:gi + 1]))
                    tpqb[ti].read(ev)
                    stgb[si].wrote(ev)
                    ev2 = k.dma(stch[si], dst.rearrange("h d t -> d h t"), stg[si][:].rearrange("d (h t) -> d h t", h=4),
                                stgb[si].rdeps())
                    stgb[si].read(ev2)

                for tb in range(4):
                    i = norm1(0, tb)
                    hd = hTb[0].wdeps() if tb == 0 else []
                    if tb == 0:
                        hTb[0].start_write()
                    norm2(0, tb, i, hd)
                pending = []
                for ti_, (t, chs) in enumerate(tiles):
                    hcur = hT[t % 2]
                    hb = hTb[t % 2]
                    nxt = tiles[ti_ + 1][0] if ti_ + 1 < len(tiles) else None
                    nstate = {}
                    for ci, c in enumerate(chs):
                        if nxt is not None and ci < 8 and ci % 2 == 0:
                            nstate[ci // 2] = norm1(nxt, ci // 2)
                        bks = []
                        for tb in range(4):
                            bks.append(cnt["pj"] % NPJ)
                            cnt["pj"] += 1
                        n2 = None
                        if nxt is not None and ci < 8 and ci % 2 == 1:
                            tb2 = ci // 2
                            hd = hTb[nxt % 2].wdeps() if tb2 == 0 else []
                            if tb2 == 0:
                                hTb[nxt % 2].start_write()
                            n2 = (tb2, hd)
                        for hf in range(2):
                            wsl, wb = ws.get()
                            wv = wsl[:].rearrange("p (a b) -> p a b", a=16)
                            for tb in range(4):
                                bk = bks[tb]
                                deps = wb.rdeps() + hb.rdeps()
                                if hf == 0:
                                    deps = deps + pjb[bk].wdeps()
                                PE.wait(deps)
                                if hf == 0:
                                    pjb[bk].start_write()
                                for kc in range(16):
                                    ins = nc.tensor.matmul(pj[bk][:], hcur[:, hf * 16 + kc, tb * 128:(tb + 1) * 128], wv[:, kc, :],
                                                           start=(hf == 0 and kc == 0), stop=(hf == 1 and kc == 15))
                                if hf == 1 or tb == 3:
                                    ev = PE.sig(ins)
                                    wb.read(ev)
                                    hb.read(ev)
                                    if hf == 1:
                                        pjb[bk].wrote(ev)
                                if hf == 1 and pending:
                                    evac2(pending[tb])
                                if hf == 1 and n2 is not None:
                                    norm2(nxt, n2[0], nstate[n2[0]], n2[1], groups=(tb,))
                            cnt["step"] += 1
                            if cnt["step"] % 3 == 0:
                                k.pump(1)
                        pending = [evac1(t, tb, c, bks[tb]) for tb in range(4)]

                for stt_ in pending:
                    evac2(stt_)
                k.barrier()

        phase_A()
        if stop_after == "A":
            return nc

        def kvblock(j, dp):
            if j < 16:
                jj = j + dp
                return jj if 0 <= jj < 16 else None
            jq = j - 16 + dp
            if jq < 0:
                return 24 + 3 + jq
            if jq >= 8:
                return 27 + jq - 8
            return 16 + jq

        NKB = 11
        B_chans = {}
        ot_events = {}

        def alloc_B(stk, tag, cset=0):
            BR = {}
            if cset not in B_chans:
                B_chans[cset] = ([Chan(k, f"hch{cset}_{i}") for i in range(2)], [Chan(k, f"och{cset}_{i}") for i in range(2)])
            BR["hch"], BR["och"] = B_chans[cset]
            BR["nb"] = 0
            BR["KT"] = [sb(f"KT{tag}{i}", [128, NKB * 128], BF16, stk) for i in range(2)]
            BR["VV"] = [sb(f"VV{tag}{i}", [128, NKB, 128], BF16, stk) for i in range(2)]
            BR["QT"] = [sb(f"QT{tag}{i}", [128, 512], BF16, stk) for i in range(2)]
            BR["BT"] = [sb(f"BT{tag}{i}", [128, 7 * 128], F32, stk) for i in range(2)]
            BR["BM"] = [sb(f"BM{tag}{i}", [16, 512], F32, stk) for i in range(2)]
            BR["OS"] = [sb(f"OS{tag}{i}", [128, 512], BF16, stk) for i in range(2)]
            BR["tmp"] = sb(f"tmpB{tag}", [128, 8, 128], F32, stk)
            BR["PT"] = [sb(f"PTB{tag}{i}", [128, 8, 128], BF16, stk) for i in range(2)]
            BR["rden"] = sb(f"rdenB{tag}", [128, 128], F32, stk)
            BR["S"] = [ps(f"SpsB{tag}{i}", [128, 512], F32, stk) for i in range(2)]
            BR["O"] = ps(f"OpsB{tag}", [128, 512], F32, stk)
            BR["hbuf"] = [Buf() for _ in range(2)]
            BR["osb"] = [Buf() for _ in range(2)]
            BR["tmpb"] = Buf()
            BR["PTb"] = [Buf() for _ in range(2)]
            BR["rdb"] = Buf()
            BR["Sb"] = Buf()
            BR["Ob"] = Buf()
            return BR

        def kvblock(j, dp):
            if j < 16:
                jj = j + dp
                return jj if 0 <= jj < 16 else None
            jq = j - 16 + dp
            if jq < 0:
                return 24 + 3 + jq
            if jq >= 8:
                return 27 + jq - 8
            return 16 + jq

        def runs_of(blks):
            out = []
            i = 0
            while i < len(blks):
                j2 = i
                while j2 + 1 < len(blks) and blks[j2 + 1] == blks[j2] + 1:
                    j2 += 1
                out.append((i, blks[i], j2 - i + 1))
                i = j2 + 1
            return out

        def gen_B(T, BR, heads):
            B_hch, B_och = BR["hch"], BR["och"]
            Sps = BR["S"]
            Ops = BR["O"]
            B_KT, B_VV, B_QT, B_BT, B_BM, B_OS = BR["KT"], BR["VV"], BR["QT"], BR["BT"], BR["BM"], BR["OS"]
            B_tmp, B_PT, B_rden = BR["tmp"], BR["PT"], BR["rden"]
            B_hbuf, B_osb, B_tmpb, B_PTb, B_rdb, B_Sb, B_Ob = (BR["hbuf"], BR["osb"], BR["tmpb"], BR["PTb"], BR["rdb"], BR["Sb"],
                                                               BR["Ob"])
            js = list(range(4 * T, 4 * T + 4))
            kb = []
            for j in js:
                for dp in range(-3, 4):
                    blk = kvblock(j, dp)
                    if blk is not None and blk not in kb:
                        kb.append(blk)
            kb = sorted(kb) + [META_BLK]
            assert len(kb) <= NKB
            pos = {blk: i for i, blk in enumerate(kb)}
            runs = runs_of(kb)
            tsl = slice(T * 512, (T + 1) * 512)
            ot_events.setdefault(T, [])

            def load_head(hi):
                typ, h = heads[hi]
                s_ = hi % 2
                deps = B_hbuf[s_].wdeps()
                B_hbuf[s_].start_write()
                ch = B_hch[s_]
                if typ == "na":
                    ksrc, vsrc, vcol, qsrc, bsrc, nb_ = KTna[h], Vna, h, QTna[h], bna[h], 7
                else:
                    g = h // 4
                    ksrc, vsrc, vcol, qsrc, bsrc, nb_ = KTsw[g], Vsw, g, QTsw[h], bsw[h], 3
                ev = None
                for (i0, b0, n_) in runs:
                    ev = k.dma(ch, B_KT[s_][:, i0 * 128:(i0 + n_) * 128], ksrc[:, b0 * 128:(b0 + n_) * 128], deps)
                    deps = []
                    ev = k.dma(ch, B_VV[s_][:, i0:i0 + n_, :],
                               vsrc[b0 * 128:(b0 + n_) * 128, vcol * 128:(vcol + 1) * 128].rearrange("(b p) d -> p b d", p=128), [])
                ev = k.dma(ch, B_QT[s_][:], qsrc[:, tsl], [])
                ev = k.dma(ch, B_BT[s_][:, 0:nb_ * 128], bsrc[:, :], [])
                if typ == "sw":
                    ev = k.dma(ch, B_BM[s_][:], bmeta[h][:, tsl], [])
                B_hbuf[s_].wrote(ev)

            def Sview(ci):
                return Sps[ci // 4][:, (ci % 4) * 128:(ci % 4 + 1) * 128]

            prev = None

            def finish(pb):
                s_, hglob, jl, cl, p_, hb, is_last_j = pb
                PE.wait(B_PTb[p_].rdeps() + hb.rdeps() + B_Ob.wdeps())
                B_Ob.start_write()
                n_ = len(cl)
                for ci, (kind, blk, dpi) in enumerate(cl):
                    first = ci == 0
                    last = ci == n_ - 1
                    if kind == "k":
                        nc.tensor.matmul(Ops[:, 0:128], B_VV[s_][:, pos[blk], :], B_PT[p_][:, ci, :], start=first, stop=last)
                        ins = nc.tensor.matmul(Ops[:, 128:256], ones[:], B_PT[p_][:, ci, :], start=False, stop=last,
                                               skip_group_check=True)
                    else:
                        nc.tensor.matmul(Ops[:, 0:128], B_VV[s_][0:16, pos[blk], :], B_PT[p_][0:16, ci, :], start=first, stop=last)
                        ins = nc.tensor.matmul(Ops[:, 128:256], ones[0:16, :], B_PT[p_][0:16, ci, :], start=False, stop=last,
                                               skip_group_check=True)
                ev = PE.sig(ins)
                B_PTb[p_].read(ev)
                hb.read(ev)
                B_Ob.wrote(ev)
                DVE.wait(B_Ob.rdeps() + B_rdb.wdeps())
                B_rdb.start_write()
                ev = DVE.sig(nc.vector.tensor_scalar(out=B_rden[:], in0=Ops[:, 128:256], scalar1=esink[:, hglob:hglob + 1],
                                                     scalar2=None, op0=ALU.add))
                B_Ob.read(ev)
                DVE.wait([ev])
                ev = DVE.sig(nc.vector.reciprocal(out=B_rden[:], in_=B_rden[:]))
                B_rdb.wrote(ev)
                DVE.wait([ev] + B_osb[s_].wdeps())
                ev = DVE.sig(nc.vector.tensor_tensor(out=B_OS[s_][:, jl * 128:(jl + 1) * 128], in0=Ops[:, 0:128], in1=B_rden[:],
                                                     op=ALU.mult))
                B_Ob.read(ev)
                B_rdb.read(ev)
                B_osb[s_].wrote(ev)
                if is_last_j:
                    ev = k.dma(B_och[s_], OT[hglob][:, tsl], B_OS[s_][:], B_osb[s_].rdeps())
                    B_osb[s_].read(ev)
                    ot_events[T].append(ev)

            load_head(0)
            for hi, (typ, h) in enumerate(heads):
                s_ = hi % 2
                hb = B_hbuf[s_]
                hglob = h if typ == "na" else 16 + h
                dps = list(range(-3, 4)) if typ == "na" else [-1, 0, 1]
                for jl, j in enumerate(js):
                    cl = []
                    full = {}
                    for dpi, dp in enumerate(dps):
                        blk = kvblock(j, dp)
                        if blk is None:
                            continue
                        if typ == "na" and j < 16:
                            v = []
                            for b_ in range(2):
                                qr = 2 * j + b_
                                rs_ = min(max(qr - 4, 0), 32 - 8)
                                for a_ in range(2):
                                    kr = 2 * (j + dp) + a_
                                    v.append(0 <= kr < 32 and rs_ <= kr < rs_ + 8)
                            if not any(v):
                                continue
                            full[dpi] = all(v)
                        cl.append(("k", blk, dpi))
                    cl.append(("m", META_BLK, None))
                    if prev is not None:
                        finish(prev)
                    if jl == 0 and hi + 1 < len(heads):
                        load_head(hi + 1)
                    if jl == 0:
                        pass
                    p_ = BR["nb"] % 2
                    BR["nb"] += 1
                    PE.wait(hb.rdeps() + B_Sb.wdeps())
                    B_Sb.start_write()
                    qsl = B_QT[s_][:, jl * 128:(jl + 1) * 128]
                    for ci, (kind, blk, dpi) in enumerate(cl):
                        if kind == "k":
                            ins = nc.tensor.matmul(Sview(ci), B_KT[s_][:, pos[blk] * 128:(pos[blk] + 1) * 128], qsl, start=True, stop=True)
                        else:
                            ins = nc.tensor.matmul(Sview(ci)[0:16, :], B_KT[s_][:, pos[blk] * 128:pos[blk] * 128 + 16], qsl,
                                                   start=True, stop=True)
                    ev = PE.sig(ins)
                    hb.read(ev)
                    B_Sb.wrote(ev)
                    DVE.wait(B_Sb.rdeps() + hb.rdeps() + B_tmpb.wdeps())
                    B_tmpb.start_write()
                    for ci, (kind, blk, dpi) in enumerate(cl):
                        if kind == "k":
                            ins = nc.vector.tensor_tensor(out=B_tmp[:, ci, :], in0=Sview(ci), in1=B_BT[s_][:, dpi * 128:(dpi + 1) * 128],
                                                          op=ALU.add)
                        elif typ == "na":
                            ins = nc.vector.tensor_copy(out=B_tmp[0:16, ci, :], in_=Sview(ci)[0:16, :])
                        else:
                            ins = nc.vector.tensor_tensor(out=B_tmp[0:16, ci, :], in0=Sview(ci)[0:16, :],
                                                          in1=B_BM[s_][:, jl * 128:(jl + 1) * 128], op=ALU.add)
                    ev = DVE.sig(ins)
                    B_Sb.read(ev)
                    hb.read(ev)
                    B_tmpb.wrote(ev)
                    ACT.wait(B_tmpb.rdeps() + B_PTb[p_].wdeps())
                    B_PTb[p_].start_write()
                    for ci, (kind, blk, dpi) in enumerate(cl):
                        if kind == "k":
                            if typ == "na" and full.get(dpi, False):
                                ins = nc.scalar.activation(out=B_PT[p_][:, ci, :], in_=B_tmp[:, ci, :], func=AF.Exp)
                            elif typ == "na":
                                e0 = (j * 7 + dpi) * 2
                                nc.scalar.activation(out=B_PT[p_][:, ci, 0:64], in_=B_tmp[:, ci, 0:64], func=AF.Exp,
                                                     bias=mna[:, e0:e0 + 1])
                                ins = nc.scalar.activation(out=B_PT[p_][:, ci, 64:128], in_=B_tmp[:, ci, 64:128], func=AF.Exp,
                                                           bias=mna[:, e0 + 1:e0 + 2])
                            else:
                                e0 = j * 3 + dpi
                                ins = nc.scalar.activation(out=B_PT[p_][:, ci, :], in_=B_tmp[:, ci, :], func=AF.Exp,
                                                           bias=msw[:, e0:e0 + 1])
                        else:
                            ins = nc.scalar.activation(out=B_PT[p_][0:16, ci, :], in_=B_tmp[0:16, ci, :], func=AF.Exp)
                    ev = ACT.sig(ins)
                    B_tmpb.read(ev)
                    B_PTb[p_].wrote(ev)
                    prev = (s_, hglob, jl, cl, p_, hb, jl == 3)
                    yield
            finish(prev)
            yield

        bgen = {"g": None, "T": None, "hook": 0, "gs": []}
        ALL_HEADS = [("na", h) for h in range(16)] + [("sw", h) for h in range(16)]

        def pumpB(n_=1):
            for _ in range(n_):
                if not bgen["gs"]:
                    bgen["g"] = None
                    return
                g_ = bgen["gs"].pop(0)
                try:
                    next(g_)
                    bgen["gs"].append(g_)
                except StopIteration:
                    pass
                if not bgen["gs"]:
                    bgen["g"] = None

        def hookB():
            pumpB(1)
            bgen["hook"] += 1
            if bgen["hook"] % 4 == 0:
                k.pump(1)

        def startB(T, rsets):
            n_ = len(rsets)
            bgen["gs"] = [gen_B(T, R_, ALL_HEADS[i::n_]) for i, R_ in enumerate(rsets)]
            bgen["g"] = True
            bgen["T"] = T

        def drainB():
            while bgen["g"] is not None:
                pumpB(1)

        with ExitStack() as sB0:
            startB(0, [alloc_B(sB0, "s", 0), alloc_B(sB0, "t", 1)])
            nstep = 0
            while bgen["g"] is not None:
                pumpB(1)
                nstep += 1
                if nstep % 4 == 0:
                    k.pump(1)
            k.barrier(exclude=cch)
        if stop_after == "B":
            return nc

        def phase_CD():
            with ExitStack() as s1:
                BRc = alloc_B(s1, "c", 0)
                ws = make_ws(3, s1, "c")
                x1 = sb("x1", [128, 4, D], F32, s1)
                x1b = [Buf() for _ in range(4)]
                xch = [Chan(k, f"x1ch{i}") for i in range(4)]
                ych = Chan(k, "ych")
                hT = sb("h2T", [128, 32, 512], BF16, s1)
                hTb = Buf()
                och = Chan(k, "otch")
                uT = sb("uT", [128, 16, 512], BF16, s1)
                uTb = Buf()
                xn = uT[:].rearrange("p a b -> p (a b)")[:, 0:D]
                xnb = uTb
                stt = sb("stt2", [128, 8], F32, s1)
                sttb = Buf()
                ssp = sb("ssp", [128, 64], F32, s1)
                sspb = Buf()
                sqj = sb("sqj", [128, 256], BF16, s1)
                NXS = 4
                xs = [sb(f"xs{i}", [128, 256], F32, s1) for i in range(NXS)]
                xsb = [Buf() for _ in range(NXS)]
                xsch = [Chan(k, f"xsch{i}") for i in range(NXS)]
                rl = [sb(f"rl{i}", [128, 512], F32, s1) for i in range(2)]
                rlb = [Buf() for _ in range(2)]
                pc = [ps(f"pc{i}", [128, 512], F32, s1) for i in range(2)]
                pcb = [Buf() for _ in range(2)]
                tpx = [ps("tpy0", [128, 1024], BF16, s1)]
                tpxb = [Buf()]
                pu = [ps(f"pu{i}", [128, 512], F32, s1) for i in range(2)]
                pub = [Buf() for _ in range(2)]
                tpx.append(pu[0][:].bitcast(BF16))
                tpxb.append(pub[0])
                cnt = {"pc": 0, "pu": 0, "rl": 0}

                aps = []
                for t in range(6):
                    for c in range(16):
                        aps.append((("WO", c), WO[c]))
                    for j in range(8):
                        for fq in range(8):
                            aps.append((("WU", j * 8 + fq), WU[j * 8 + fq]))
                        for db in range(8):
                            aps.append((("WD", j * 8 + db), WD[j * 8 + db]))
                ws.reset(aps)
                ws.topup()

                def load_oT(t_):
                    deps = hTb.wdeps() + ot_events[t_]
                    hTb.start_write()
                    ev_ = k.dma(och, hT[:], OT[:, :, t_ * 512:(t_ + 1) * 512].rearrange("h d t -> d h t"), deps)
                    hTb.wrote(ev_)

                drainB()
                load_oT(0)
                for t in range(6):
                    tok0 = t * 512
                    if t + 1 < 6:
                        startB(t + 1, [BRc])
                    xpieces = [(c_, tb_) for c_ in range(16) for tb_ in range(4)]
                    xstate = {"i": 0}

                    def issue_x():
                        i_ = xstate["i"]
                        if i_ >= len(xpieces):
                            return
                        c_, tb_ = xpieces[i_]
                        sl_ = i_ % NXS
                        deps_ = xsb[sl_].wdeps()
                        xsb[sl_].start_write()
                        ev_ = k.dma(xsch[sl_], xs[sl_][:], xkv[tok0 + tb_ * 128:tok0 + (tb_ + 1) * 128, c_ * 256:(c_ + 1) * 256], deps_)
                        xsb[sl_].wrote(ev_)
                        xstate["i"] = i_ + 1

                    for _ in range(NXS - 1):
                        issue_x()
                    sdeps = sspb.wdeps()
                    sspb.start_write()
                    for c in range(16):
                        wsl, wb = ws.get()
                        wv = wsl[:].rearrange("p (a b) -> p a b", a=32)
                        for tb in range(4):
                            bi = cnt["pc"] % 2
                            cnt["pc"] += 1
                            PE.wait(wb.rdeps() + hTb.rdeps() + pcb[bi].wdeps())
                            pcb[bi].start_write()
                            for kc in range(32):
                                ins = nc.tensor.matmul(pc[bi][:, 0:256], hT[:, kc, tb * 128:(tb + 1) * 128], wv[:, kc, :],
                                                       start=(kc == 0), stop=(kc == 31))
                            ev = PE.sig(ins)
                            wb.read(ev)
                            hTb.read(ev)
                            pcb[bi].wrote(ev)
                            issue_x()
                            sl = (c * 4 + tb) % NXS
                            DVE.wait(pcb[bi].rdeps() + xsb[sl].rdeps() + x1b[tb].wdeps())
                            ev = DVE.sig(nc.vector.tensor_tensor(out=x1[:, tb, c * 256:(c + 1) * 256], in0=pc[bi][:, 0:256],
                                                                 in1=xs[sl][:], op=ALU.add))
                            pcb[bi].read(ev)
                            xsb[sl].read(ev)
                            x1b[tb].wrote(ev)
                            ACT.wait([ev] + sdeps)
                            ev = ACT.sig(nc.scalar.activation(out=sqj[:], in_=x1[:, tb, c * 256:(c + 1) * 256], func=AF.Square,
                                                              accum_out=ssp[:, tb * 16 + c:tb * 16 + c + 1]))
                            x1b[tb].read(ev)
                            sspb.wrote(ev)
                            if tb % 2 == 1:
                                hookB()
                    hdeps = hTb.wdeps()
                    hTb.start_write()
                    DVE.wait(sspb.rdeps() + sttb.wdeps())
                    sttb.start_write()
                    ev = DVE.sig(nc.vector.tensor_reduce(out=stt[:, 0:4], in_=ssp[:].rearrange("p (t c) -> p t c", t=4), axis=AX.X,
                                                         op=ALU.add))
                    sspb.read(ev)
                    ACT.wait([ev])
                    ev = ACT.sig(nc.scalar.activation(out=stt[:, 4:8], in_=stt[:, 0:4], func=AF.Sqrt, scale=1.0 / D,
                                                      bias=epsb[:, 0:1]))
                    DVE.wait([ev])
                    ev = DVE.sig(nc.vector.reciprocal(out=stt[:, 4:8], in_=stt[:, 4:8]))
                    sttb.wrote(ev)
                    for tb in range(4):
                        DVE.wait(x1b[tb].rdeps() + xnb.wdeps() + sttb.rdeps())
                        xnb.start_write()
                        ev = DVE.sig(nc.vector.tensor_scalar(out=xn, in0=x1[:, tb, :], scalar1=stt[:, 4 + tb:5 + tb], scalar2=None,
                                                             op0=ALU.mult))
                        x1b[tb].read(ev)
                        xnb.wrote(ev)
                        sttb.read(ev)
                        rms_part2(xn, xnb, tpx, tpxb, hT, hdeps, hTb, tb, 32, 2 * tb)
                    for j in range(8):
                        udeps = uTb.wdeps()
                        uTb.start_write()
                        for fq in range(8):
                            wsl, wb = ws.get()
                            wv = wsl[:].rearrange("p (a b) -> p a b", a=32)
                            for f2 in range(2):
                                fb_ = fq * 2 + f2
                                ui = cnt["pu"] % 2
                                cnt["pu"] += 1
                                PE.wait(wb.rdeps() + hTb.rdeps() + pub[ui].wdeps())
                                pub[ui].start_write()
                                for kc in range(32):
                                    ins = nc.tensor.matmul(pu[ui][:], wv[:, kc, f2 * 128:(f2 + 1) * 128], hT[:, kc, :],
                                                           start=(kc == 0), stop=(kc == 31))
                                ev = PE.sig(ins)
                                wb.read(ev)
                                hTb.read(ev)
                                pub[ui].wrote(ev)
                                ri = cnt["rl"] % 2
                                cnt["rl"] += 1
                                DVE.wait(pub[ui].rdeps() + rlb[ri].wdeps())
                                rlb[ri].start_write()
                                ev = DVE.sig(nc.vector.tensor_scalar(out=rl[ri][:], in0=pu[ui][:], scalar1=0.0, scalar2=None,
                                                                     op0=ALU.max))
                                pub[ui].read(ev)
                                rlb[ri].wrote(ev)
                                POOL.wait(rlb[ri].rdeps() + udeps)
                                ev = POOL.sig(nc.gpsimd.tensor_tensor(out=uT[:, fb_, :], in0=rl[ri][:], in1=rl[ri][:], op=ALU.mult))
                                rlb[ri].read(ev)
                                uTb.wrote(ev)
                                hookB()
                        if j == 7 and t + 1 < 6:
                            drainB()
                            load_oT(t + 1)
                        for db in range(8):
                            wsl, wb = ws.get()
                            wv = wsl[:].rearrange("p (a b) -> p a b", a=16)
                            for tb in range(4):
                                bi = cnt["pc"] % 2
                                cnt["pc"] += 1
                                PE.wait(wb.rdeps() + uTb.rdeps() + pcb[bi].wdeps())
                                pcb[bi].start_write()
                                for fc in range(16):
                                    ins = nc.tensor.matmul(pc[bi][:], uT[:, fc, tb * 128:(tb + 1) * 128], wv[:, fc, :],
                                                           start=(fc == 0), stop=(fc == 15))
                                ev = PE.sig(ins)
                                wb.read(ev)
                                uTb.read(ev)
                                pcb[bi].wrote(ev)
                                DVE.wait(pcb[bi].rdeps() + x1b[tb].wdeps())
                                ev = DVE.sig(nc.vector.tensor_tensor(out=x1[:, tb, db * 512:(db + 1) * 512], in0=pc[bi][:],
                                                                     in1=x1[:, tb, db * 512:(db + 1) * 512], op=ALU.add))
                                pcb[bi].read(ev)
                                x1b[tb].wrote(ev)
                                if tb % 2 == 1:
                                    hookB()
                    for tb in range(4):
                        ev = k.dma(ych, y[tok0 + tb * 128:tok0 + (tb + 1) * 128, :], x1[:, tb, :], x1b[tb].rdeps())
                        x1b[tb].read(ev)
                k.barrier()

        phase_CD()
    return nc


def _t5_bucket(rel):
    nb = 16
    max_exact = 8
    ret = np.where(rel > 0, nb, 0)
    n = np.abs(rel)
    large = max_exact + (np.log(np.maximum(n, 1) / max_exact) / np.log(128 / max_exact) * (nb - max_exact)).astype(np.int64)
    large = np.minimum(large, nb - 1)
    return ret + np.where(n < max_exact, n, large)


def _host_prep(x_prompt, x_sample, meta_tokens, t5_bias, norm_attn, q_norm_na, k_norm_na, na_rpb, q_norm_swa, k_norm_swa,
               swa_sink, norm_mlp):
    f32 = np.float32
    xp = np.asarray(x_prompt, f32)[0]
    xs = np.asarray(x_sample, f32)
    rpb = np.asarray(na_rpb, f32)[0]
    t5 = np.asarray(t5_bias, f32)
    a = np.arange(2)[:, None]
    kc = np.arange(64)[None, :]
    cs = np.clip(np.arange(64) - 8, 0, 48)
    bna = np.empty((16, 128, 7, 128), f32)
    for dpi, dp in enumerate(range(-3, 4)):
        A = np.repeat(np.arange(2), 64)
        KC = np.tile(np.arange(64), 2)
        dr = 2 * dp + A[:, None] - A[None, :]
        dc = KC[:, None] - KC[None, :]
        col_in = (KC[:, None] >= cs[KC][None, :]) & (KC[:, None] < cs[KC][None, :] + 16)
        ok = col_in & (np.abs(dr) <= 7)
        g = rpb[:, np.clip(dr, -7, 7) + 7, np.clip(dc, -15, 15) + 15]
        bna[:, :, dpi, :] = np.where(ok[None], g, f32(NEG))
    bsw = np.empty((16, 128, 3, 128), f32)
    P = np.arange(128)
    for dpi, dp in enumerate((-1, 0, 1)):
        rel = (dp * 128 + P[:, None]) - P[None, :]
        ok = np.abs(rel) <= 128
        g = t5[_t5_bucket(rel)]
        bsw[:, :, dpi, :] = np.where(ok[None], g.transpose(2, 0, 1), f32(NEG))
    pv_common = np.zeros((128, 84), f32)
    pv_common[:, 0:32] = np.asarray(norm_attn, f32)[0].reshape(32, 128).T
    pv_common[:, 32:64] = np.asarray(norm_mlp, f32)[0].reshape(32, 128).T
    pv_common[:, 64] = np.asarray(q_norm_na, f32)[0]
    pv_common[:, 65] = np.asarray(k_norm_na, f32)[0]
    pv_common[:, 66] = np.asarray(q_norm_swa, f32)[0]
    pv_common[:, 67] = np.asarray(k_norm_swa, f32)[0]
    pv_common[:, 68:84] = np.asarray(swa_sink, f32)[0][None, :]
    ident = np.eye(128, dtype=f32)
    meta = np.asarray(meta_tokens, f32)
    per_core = []
    for c in range(NCORES):
        xkv = np.zeros((NKVB * 128, D), f32)
        xkv[0:2048] = xs[c]
        xkv[2048:3072] = xp[1024 * c:1024 * (c + 1)]
        for i in range(3):
            pr = 8 * c - 3 + i
            if pr >= 0:
                xkv[(24 + i) * 128:(25 + i) * 128] = xp[pr * 128:(pr + 1) * 128]
            pr = 8 * c + 8 + i
            if pr < 64:
                xkv[(27 + i) * 128:(28 + i) * 128] = xp[pr * 128:(pr + 1) * 128]
        xkv[META_BLK * 128:META_BLK * 128 + 16] = meta
        mcna = np.zeros((128, NQB, 7, 2), f32)
        mcsw = np.zeros((128, NQB, 3), f32)
        tpos = np.empty(NQB * 128, np.int64)
        for j in range(NQB):
            if j < 16:
                rows, jg = 32, j
                tpos[j * 128:(j + 1) * 128] = j * 128 + np.arange(128)
            else:
                rows, jg = 128, 8 * c + (j - 16)
                tpos[j * 128:(j + 1) * 128] = jg * 128 + np.arange(128)
            nbk = rows // 2
            for dpi, dp in enumerate(range(-3, 4)):
                for b in range(2):
                    qr = 2 * jg + b
                    rs = min(max(qr - 4, 0), rows - 8)
                    for a_ in range(2):
                        kr = 2 * (jg + dp) + a_
                        ok = (0 <= kr < rows) and (rs <= kr < rs + 8)
                        mcna[a_ * 64:(a_ + 1) * 64, j, dpi, b] = 0.0 if ok else NEG
            for dpi, dp in enumerate((-1, 0, 1)):
                ok = 0 <= jg + dp < nbk
                mcsw[:, j, dpi] = 0.0 if ok else NEG
        relm = np.arange(16)[:, None] - (16 + tpos)[None, :]
        bmeta = np.ascontiguousarray(t5[_t5_bucket(relm)].transpose(2, 0, 1))
        per_core.append({
            "xkv": xkv,
            "pvec": pv_common,
            "ident": ident,
            "bna": bna.reshape(16, 128, 7 * 128),
            "bsw": bsw.reshape(16, 128, 3 * 128),
            "bmeta": bmeta,
            "mcna": mcna.reshape(128, NQB * 14),
            "mcsw": mcsw.reshape(128, NQB * 3),
        })
    return per_core


_NC_CACHE = {}


def kernel(x_prompt, x_sample, meta_tokens, t5_bias, norm_attn, w_in, q_norm_na, k_norm_na, na_rpb, q_norm_swa, k_norm_swa,
           swa_sink, w_out, norm_mlp, w_up, w_down):
    per_core = _host_prep(x_prompt, x_sample, meta_tokens, t5_bias, norm_attn, q_norm_na, k_norm_na, na_rpb, q_norm_swa,
                          k_norm_swa, swa_sink, norm_mlp)
    wi = np.ascontiguousarray(np.asarray(w_in, np.float32)[0])
    wo = np.ascontiguousarray(np.asarray(w_out, np.float32)[0])
    wu = np.ascontiguousarray(np.asarray(w_up, np.float32)[0])
    wd = np.ascontiguousarray(np.asarray(w_down, np.float32)[0])
    for d in per_core:
        d.update({"w_in": wi, "w_out": wo, "w_up": wu, "w_down": wd})
    if "nc" not in _NC_CACHE:
        _NC_CACHE["nc"] = build_nc()
    nc = _NC_CACHE["nc"]
    res = run_bass_kernel_spmd(nc, per_core, core_ids=list(range(NCORES)))
    y_prompt = np.empty((1, 8192, D), np.float32)
    y_sample = np.empty((8, 2048, D), np.float32)
    for c in range(NCORES):
        yc = np.asarray(res.results[c]["y"])
        y_sample[c] = yc[0:2048]
        y_prompt[0, 1024 * c:1024 * (c + 1)] = yc[2048:3072]
    return (y_prompt, y_sample)
```

```python
import numpy as np
from contextlib import ExitStack
import concourse.bass as bass
import concourse.mybir as mybir
from concourse.bass_utils import run_bass_kernel_spmd

F32 = mybir.dt.float32
BF16 = mybir.dt.bfloat16
AF = mybir.ActivationFunctionType
ALU = mybir.AluOpType
AX = mybir.AxisListType

NCORES = 8
D = 4096
NKVB = 32
NQB = 24
META_BLK = 30
NEG = -30000.0
EPS = 1e-6
SEM_LIMIT = 30000


class Ev:
    __slots__ = ("sem", "val")

    def __init__(self, sem, val):
        self.sem = sem
        self.val = val


class Buf:
    def __init__(self):
        self.w = {}
        self.r = {}

    def rdeps(self):
        return [Ev(k, v) for k, v in self.w.items()]

    def wdeps(self):
        return [Ev(k, v) for k, v in self.w.items()] + [Ev(k, v) for k, v in self.r.items()]

    def start_write(self):
        self.w = {}
        self.r = {}

    def wrote(self, ev):
        self.w[ev.sem] = max(self.w.get(ev.sem, 0), ev.val)

    def read(self, ev):
        self.r[ev.sem] = max(self.r.get(ev.sem, 0), ev.val)


class Eng:
    def __init__(self, k, eng, name):
        self.k = k
        self.eng = eng
        self.name = name
        self.sem = None
        self.cnt = 0
        self.nep = 0
        self.waited = {}
        self.last = None

    def wait(self, evs):
        for ev in evs:
            if ev is None:
                continue
            if self.waited.get(ev.sem, 0) >= ev.val:
                continue
            self.eng.wait_ge(ev.sem, ev.val)
            self.waited[ev.sem] = ev.val

    def sig(self, ins):
        if self.sem is None or self.cnt >= SEM_LIMIT:
            self.sem = self.k.newsem(f"{self.name}{self.nep}")
            self.nep += 1
            self.cnt = 0
        self.cnt += 1
        ins.then_inc(self.sem, 1)
        ev = Ev(self.sem, self.cnt)
        self.last = ev
        return ev


class Chan:
    def __init__(self, k, name):
        self.sem = k.newsem(name)
        self.cnt = 0
        k.chans.append(self)

    def ev(self):
        return Ev(self.sem, self.cnt)


class K:
    def __init__(self, nc, st):
        self.nc = nc
        self.st = st
        self.nsem = 0
        self.chans = []
        self.PE = Eng(self, nc.tensor, "pe")
        self.ACT = Eng(self, nc.scalar, "act")
        self.DVE = Eng(self, nc.vector, "dve")
        self.POOL = Eng(self, nc.gpsimd, "pool")
        self.SP = Eng(self, nc.sync, "sp")
        self.engs = [self.PE, self.ACT, self.DVE, self.POOL]

    def newsem(self, name):
        self.nsem += 1
        return self.st.enter_context(self.nc.semaphore(name))

    def dma(self, chan, out, in_, deps, q=None):
        q = q or self.SP
        q.wait(deps)
        ins = q.eng.dma_start(out=out, in_=in_)
        chan.cnt += 16
        assert chan.cnt < SEM_LIMIT, chan.cnt
        ins.then_inc(chan.sem, 16)
        return Ev(chan.sem, chan.cnt)

    def barrier(self, exclude=()):
        evs = [e.last for e in self.engs if e.last is not None]
        evs += [c.ev() for c in self.chans if c.cnt > 0 and c not in exclude]
        for e in self.engs + [self.SP]:
            e.wait(evs)


class WStream:
    def __init__(self, k, slots, bufs, chans):
        self.k = k
        self.slots = slots
        self.bufs = bufs
        self.chans = chans
        self.n = len(slots)
        self.pos = 0
        self.reset([])

    def reset(self, aps):
        self.aps = aps
        self.issued = 0
        self.consumed = 0
        self.base = self.pos

    def _issue(self):
        i = self.issued
        s = (self.base + i) % self.n
        b = self.bufs[s]
        deps = b.wdeps()
        b.start_write()
        key, ap = self.aps[i]
        deps = deps + self.k.ensure(key)
        ev = self.k.dma(self.chans[s], self.slots[s][:], ap, deps)
        b.wrote(ev)
        self.issued += 1

    def topup(self):
        while self.issued < min(self.consumed + self.n, len(self.aps)):
            self._issue()

    def get(self):
        self.topup()
        s = (self.base + self.consumed) % self.n
        self.consumed += 1
        self.pos = self.base + self.consumed
        return self.slots[s], self.bufs[s]


def build_nc(stop_after=None, debug=False):
    nc = bass.Bass("TRN2", target_bir_lowering=False)
    dk = "ExternalOutput" if debug else "Internal"

    def din(name, shape, dt=F32):
        return nc.dram_tensor(name, list(shape), dt, kind="ExternalInput").ap()

    xkv = din("xkv", [NKVB * 128, D])
    w_in = din("w_in", [D, 9216])
    w_out = din("w_out", [D, D])
    w_up = din("w_up", [D, 4 * D])
    w_down = din("w_down", [4 * D, D])
    pvec = din("pvec", [128, 84])
    ident_in = din("ident", [128, 128])
    bna = din("bna", [16, 128, 7 * 128])
    bsw = din("bsw", [16, 128, 3 * 128])
    bmeta = din("bmeta", [16, 16, NQB * 128])
    mcna = din("mcna", [128, NQB * 14])
    mcsw = din("mcsw", [128, NQB * 3])
    y = nc.dram_tensor("y", [NQB * 128, D], F32, kind="ExternalOutput").ap()

    WI = nc.dram_tensor("WI", [36, 128, 8192], BF16, kind=dk).ap()
    WO = nc.dram_tensor("WO", [16, 128, 8192], BF16).ap()
    WU = nc.dram_tensor("WU", [64, 128, 8192], BF16).ap()
    WD = nc.dram_tensor("WD", [64, 128, 8192], BF16).ap()
    QTna = nc.dram_tensor("QTna", [16, 128, NQB * 128], BF16, kind=dk).ap()
    QTsw = nc.dram_tensor("QTsw", [16, 128, NQB * 128], BF16, kind=dk).ap()
    KTna = nc.dram_tensor("KTna", [16, 128, NKVB * 128], BF16, kind=dk).ap()
    KTsw = nc.dram_tensor("KTsw", [4, 128, NKVB * 128], BF16, kind=dk).ap()
    Vna = nc.dram_tensor("Vna", [NKVB * 128, 2048], BF16, kind=dk).ap()
    Vsw = nc.dram_tensor("Vsw", [NKVB * 128, 512], BF16, kind=dk).ap()
    OT = nc.dram_tensor("OT", [32, 128, NQB * 128], BF16, kind=dk).ap()

    with ExitStack() as st:
        k = K(nc, st)
        PE, ACT, DVE, POOL, SP = k.PE, k.ACT, k.DVE, k.POOL, k.SP

        def sb(name, shape, dt, stack=st):
            return stack.enter_context(nc.sbuf_tensor(name, list(shape), dt))

        def ps(name, shape, dt, stack=st):
            return stack.enter_context(nc.psum_tensor(name, list(shape), dt))

        ident = sb("ident_sb", [128, 128], BF16)
        ones = sb("ones", [128, 128], BF16)
        pv = sb("pv", [128, 84], F32)
        gsc = sb("gsc", [128, 4], F32)
        esink = sb("esink", [128, 32], F32)
        epsb = sb("epsb", [128, 1], F32)
        mna = sb("mna", [128, NQB * 14], F32)
        msw = sb("msw", [128, NQB * 3], F32)
        wchans = [Chan(k, f"wch{i}") for i in range(4)]

        def make_ws(nw, stack, tag):
            slots = [sb(f"w{tag}{i}", [128, 8192], BF16, stack) for i in range(nw)]
            return WStream(k, slots, [Buf() for _ in range(nw)], wchans[:nw])
        c_const = Chan(k, "cconst")
        with ExitStack() as s0:
            id32 = sb("id32", [128, 128], F32, s0)
            e1 = k.dma(c_const, id32[:], ident_in[:, :], [])
            e2 = k.dma(c_const, pv[:], pvec[:, :], [])
            e3 = k.dma(c_const, mna[:], mcna[:, :], [])
            e4 = k.dma(c_const, msw[:], mcsw[:, :], [])
            DVE.wait([e4])
            DVE.sig(nc.vector.tensor_copy(out=ident[:], in_=id32[:]))
            DVE.sig(nc.vector.memset(ones[:], 1.0))
            DVE.sig(nc.vector.memset(epsb[:], EPS))
            DVE.sig(nc.vector.memset(esink[:], 0.0))
            sc = 128.0 ** -0.5
            DVE.sig(nc.vector.tensor_scalar(out=gsc[:, 0:1], in0=pv[:, 64:65], scalar1=sc, scalar2=None, op0=ALU.mult))
            DVE.sig(nc.vector.tensor_copy(out=gsc[:, 1:2], in_=pv[:, 65:66]))
            DVE.sig(nc.vector.tensor_scalar(out=gsc[:, 2:3], in0=pv[:, 66:67], scalar1=sc, scalar2=None, op0=ALU.mult))
            ev = DVE.sig(nc.vector.tensor_copy(out=gsc[:, 3:4], in_=pv[:, 67:68]))
            ACT.wait([e4, ev])
            ACT.sig(nc.scalar.activation(out=esink[:, 16:32], in_=pv[:, 68:84], func=AF.Exp))
            k.barrier()

        NCC = 8
        cch = [Chan(k, f"cch{i}") for i in range(NCC)]
        cchunks = []
        ready = {}

        def add_chunk(key, view, dst, k0, nk, c0, ncol):
            cchunks.append((key, view[:, k0:k0 + nk, c0:c0 + ncol], dst.rearrange("p (k n) -> p k n", k=nk)))

        wi_v = w_in.rearrange("(k p) n -> p k n", p=128)
        for c in range(18):
            for hf in range(2):
                add_chunk(("WI", c * 2 + hf), wi_v, WI[c * 2 + hf], hf * 16, 16, c * 512, 512)
        wo_v = w_out.rearrange("(k p) n -> p k n", p=128)
        for c in range(16):
            add_chunk(("WO", c), wo_v, WO[c], 0, 32, c * 256, 256)
        wu_v = w_up.rearrange("(k p) n -> p k n", p=128)
        wd_v = w_down.rearrange("(k p) n -> p k n", p=128)
        for j in range(8):
            for fq in range(8):
                c = j * 8 + fq
                add_chunk(("WU", c), wu_v, WU[c], 0, 32, c * 256, 256)
            for db in range(8):
                add_chunk(("WD", j * 8 + db), wd_v, WD[j * 8 + db], j * 16, 16, db * 512, 512)
        cstate = {"i": 0}

        def issue_cast(deps):
            i = cstate["i"]
            if i >= len(cchunks):
                return False
            key, src, dst = cchunks[i]
            ch = cch[i % NCC]
            ev = k.dma(ch, dst, src, [Ev(ch.sem, ch.cnt)] + deps, q=POOL)
            ready[key] = [ev]
            cstate["i"] = i + 1
            return True

        def pump(n_):
            for _ in range(n_):
                deps = [PE.last] if PE.last is not None else []
                if not issue_cast(deps):
                    return

        def ensure(key):
            while key not in ready:
                assert issue_cast([])
            return ready[key]

        k.pump = pump
        k.ensure = ensure
        if stop_after == "S":
            return nc
        ensure(("WI", 35))

        def rms_part1(xb, xbuf, xn, xnbuf, ssx, rsx, statbuf, src_ap, chan):
            deps = xbuf.wdeps()
            xbuf.start_write()
            ev = k.dma(chan, xb[:], src_ap, deps)
            xbuf.wrote(ev)
            ACT.wait(xbuf.rdeps() + xnbuf.wdeps() + statbuf.wdeps())
            xnbuf.start_write()
            statbuf.start_write()
            ev = ACT.sig(nc.scalar.activation(out=xn[:], in_=xb[:], func=AF.Square, accum_out=ssx[:, 0:1]))
            xbuf.read(ev)
            ACT.wait([ev])
            ev = ACT.sig(nc.scalar.activation(out=rsx[:, 0:1], in_=ssx[:, 0:1], func=AF.Sqrt, scale=1.0 / D, bias=epsb[:, 0:1]))
            DVE.wait([ev])
            ev = DVE.sig(nc.vector.reciprocal(out=rsx[:, 0:1], in_=rsx[:, 0:1]))
            DVE.wait([ev])
            ev = DVE.sig(nc.vector.tensor_scalar(out=xn[:], in0=xb[:], scalar1=rsx[:, 0:1], scalar2=None, op0=ALU.mult))
            xbuf.read(ev)
            xnbuf.wrote(ev)
            statbuf.wrote(ev)

        def rms_part2(xn, xnbuf, tpx, tpbufs, hT, hbuf_deps, hbuf, tb, gcol0, tpi, groups=(0, 1, 2, 3)):
            for g in groups:
                t = tpx[(tpi + g) % len(tpx)]
                tbuf = tpbufs[(tpi + g) % len(tpx)]
                PE.wait(xnbuf.rdeps() + tbuf.wdeps())
                tbuf.start_write()
                for i in range(8):
                    kc = g * 8 + i
                    ins = nc.tensor.transpose(out=t[:, i * 128:(i + 1) * 128], in_=xn[:, kc * 128:(kc + 1) * 128], identity=ident[:])
                ev = PE.sig(ins)
                xnbuf.read(ev)
                tbuf.wrote(ev)
                for i in range(8):
                    kc = g * 8 + i
                    E = ACT if ((tpi + g) % 2 == 0) else DVE
                    E.wait(tbuf.rdeps() + hbuf_deps)
                    if E is ACT:
                        ins = nc.scalar.activation(out=hT[:, kc, tb * 128:(tb + 1) * 128], in_=t[:, i * 128:(i + 1) * 128],
                                                   func=AF.Copy, scale=pv[:, gcol0 + kc:gcol0 + kc + 1])
                    else:
                        ins = nc.vector.tensor_scalar(out=hT[:, kc, tb * 128:(tb + 1) * 128], in0=t[:, i * 128:(i + 1) * 128],
                                                      scalar1=pv[:, gcol0 + kc:gcol0 + kc + 1], scalar2=None, op0=ALU.mult)
                    ev = E.sig(ins)
                    tbuf.read(ev)
                    hbuf.wrote(ev)

        def phase_A():
            with ExitStack() as s1:
                ws = make_ws(4, s1, "a")
                NXB = 2
                xb = [sb(f"xb{i}", [128, D], F32, s1) for i in range(NXB)]
                xbb = [Buf() for _ in range(NXB)]
                xch = [Chan(k, f"xch{i}") for i in range(NXB)]
                xn = [sb(f"xn{i}", [128, D], BF16, s1) for i in range(NXB)]
                xnb = [Buf() for _ in range(NXB)]
                stt = [sb(f"stt{i}", [128, 2], F32, s1) for i in range(NXB)]
                sttb = [Buf() for _ in range(NXB)]
                hT = [sb(f"hT{i}", [128, 32, 512], BF16, s1) for i in range(2)]
                hTb = [Buf() for _ in range(2)]
                sq = [sb(f"sq{i}", [128, 512], F32, s1) for i in range(2)]
                sqb = [Buf() for _ in range(2)]
                ss = [sb(f"ss{i}", [128, 8], F32, s1) for i in range(2)]
                NQN = 8
                qn = [sb(f"qn{i}", [128, 512], BF16, s1) for i in range(NQN)]
                qnb = [Buf() for _ in range(NQN)]
                stg = [sb(f"stg{i}", [128, 512], BF16, s1) for i in range(4)]
                stgb = [Buf() for _ in range(4)]
                stch = [Chan(k, f"stch{i}") for i in range(4)]
                NPJ = 5
                pj = [ps(f"pj{i}", [128, 512], F32, s1) for i in range(NPJ)]
                pjb = [Buf() for _ in range(NPJ)]
                tpx = [ps(f"tpx{i}", [128, 1024], BF16, s1) for i in range(2)]
                tpxb = [Buf() for _ in range(2)]
                tpq = [ps(f"tpq{i}", [128, 1024], BF16, s1) for i in range(1)]
                tpqb = [Buf() for _ in range(1)]

                full_chunks = list(range(18))
                kv_chunks = [4, 5, 6, 7, 8, 9, 10, 11, 16, 17]
                tiles = [(t, full_chunks) for t in range(6)] + [(t, kv_chunks) for t in (6, 7)]
                aps = []
                for t, chs in tiles:
                    for c in chs:
                        for hf in range(2):
                            aps.append((("WI", c * 2 + hf), WI[c * 2 + hf]))
                ws.reset(aps)
                ws.topup()
                cnt = {"x": 0, "qn": 0, "stg": 0, "tpq": 0, "sq": 0, "pj": 0, "step": 0}

                def norm1(t, tb):
                    i = cnt["x"] % NXB
                    blk = t * 4 + tb
                    rms_part1(xb[i], xbb[i], xn[i], xnb[i], stt[i][:, 0:1], stt[i][:, 1:2], sttb[i],
                              xkv[blk * 128:(blk + 1) * 128, :], xch[i])
                    cnt["x"] += 1
                    return i

                def norm2(t, tb, i, hdeps, groups=(0, 1, 2, 3)):
                    rms_part2(xn[i], xnb[i], tpx, tpxb, hT[t % 2], hdeps, hTb[t % 2], tb, 0, 0, groups)

                def dest_for(c, blk):
                    tsl = slice(blk * 128, (blk + 1) * 128)
                    if c < 4:
                        return ("T", QTna[4 * c:4 * c + 4, :, tsl], 0)
                    if c < 8:
                        return ("T", KTna[4 * (c - 4):4 * (c - 4) + 4, :, tsl], 1)
                    if c < 12:
                        return ("V", Vna[tsl, (c - 8) * 512:(c - 7) * 512], None)
                    if c < 16:
                        return ("T", QTsw[4 * (c - 12):4 * (c - 12) + 4, :, tsl], 2)
                    if c == 16:
                        return ("T", KTsw[0:4, :, tsl], 3)
                    return ("V", Vsw[tsl, 0:512], None)

                def evac1(t, tb_, c, bk):
                    blk = t * 4 + tb_
                    tb = bk
                    kind, dst, gi = dest_for(c, blk)
                    if kind == "V":
                        si = cnt["stg"] % 4
                        cnt["stg"] += 1
                        E = ACT if (tb_ % 2 == 0) else DVE
                        E.wait(pjb[tb].rdeps() + stgb[si].wdeps())
                        stgb[si].start_write()
                        if E is ACT:
                            ins = nc.scalar.copy(out=stg[si][:], in_=pj[tb][:])
                        else:
                            ins = nc.vector.tensor_copy(out=stg[si][:], in_=pj[tb][:])
                        ev = E.sig(ins)
                        pjb[tb].read(ev)
                        stgb[si].wrote(ev)
                        ev2 = k.dma(stch[si], dst, stg[si][:], stgb[si].rdeps())
                        stgb[si].read(ev2)
                        return None
                    qi = cnt["sq"] % 2
                    cnt["sq"] += 1
                    ni = cnt["qn"] % NQN
                    cnt["qn"] += 1
                    ACT.wait(pjb[tb].rdeps() + sqb[qi].wdeps())
                    sqb[qi].start_write()
                    ev = ACT.sig(nc.scalar.activation(out=sq[qi][:], in_=pj[tb][:], func=AF.Square))
                    pjb[tb].read(ev)
                    DVE.wait([ev])
                    ev = DVE.sig(nc.vector.tensor_reduce(out=ss[qi][:, 0:4], in_=sq[qi][:].rearrange("p (h d) -> p h d", h=4),
                                                         axis=AX.X, op=ALU.add))
                    ACT.wait([ev])
                    ev = ACT.sig(nc.scalar.activation(out=ss[qi][:, 4:8], in_=ss[qi][:, 0:4], func=AF.Sqrt, scale=1.0 / 128,
                                                      bias=epsb[:, 0:1]))
                    DVE.wait([ev])
                    ev = DVE.sig(nc.vector.reciprocal(out=ss[qi][:, 4:8], in_=ss[qi][:, 4:8]))
                    DVE.wait([ev] + qnb[ni].wdeps())
                    qnb[ni].start_write()
                    ev = DVE.sig(nc.vector.tensor_tensor(out=qn[ni][:].rearrange("p (h d) -> p h d", h=4),
                                                         in0=pj[tb][:].rearrange("p (h d) -> p h d", h=4),
                                                         in1=ss[qi][:, 4:8].unsqueeze(2).to_broadcast([128, 4, 128]), op=ALU.mult))
                    pjb[tb].read(ev)
                    qnb[ni].wrote(ev)
                    sqb[qi].wrote(ev)
                    return (ni, dst, gi)

                def evac2(state):
                    if state is None:
                        return
                    ni, dst, gi = state
                    ti = 0
                    cnt["tpq"] += 1
                    PE.wait(qnb[ni].rdeps() + tpqb[ti].wdeps())
                    tpqb[ti].start_write()
                    for hh in range(4):
                        ins = nc.tensor.transpose(out=tpq[ti][:, hh * 128:(hh + 1) * 128], in_=qn[ni][:, hh * 128:(hh + 1) * 128],
                                                  identity=ident[:])
                    ev = PE.sig(ins)
                    qnb[ni].read(ev)
                    tpqb[ti].wrote(ev)
                    si = cnt["stg"] % 4
                    cnt["stg"] += 1
                    ACT.wait(tpqb[ti].rdeps() + stgb[si].wdeps())
                    stgb[si].start_write()
                    ev = ACT.sig(nc.scalar.activation(out=stg[si][:], in_=tpq[ti][:, 0:512], func=AF.Copy, scale=gsc[:, gi:gi + 1]))
                    tpqb[ti].read(ev)
                    stgb[si].wrote(ev)
                    ev2 = k.dma(stch[si], dst.rearrange("h d t -> d h t"), stg[si][:].rearrange("d (h t) -> d h t", h=4),
                                stgb[si].rdeps())
                    stgb[si].read(ev2)

                for tb in range(4):
                    i = norm1(0, tb)
                    hd = hTb[0].wdeps() if tb == 0 else []
                    if tb == 0:
                        hTb[0].start_write()
                    norm2(0, tb, i, hd)
                pending = []
                for ti_, (t, chs) in enumerate(tiles):
                    hcur = hT[t % 2]
                    hb = hTb[t % 2]
                    nxt = tiles[ti_ + 1][0] if ti_ + 1 < len(tiles) else None
                    nstate = {}
                    for ci, c in enumerate(chs):
                        if nxt is not None and ci < 8 and ci % 2 == 0:
                            nstate[ci // 2] = norm1(nxt, ci // 2)
                        bks = []
                        for tb in range(4):
                            bks.append(cnt["pj"] % NPJ)
                            cnt["pj"] += 1
                        n2 = None
                        if nxt is not None and ci < 8 and ci % 2 == 1:
                            tb2 = ci // 2
                            hd = hTb[nxt % 2].wdeps() if tb2 == 0 else []
                            if tb2 == 0:
                                hTb[nxt % 2].start_write()
                            n2 = (tb2, hd)
                        for hf in range(2):
                            wsl, wb = ws.get()
                            wv = wsl[:].rearrange("p (a b) -> p a b", a=16)
                            for tb in range(4):
                                bk = bks[tb]
                                deps = wb.rdeps() + hb.rdeps()
                                if hf == 0:
                                    deps = deps + pjb[bk].wdeps()
                                PE.wait(deps)
                                if hf == 0:
                                    pjb[bk].start_write()
                                for kc in range(16):
                                    ins = nc.tensor.matmul(pj[bk][:], hcur[:, hf * 16 + kc, tb * 128:(tb + 1) * 128], wv[:, kc, :],
                                                           start=(hf == 0 and kc == 0), stop=(hf == 1 and kc == 15))
                                if hf == 1 or tb == 3:
                                    ev = PE.sig(ins)
                                    wb.read(ev)
                                    hb.read(ev)
                                    if hf == 1:
                                        pjb[bk].wrote(ev)
                                if hf == 1 and pending:
                                    evac2(pending[tb])
                                if hf == 1 and n2 is not None:
                                    norm2(nxt, n2[0], nstate[n2[0]], n2[1], groups=(tb,))
                            cnt["step"] += 1
                            if cnt["step"] % 3 == 0:
                                k.pump(1)
                        pending = [evac1(t, tb, c, bks[tb]) for tb in range(4)]

                for stt_ in pending:
                    evac2(stt_)
                k.barrier()

        phase_A()
        if stop_after == "A":
            return nc

        def kvblock(j, dp):
            if j < 16:
                jj = j + dp
                return jj if 0 <= jj < 16 else None
            jq = j - 16 + dp
            if jq < 0:
                return 24 + 3 + jq
            if jq >= 8:
                return 27 + jq - 8
            return 16 + jq

        NKB = 11
        B_chans = {}
        ot_events = {}

        def alloc_B(stk, tag, cset=0):
            BR = {}
            if cset not in B_chans:
                B_chans[cset] = ([Chan(k, f"hch{cset}_{i}") for i in range(2)], [Chan(k, f"och{cset}_{i}") for i in range(2)])
            BR["hch"], BR["och"] = B_chans[cset]
            BR["nb"] = 0
            BR["KT"] = [sb(f"KT{tag}{i}", [128, NKB * 128], BF16, stk) for i in range(2)]
            BR["VV"] = [sb(f"VV{tag}{i}", [128, NKB, 128], BF16, stk) for i in range(2)]
            BR["QT"] = [sb(f"QT{tag}{i}", [128, 512], BF16, stk) for i in range(2)]
            BR["BT"] = [sb(f"BT{tag}{i}", [128, 7 * 128], F32, stk) for i in range(2)]
            BR["BM"] = [sb(f"BM{tag}{i}", [16, 512], F32, stk) for i in range(2)]
            BR["OS"] = [sb(f"OS{tag}{i}", [128, 512], BF16, stk) for i in range(2)]
            BR["tmp"] = sb(f"tmpB{tag}", [128, 8, 128], F32, stk)
            BR["PT"] = [sb(f"PTB{tag}{i}", [128, 8, 128], BF16, stk) for i in range(2)]
            BR["rden"] = sb(f"rdenB{tag}", [128, 128], F32, stk)
            BR["S"] = [ps(f"SpsB{tag}{i}", [128, 512], F32, stk) for i in range(2)]
            BR["O"] = ps(f"OpsB{tag}", [128, 512], F32, stk)
            BR["hbuf"] = [Buf() for _ in range(2)]
            BR["osb"] = [Buf() for _ in range(2)]
            BR["tmpb"] = Buf()
            BR["PTb"] = [Buf() for _ in range(2)]
            BR["rdb"] = Buf()
            BR["Sb"] = Buf()
            BR["Ob"] = Buf()
            return BR

        def kvblock(j, dp):
            if j < 16:
                jj = j + dp
                return jj if 0 <= jj < 16 else None
            jq = j - 16 + dp
            if jq < 0:
                return 24 + 3 + jq
            if jq >= 8:
                return 27 + jq - 8
            return 16 + jq

        def runs_of(blks):
            out = []
            i = 0
            while i < len(blks):
                j2 = i
                while j2 + 1 < len(blks) and blks[j2 + 1] == blks[j2] + 1:
                    j2 += 1
                out.append((i, blks[i], j2 - i + 1))
                i = j2 + 1
            return out

        def gen_B(T, BR, heads):
            B_hch, B_och = BR["hch"], BR["och"]
            Sps = BR["S"]
            Ops = BR["O"]
            B_KT, B_VV, B_QT, B_BT, B_BM, B_OS = BR["KT"], BR["VV"], BR["QT"], BR["BT"], BR["BM"], BR["OS"]
            B_tmp, B_PT, B_rden = BR["tmp"], BR["PT"], BR["rden"]
            B_hbuf, B_osb, B_tmpb, B_PTb, B_rdb, B_Sb, B_Ob = (BR["hbuf"], BR["osb"], BR["tmpb"], BR["PTb"], BR["rdb"], BR["Sb"],
                                                               BR["Ob"])
            js = list(range(4 * T, 4 * T + 4))
            kb = []
            for j in js:
                for dp in range(-3, 4):
                    blk = kvblock(j, dp)
                    if blk is not None and blk not in kb:
                        kb.append(blk)
            kb = sorted(kb) + [META_BLK]
            assert len(kb) <= NKB
            pos = {blk: i for i, blk in enumerate(kb)}
            runs = runs_of(kb)
            tsl = slice(T * 512, (T + 1) * 512)
            ot_events.setdefault(T, [])

            def load_head(hi):
                typ, h = heads[hi]
                s_ = hi % 2
                deps = B_hbuf[s_].wdeps()
                B_hbuf[s_].start_write()
                ch = B_hch[s_]
                if typ == "na":
                    ksrc, vsrc, vcol, qsrc, bsrc, nb_ = KTna[h], Vna, h, QTna[h], bna[h], 7
                else:
                    g = h // 4
                    ksrc, vsrc, vcol, qsrc, bsrc, nb_ = KTsw[g], Vsw, g, QTsw[h], bsw[h], 3
                ev = None
                for (i0, b0, n_) in runs:
                    ev = k.dma(ch, B_KT[s_][:, i0 * 128:(i0 + n_) * 128], ksrc[:, b0 * 128:(b0 + n_) * 128], deps)
                    deps = []
                    ev = k.dma(ch, B_VV[s_][:, i0:i0 + n_, :],
                               vsrc[b0 * 128:(b0 + n_) * 128, vcol * 128:(vcol + 1) * 128].rearrange("(b p) d -> p b d", p=128), [])
                ev = k.dma(ch, B_QT[s_][:], qsrc[:, tsl], [])
                ev = k.dma(ch, B_BT[s_][:, 0:nb_ * 128], bsrc[:, :], [])
                if typ == "sw":
                    ev = k.dma(ch, B_BM[s_][:], bmeta[h][:, tsl], [])
                B_hbuf[s_].wrote(ev)

            def Sview(ci):
                return Sps[ci // 4][:, (ci % 4) * 128:(ci % 4 + 1) * 128]

            prev = None

            def finish(pb):
                s_, hglob, jl, cl, p_, hb, is_last_j = pb
                PE.wait(B_PTb[p_].rdeps() + hb.rdeps() + B_Ob.wdeps())
                B_Ob.start_write()
                n_ = len(cl)
                for ci, (kind, blk, dpi) in enumerate(cl):
                    first = ci == 0
                    last = ci == n_ - 1
                    if kind == "k":
                        nc.tensor.matmul(Ops[:, 0:128], B_VV[s_][:, pos[blk], :], B_PT[p_][:, ci, :], start=first, stop=last)
                        ins = nc.tensor.matmul(Ops[:, 128:256], ones[:], B_PT[p_][:, ci, :], start=False, stop=last,
                                               skip_group_check=True)
                    else:
                        nc.tensor.matmul(Ops[:, 0:128], B_VV[s_][0:16, pos[blk], :], B_PT[p_][0:16, ci, :], start=first, stop=last)
                        ins = nc.tensor.matmul(Ops[:, 128:256], ones[0:16, :], B_PT[p_][0:16, ci, :], start=False, stop=last,
                                               skip_group_check=True)
                ev = PE.sig(ins)
                B_PTb[p_].read(ev)
                hb.read(ev)
                B_Ob.wrote(ev)
                DVE.wait(B_Ob.rdeps() + B_rdb.wdeps())
                B_rdb.start_write()
                ev = DVE.sig(nc.vector.tensor_scalar(out=B_rden[:], in0=Ops[:, 128:256], scalar1=esink[:, hglob:hglob + 1],
                                                     scalar2=None, op0=ALU.add))
                B_Ob.read(ev)
                DVE.wait([ev])
                ev = DVE.sig(nc.vector.reciprocal(out=B_rden[:], in_=B_rden[:]))
                B_rdb.wrote(ev)
                DVE.wait([ev] + B_osb[s_].wdeps())
                ev = DVE.sig(nc.vector.tensor_tensor(out=B_OS[s_][:, jl * 128:(jl + 1) * 128], in0=Ops[:, 0:128], in1=B_rden[:],
                                                     op=ALU.mult))
                B_Ob.read(ev)
                B_rdb.read(ev)
                B_osb[s_].wrote(ev)
                if is_last_j:
                    ev = k.dma(B_och[s_], OT[hglob][:, tsl], B_OS[s_][:], B_osb[s_].rdeps())
                    B_osb[s_].read(ev)
                    ot_events[T].append(ev)

            load_head(0)
            for hi, (typ, h) in enumerate(heads):
                s_ = hi % 2
                hb = B_hbuf[s_]
                hglob = h if typ == "na" else 16 + h
                dps = list(range(-3, 4)) if typ == "na" else [-1, 0, 1]
                for jl, j in enumerate(js):
                    cl = []
                    full = {}
                    for dpi, dp in enumerate(dps):
                        blk = kvblock(j, dp)
                        if blk is None:
                            continue
                        if typ == "na" and j < 16:
                            v = []
                            for b_ in range(2):
                                qr = 2 * j + b_
                                rs_ = min(max(qr - 4, 0), 32 - 8)
                                for a_ in range(2):
                                    kr = 2 * (j + dp) + a_
                                    v.append(0 <= kr < 32 and rs_ <= kr < rs_ + 8)
                            if not any(v):
                                continue
                            full[dpi] = all(v)
                        cl.append(("k", blk, dpi))
                    cl.append(("m", META_BLK, None))
                    if prev is not None:
                        finish(prev)
                    if jl == 0 and hi + 1 < len(heads):
                        load_head(hi + 1)
                    if jl == 0:
                        pass
                    p_ = BR["nb"] % 2
                    BR["nb"] += 1
                    PE.wait(hb.rdeps() + B_Sb.wdeps())
                    B_Sb.start_write()
                    qsl = B_QT[s_][:, jl * 128:(jl + 1) * 128]
                    for ci, (kind, blk, dpi) in enumerate(cl):
                        if kind == "k":
                            ins = nc.tensor.matmul(Sview(ci), B_KT[s_][:, pos[blk] * 128:(pos[blk] + 1) * 128], qsl, start=True, stop=True)
                        else:
                            ins = nc.tensor.matmul(Sview(ci)[0:16, :], B_KT[s_][:, pos[blk] * 128:pos[blk] * 128 + 16], qsl,
                                                   start=True, stop=True)
                    ev = PE.sig(ins)
                    hb.read(ev)
                    B_Sb.wrote(ev)
                    DVE.wait(B_Sb.rdeps() + hb.rdeps() + B_tmpb.wdeps())
                    B_tmpb.start_write()
                    for ci, (kind, blk, dpi) in enumerate(cl):
                        if kind == "k":
                            ins = nc.vector.tensor_tensor(out=B_tmp[:, ci, :], in0=Sview(ci), in1=B_BT[s_][:, dpi * 128:(dpi + 1) * 128],
                                                          op=ALU.add)
                        elif typ == "na":
                            ins = nc.vector.tensor_copy(out=B_tmp[0:16, ci, :], in_=Sview(ci)[0:16, :])
                        else:
                            ins = nc.vector.tensor_tensor(out=B_tmp[0:16, ci, :], in0=Sview(ci)[0:16, :],
                                                          in1=B_BM[s_][:, jl * 128:(jl + 1) * 128], op=ALU.add)
                    ev = DVE.sig(ins)
                    B_Sb.read(ev)
                    hb.read(ev)
                    B_tmpb.wrote(ev)
                    ACT.wait(B_tmpb.rdeps() + B_PTb[p_].wdeps())
                    B_PTb[p_].start_write()
                    for ci, (kind, blk, dpi) in enumerate(cl):
                        if kind == "k":
                            if typ == "na" and full.get(dpi, False):
                                ins = nc.scalar.activation(out=B_PT[p_][:, ci, :], in_=B_tmp[:, ci, :], func=AF.Exp)
                            elif typ == "na":
                                e0 = (j * 7 + dpi) * 2
                                nc.scalar.activation(out=B_PT[p_][:, ci, 0:64], in_=B_tmp[:, ci, 0:64], func=AF.Exp,
                                                     bias=mna[:, e0:e0 + 1])
                                ins = nc.scalar.activation(out=B_PT[p_][:, ci, 64:128], in_=B_tmp[:, ci, 64:128], func=AF.Exp,
                                                           bias=mna[:, e0 + 1:e0 + 2])
                            else:
                                e0 = j * 3 + dpi
                                ins = nc.scalar.activation(out=B_PT[p_][:, ci, :], in_=B_tmp[:, ci, :], func=AF.Exp,
                                                           bias=msw[:, e0:e0 + 1])
                        else:
                            ins = nc.scalar.activation(out=B_PT[p_][0:16, ci, :], in_=B_tmp[0:16, ci, :], func=AF.Exp)
                    ev = ACT.sig(ins)
                    B_tmpb.read(ev)
                    B_PTb[p_].wrote(ev)
                    prev = (s_, hglob, jl, cl, p_, hb, jl == 3)
                    yield
            finish(prev)
            yield

        bgen = {"g": None, "T": None, "hook": 0, "gs": []}
        ALL_HEADS = [("na", h) for h in range(16)] + [("sw", h) for h in range(16)]

        def pumpB(n_=1):
            for _ in range(n_):
                if not bgen["gs"]:
                    bgen["g"] = None
                    return
                g_ = bgen["gs"].pop(0)
                try:
                    next(g_)
                    bgen["gs"].append(g_)
                except StopIteration:
                    pass
                if not bgen["gs"]:
                    bgen["g"] = None

        def hookB():
            pumpB(1)
            bgen["hook"] += 1
            if bgen["hook"] % 4 == 0:
                k.pump(1)

        def startB(T, rsets):
            n_ = len(rsets)
            bgen["gs"] = [gen_B(T, R_, ALL_HEADS[i::n_]) for i, R_ in enumerate(rsets)]
            bgen["g"] = True
            bgen["T"] = T

        def drainB():
            while bgen["g"] is not None:
                pumpB(1)

        with ExitStack() as sB0:
            startB(0, [alloc_B(sB0, "s", 0), alloc_B(sB0, "t", 1)])
            nstep = 0
            while bgen["g"] is not None:
                pumpB(1)
                nstep += 1
                if nstep % 4 == 0:
                    k.pump(1)
            k.barrier(exclude=cch)
        if stop_after == "B":
            return nc

        def phase_CD():
            with ExitStack() as s1:
                BRc = alloc_B(s1, "c", 0)
                ws = make_ws(3, s1, "c")
                x1 = sb("x1", [128, 4, D], F32, s1)
                x1b = [Buf() for _ in range(4)]
                xch = [Chan(k, f"x1ch{i}") for i in range(4)]
                ych = [Chan(k, f"ych{i}") for i in range(4)]
                hT = sb("h2T", [128, 32, 512], BF16, s1)
                hTb = Buf()
                och = Chan(k, "otch")
                uT = sb("uT", [128, 16, 512], BF16, s1)
                uTb = Buf()
                xn = uT[:].rearrange("p a b -> p (a b)")[:, 0:D]
                xnb = uTb
                stt = sb("stt2", [128, 8], F32, s1)
                sttb = Buf()
                ssp = sb("ssp", [128, 64], F32, s1)
                sspb = Buf()
                sqj = sb("sqj", [128, 256], BF16, s1)
                NXS = 4
                xs = [sb(f"xs{i}", [128, 256], F32, s1) for i in range(NXS)]
                xsb = [Buf() for _ in range(NXS)]
                xsch = [Chan(k, f"xsch{i}") for i in range(NXS)]
                rl = [sb(f"rl{i}", [128, 512], F32, s1) for i in range(2)]
                rlb = [Buf() for _ in range(2)]
                pc = [ps(f"pc{i}", [128, 512], F32, s1) for i in range(2)]
                pcb = [Buf() for _ in range(2)]
                tpx = [ps("tpy0", [128, 1024], BF16, s1)]
                tpxb = [Buf()]
                pu = [ps(f"pu{i}", [128, 512], F32, s1) for i in range(2)]
                pub = [Buf() for _ in range(2)]
                tpx.append(pu[0][:].bitcast(BF16))
                tpxb.append(pub[0])
                cnt = {"pc": 0, "pu": 0, "rl": 0}

                aps = []
                for t in range(6):
                    for c in range(16):
                        aps.append((("WO", c), WO[c]))
                    for j in range(8):
                        for fq in range(8):
                            aps.append((("WU", j * 8 + fq), WU[j * 8 + fq]))
                        for db in range(8):
                            aps.append((("WD", j * 8 + db), WD[j * 8 + db]))
                ws.reset(aps)
                ws.topup()

                def load_oT(t_):
                    deps = hTb.wdeps() + ot_events[t_]
                    hTb.start_write()
                    ev_ = k.dma(och, hT[:], OT[:, :, t_ * 512:(t_ + 1) * 512].rearrange("h d t -> d h t"), deps)
                    hTb.wrote(ev_)

                drainB()
                load_oT(0)
                for t in range(6):
                    tok0 = t * 512
                    if t + 1 < 6:
                        startB(t + 1, [BRc])
                    xpieces = [(c_, tb_) for c_ in range(16) for tb_ in range(4)]
                    xstate = {"i": 0}

                    def issue_x():
                        i_ = xstate["i"]
                        if i_ >= len(xpieces):
                            return
                        c_, tb_ = xpieces[i_]
                        sl_ = i_ % NXS
                        deps_ = xsb[sl_].wdeps()
                        xsb[sl_].start_write()
                        ev_ = k.dma(xsch[sl_], xs[sl_][:], xkv[tok0 + tb_ * 128:tok0 + (tb_ + 1) * 128, c_ * 256:(c_ + 1) * 256], deps_)
                        xsb[sl_].wrote(ev_)
                        xstate["i"] = i_ + 1

                    for _ in range(NXS - 1):
                        issue_x()
                    sdeps = sspb.wdeps()
                    sspb.start_write()
                    for c in range(16):
                        wsl, wb = ws.get()
                        wv = wsl[:].rearrange("p (a b) -> p a b", a=32)
                        for tb in range(4):
                            bi = cnt["pc"] % 2
                            cnt["pc"] += 1
                            PE.wait(wb.rdeps() + hTb.rdeps() + pcb[bi].wdeps())
                            pcb[bi].start_write()
                            for kc in range(32):
                                ins = nc.tensor.matmul(pc[bi][:, 0:256], hT[:, kc, tb * 128:(tb + 1) * 128], wv[:, kc, :],
                                                       start=(kc == 0), stop=(kc == 31))
                            ev = PE.sig(ins)
                            wb.read(ev)
                            hTb.read(ev)
                            pcb[bi].wrote(ev)
                            issue_x()
                            sl = (c * 4 + tb) % NXS
                            DVE.wait(pcb[bi].rdeps() + xsb[sl].rdeps() + x1b[tb].wdeps())
                            ev = DVE.sig(nc.vector.tensor_tensor(out=x1[:, tb, c * 256:(c + 1) * 256], in0=pc[bi][:, 0:256],
                                                                 in1=xs[sl][:], op=ALU.add))
                            pcb[bi].read(ev)
                            xsb[sl].read(ev)
                            x1b[tb].wrote(ev)
                            ACT.wait([ev] + sdeps)
                            ev = ACT.sig(nc.scalar.activation(out=sqj[:], in_=x1[:, tb, c * 256:(c + 1) * 256], func=AF.Square,
                                                              accum_out=ssp[:, tb * 16 + c:tb * 16 + c + 1]))
                            x1b[tb].read(ev)
                            sspb.wrote(ev)
                            if tb % 2 == 1:
                                hookB()
                    hdeps = hTb.wdeps()
                    hTb.start_write()
                    DVE.wait(sspb.rdeps() + sttb.wdeps())
                    sttb.start_write()
                    ev = DVE.sig(nc.vector.tensor_reduce(out=stt[:, 0:4], in_=ssp[:].rearrange("p (t c) -> p t c", t=4), axis=AX.X,
                                                         op=ALU.add))
                    sspb.read(ev)
                    ACT.wait([ev])
                    ev = ACT.sig(nc.scalar.activation(out=stt[:, 4:8], in_=stt[:, 0:4], func=AF.Sqrt, scale=1.0 / D,
                                                      bias=epsb[:, 0:1]))
                    DVE.wait([ev])
                    ev = DVE.sig(nc.vector.reciprocal(out=stt[:, 4:8], in_=stt[:, 4:8]))
                    sttb.wrote(ev)
                    for tb in range(4):
                        DVE.wait(x1b[tb].rdeps() + xnb.wdeps() + sttb.rdeps())
                        xnb.start_write()
                        ev = DVE.sig(nc.vector.tensor_scalar(out=xn, in0=x1[:, tb, :], scalar1=stt[:, 4 + tb:5 + tb], scalar2=None,
                                                             op0=ALU.mult))
                        x1b[tb].read(ev)
                        xnb.wrote(ev)
                        sttb.read(ev)
                        rms_part2(xn, xnb, tpx, tpxb, hT, hdeps, hTb, tb, 32, 2 * tb)
                    for j in range(8):
                        udeps = uTb.wdeps()
                        uTb.start_write()
                        for fq in range(8):
                            wsl, wb = ws.get()
                            wv = wsl[:].rearrange("p (a b) -> p a b", a=32)
                            for f2 in range(2):
                                fb_ = fq * 2 + f2
                                ui = cnt["pu"] % 2
                                cnt["pu"] += 1
                                PE.wait(wb.rdeps() + hTb.rdeps() + pub[ui].wdeps())
                                pub[ui].start_write()
                                for kc in range(32):
                                    ins = nc.tensor.matmul(pu[ui][:], wv[:, kc, f2 * 128:(f2 + 1) * 128], hT[:, kc, :],
                                                           start=(kc == 0), stop=(kc == 31))
                                ev = PE.sig(ins)
                                wb.read(ev)
                                hTb.read(ev)
                                pub[ui].wrote(ev)
                                ri = cnt["rl"] % 2
                                cnt["rl"] += 1
                                DVE.wait(pub[ui].rdeps() + rlb[ri].wdeps())
                                rlb[ri].start_write()
                                ev = DVE.sig(nc.vector.tensor_scalar(out=rl[ri][:], in0=pu[ui][:], scalar1=0.0, scalar2=None,
                                                                     op0=ALU.max))
                                pub[ui].read(ev)
                                rlb[ri].wrote(ev)
                                POOL.wait(rlb[ri].rdeps() + udeps)
                                ev = POOL.sig(nc.gpsimd.tensor_tensor(out=uT[:, fb_, :], in0=rl[ri][:], in1=rl[ri][:], op=ALU.mult))
                                rlb[ri].read(ev)
                                uTb.wrote(ev)
                                hookB()
                        if j == 7 and t + 1 < 6:
                            drainB()
                            load_oT(t + 1)
                        for db in range(8):
                            wsl, wb = ws.get()
                            wv = wsl[:].rearrange("p (a b) -> p a b", a=16)
                            for tb in range(4):
                                bi = cnt["pc"] % 2
                                cnt["pc"] += 1
                                PE.wait(wb.rdeps() + uTb.rdeps() + pcb[bi].wdeps())
                                pcb[bi].start_write()
                                for fc in range(16):
                                    ins = nc.tensor.matmul(pc[bi][:], uT[:, fc, tb * 128:(tb + 1) * 128], wv[:, fc, :],
                                                           start=(fc == 0), stop=(fc == 15))
                                ev = PE.sig(ins)
                                wb.read(ev)
                                uTb.read(ev)
                                pcb[bi].wrote(ev)
                                DVE.wait(pcb[bi].rdeps() + x1b[tb].wdeps())
                                ev = DVE.sig(nc.vector.tensor_tensor(out=x1[:, tb, db * 512:(db + 1) * 512], in0=pc[bi][:],
                                                                     in1=x1[:, tb, db * 512:(db + 1) * 512], op=ALU.add))
                                pcb[bi].read(ev)
                                x1b[tb].wrote(ev)
                                if j == 7:
                                    ev = k.dma(ych[tb], y[tok0 + tb * 128:tok0 + (tb + 1) * 128, db * 512:(db + 1) * 512],
                                               x1[:, tb, db * 512:(db + 1) * 512], x1b[tb].rdeps())
                                    x1b[tb].read(ev)
                                if tb % 2 == 1:
                                    hookB()
                k.barrier()

        phase_CD()
    return nc


def _t5_bucket(rel):
    nb = 16
    max_exact = 8
    ret = np.where(rel > 0, nb, 0)
    n = np.abs(rel)
    large = max_exact + (np.log(np.maximum(n, 1) / max_exact) / np.log(128 / max_exact) * (nb - max_exact)).astype(np.int64)
    large = np.minimum(large, nb - 1)
    return ret + np.where(n < max_exact, n, large)


def _host_prep(x_prompt, x_sample, meta_tokens, t5_bias, norm_attn, q_norm_na, k_norm_na, na_rpb, q_norm_swa, k_norm_swa,
               swa_sink, norm_mlp):
    f32 = np.float32
    xp = np.asarray(x_prompt, f32)[0]
    xs = np.asarray(x_sample, f32)
    rpb = np.asarray(na_rpb, f32)[0]
    t5 = np.asarray(t5_bias, f32)
    a = np.arange(2)[:, None]
    kc = np.arange(64)[None, :]
    cs = np.clip(np.arange(64) - 8, 0, 48)
    bna = np.empty((16, 128, 7, 128), f32)
    for dpi, dp in enumerate(range(-3, 4)):
        A = np.repeat(np.arange(2), 64)
        KC = np.tile(np.arange(64), 2)
        dr = 2 * dp + A[:, None] - A[None, :]
        dc = KC[:, None] - KC[None, :]
        col_in = (KC[:, None] >= cs[KC][None, :]) & (KC[:, None] < cs[KC][None, :] + 16)
        ok = col_in & (np.abs(dr) <= 7)
        g = rpb[:, np.clip(dr, -7, 7) + 7, np.clip(dc, -15, 15) + 15]
        bna[:, :, dpi, :] = np.where(ok[None], g, f32(NEG))
    bsw = np.empty((16, 128, 3, 128), f32)
    P = np.arange(128)
    for dpi, dp in enumerate((-1, 0, 1)):
        rel = (dp * 128 + P[:, None]) - P[None, :]
        ok = np.abs(rel) <= 128
        g = t5[_t5_bucket(rel)]
        bsw[:, :, dpi, :] = np.where(ok[None], g.transpose(2, 0, 1), f32(NEG))
    pv_common = np.zeros((128, 84), f32)
    pv_common[:, 0:32] = np.asarray(norm_attn, f32)[0].reshape(32, 128).T
    pv_common[:, 32:64] = np.asarray(norm_mlp, f32)[0].reshape(32, 128).T
    pv_common[:, 64] = np.asarray(q_norm_na, f32)[0]
    pv_common[:, 65] = np.asarray(k_norm_na, f32)[0]
    pv_common[:, 66] = np.asarray(q_norm_swa, f32)[0]
    pv_common[:, 67] = np.asarray(k_norm_swa, f32)[0]
    pv_common[:, 68:84] = np.asarray(swa_sink, f32)[0][None, :]
    ident = np.eye(128, dtype=f32)
    meta = np.asarray(meta_tokens, f32)
    per_core = []
    for c in range(NCORES):
        xkv = np.zeros((NKVB * 128, D), f32)
        xkv[0:2048] = xs[c]
        xkv[2048:3072] = xp[1024 * c:1024 * (c + 1)]
        for i in range(3):
            pr = 8 * c - 3 + i
            if pr >= 0:
                xkv[(24 + i) * 128:(25 + i) * 128] = xp[pr * 128:(pr + 1) * 128]
            pr = 8 * c + 8 + i
            if pr < 64:
                xkv[(27 + i) * 128:(28 + i) * 128] = xp[pr * 128:(pr + 1) * 128]
        xkv[META_BLK * 128:META_BLK * 128 + 16] = meta
        mcna = np.zeros((128, NQB, 7, 2), f32)
        mcsw = np.zeros((128, NQB, 3), f32)
        tpos = np.empty(NQB * 128, np.int64)
        for j in range(NQB):
            if j < 16:
                rows, jg = 32, j
                tpos[j * 128:(j + 1) * 128] = j * 128 + np.arange(128)
            else:
                rows, jg = 128, 8 * c + (j - 16)
                tpos[j * 128:(j + 1) * 128] = jg * 128 + np.arange(128)
            nbk = rows // 2
            for dpi, dp in enumerate(range(-3, 4)):
                for b in range(2):
                    qr = 2 * jg + b
                    rs = min(max(qr - 4, 0), rows - 8)
                    for a_ in range(2):
                        kr = 2 * (jg + dp) + a_
                        ok = (0 <= kr < rows) and (rs <= kr < rs + 8)
                        mcna[a_ * 64:(a_ + 1) * 64, j, dpi, b] = 0.0 if ok else NEG
            for dpi, dp in enumerate((-1, 0, 1)):
                ok = 0 <= jg + dp < nbk
                mcsw[:, j, dpi] = 0.0 if ok else NEG
        relm = np.arange(16)[:, None] - (16 + tpos)[None, :]
        bmeta = np.ascontiguousarray(t5[_t5_bucket(relm)].transpose(2, 0, 1))
        per_core.append({
            "xkv": xkv,
            "pvec": pv_common,
            "ident": ident,
            "bna": bna.reshape(16, 128, 7 * 128),
            "bsw": bsw.reshape(16, 128, 3 * 128),
            "bmeta": bmeta,
            "mcna": mcna.reshape(128, NQB * 14),
            "mcsw": mcsw.reshape(128, NQB * 3),
        })
    return per_core


_NC_CACHE = {}


def kernel(x_prompt, x_sample, meta_tokens, t5_bias, norm_attn, w_in, q_norm_na, k_norm_na, na_rpb, q_norm_swa, k_norm_swa,
           swa_sink, w_out, norm_mlp, w_up, w_down):
    per_core = _host_prep(x_prompt, x_sample, meta_tokens, t5_bias, norm_attn, q_norm_na, k_norm_na, na_rpb, q_norm_swa,
                          k_norm_swa, swa_sink, norm_mlp)
    wi = np.ascontiguousarray(np.asarray(w_in, np.float32)[0])
    wo = np.ascontiguousarray(np.asarray(w_out, np.float32)[0])
    wu = np.ascontiguousarray(np.asarray(w_up, np.float32)[0])
    wd = np.ascontiguousarray(np.asarray(w_down, np.float32)[0])
    for d in per_core:
        d.update({"w_in": wi, "w_out": wo, "w_up": wu, "w_down": wd})
    if "nc" not in _NC_CACHE:
        _NC_CACHE["nc"] = build_nc()
    nc = _NC_CACHE["nc"]
    res = run_bass_kernel_spmd(nc, per_core, core_ids=list(range(NCORES)))
    y_prompt = np.empty((1, 8192, D), np.float32)
    y_sample = np.empty((8, 2048, D), np.float32)
    for c in range(NCORES):
        yc = np.asarray(res.results[c]["y"])
        y_sample[c] = yc[0:2048]
        y_prompt[0, 1024 * c:1024 * (c + 1)] = yc[2048:3072]
    return (y_prompt, y_sample)
```
